# Optimizing a Trainium2 kernel written in Bass

```python
import math
import jax
import jax.numpy as jnp
from jax import lax
import numpy as np

D_MODEL = 1024
BATCH = 32
SEQ = 256
DEPTH = 4
DEC_BATCH = 4
DEC_SEQ = 2048
PAST_LEN = 512

GRID_W = 64
N_MIXERS = 4
HEAD_DIM = 64
ROPE_THETA = 10000.0
Q_BLOCK = 128
WINDOW = 128
EPS = 1e-6
NEG_INF = -1e30
A_HEADS = D_MODEL // (2 * HEAD_DIM)
A_VDIM = 2 * HEAD_DIM
GQA_HEADS = D_MODEL // HEAD_DIM
GQA_KV_HEADS = GQA_HEADS // 4
GQA_GROUP = GQA_HEADS // GQA_KV_HEADS
D_KDIM = 128
D_HEADS = D_MODEL // D_KDIM
D_VDIM = D_MODEL // D_HEADS
D_FDIM = D_HEADS * D_KDIM
CHUNK = 64
D_FF = -(-8 * D_MODEL // (3 * 256)) * 256
N_LAYERS_A = (DEPTH + 3) // 4
N_LAYERS_B = (DEPTH + 2) // 4
N_LAYERS_C = (DEPTH + 1) // 4
N_LAYERS_D = DEPTH // 4

kernel_name = 'hybrid_diffusion_prefix_trunk_step'


def rms_norm(x, gain):
    xf = x.astype(jnp.float32)
    y = xf * lax.rsqrt(jnp.mean(xf * xf, axis=-1, keepdims=True) + EPS)
    return (y * gain.astype(jnp.float32)).astype(x.dtype)


def modulate(h, shift, scale):
    return h * (1 + scale[:, None]) + shift[:, None]


def axial_rope_tables(n_tokens):
    rows = n_tokens // GRID_W
    row = jnp.repeat(jnp.arange(rows, dtype=jnp.float32), GRID_W)
    col = jnp.tile(jnp.arange(GRID_W, dtype=jnp.float32), rows)
    axis_dim = HEAD_DIM // 2
    inv_freq = ROPE_THETA ** (-jnp.arange(0, axis_dim, 2, dtype=jnp.float32) / axis_dim)
    ang = jnp.stack([row[:, None] * inv_freq, col[:, None] * inv_freq], axis=1)
    return jnp.cos(ang), jnp.sin(ang)


def apply_rope(x, cos, sin):
    b, t, h, dh = x.shape
    xr = x.reshape(b, t, h, 2, 2, dh // 4)
    x1, x2 = xr[..., 0, :], xr[..., 1, :]
    cs = cos[None, :, None].astype(x.dtype)
    sn = sin[None, :, None].astype(x.dtype)
    out = jnp.stack([x1 * cs - x2 * sn, x2 * cs + x1 * sn], axis=-2)
    return out.reshape(b, t, h, dh)


def sweep_query_blocks(fn, q):
    b, t = q.shape[:2]
    nb = t // Q_BLOCK
    qb = jnp.moveaxis(q.reshape((b, nb, Q_BLOCK) + q.shape[2:]), 1, 0)
    out = lax.map(lambda a: fn(a[0], a[1]), (qb, jnp.arange(nb)))
    return jnp.moveaxis(out, 0, 1).reshape((b, t) + out.shape[3:])


def gqa_scores(qb, k):
    return jnp.einsum('bqgrd,bsgd->bgrqs', qb, k).astype(jnp.float32) * (HEAD_DIM ** -0.5)


def gqa_values(p, v):
    return jnp.einsum('bgrqs,bsgd->bqgrd', p.astype(v.dtype), v)


def softmax_with_sink(s, sink):
    sk = jnp.broadcast_to(sink[None, :, :, None, None], s.shape[:-1] + (1,))
    return jax.nn.softmax(jnp.concatenate([sk, s], axis=-1), axis=-1)[..., 1:]


def dense_gqa(q, k, v, sink):
    def block(qb, _):
        s = gqa_scores(qb, k)
        p = jax.nn.softmax(s, axis=-1) if sink is None else softmax_with_sink(s, sink)
        return gqa_values(p, v)
    return sweep_query_blocks(block, q)


def project_gqa(h, w_qkv, q_gain, k_gain, rope):
    b, t, _ = h.shape
    q, k, v = jnp.split(h @ w_qkv, [GQA_HEADS * HEAD_DIM, (GQA_HEADS + GQA_KV_HEADS) * HEAD_DIM], axis=-1)
    q = rms_norm(q.reshape(b, t, GQA_HEADS, HEAD_DIM), q_gain)
    k = rms_norm(k.reshape(b, t, GQA_KV_HEADS, HEAD_DIM), k_gain)
    v = v.reshape(b, t, GQA_KV_HEADS, HEAD_DIM)
    if rope is not None:
        q = apply_rope(q, *rope)
        k = apply_rope(k, *rope)
    return q.reshape(b, t, GQA_KV_HEADS, GQA_GROUP, HEAD_DIM), k, v


def mixer_diff_attn(h, w_qkv, w_o, q_gain, k_gain, sub_gain, lam, lam_init, rope, ctx_k, ctx_v):
    b, t, _ = h.shape
    q, k, v = jnp.split(h @ w_qkv, [D_MODEL, 2 * D_MODEL], axis=-1)
    q = rms_norm(q.reshape(b, t, 2 * A_HEADS, HEAD_DIM), q_gain)
    k = rms_norm(k.reshape(b, t, 2 * A_HEADS, HEAD_DIM), k_gain)
    v = v.reshape(b, t, A_HEADS, A_VDIM)
    if rope is not None:
        q = apply_rope(q, *rope)
        k = apply_rope(k, *rope)
    keys, vals = (k, v) if ctx_k is None else (jnp.concatenate([ctx_k, k], axis=1), jnp.concatenate([ctx_v, v], axis=1))
    keys = keys.reshape(b, keys.shape[1], A_HEADS, 2, HEAD_DIM)
    qh = q.reshape(b, t, A_HEADS, 2, HEAD_DIM)

    def block(qb, _):
        s = jnp.einsum('bqhmd,bkhmd->bhmqk', qb, keys).astype(jnp.float32) * (HEAD_DIM ** -0.5)
        p = jax.nn.softmax(s, axis=-1)
        a = p[:, :, 0] - lam * p[:, :, 1]
        return jnp.einsum('bhqk,bkhe->bqhe', a.astype(vals.dtype), vals)

    o = sweep_query_blocks(block, qh)
    o = rms_norm(o, sub_gain) * (1.0 - lam_init)
    return o.reshape(b, t, D_MODEL) @ w_o, k, v


def mixer_window_sink(h, w_qkv, w_o, q_gain, k_gain, sink, rope, ctx_k, ctx_v):
    b, t, _ = h.shape
    q, k, v = project_gqa(h, w_qkv, q_gain, k_gain, rope)
    sink = sink.astype(jnp.float32).reshape(GQA_KV_HEADS, GQA_GROUP)
    if ctx_k is None:
        o = dense_gqa(q, k, v, sink)
    else:
        n_ctx = ctx_k.shape[1]
        pad = ((0, 0), (Q_BLOCK, Q_BLOCK), (0, 0), (0, 0))
        kp, vp = jnp.pad(k, pad), jnp.pad(v, pad)
        offs = jnp.arange(3 * Q_BLOCK) - Q_BLOCK
        qi = jnp.arange(Q_BLOCK)

        def block(qb, j):
            kb = lax.dynamic_slice_in_dim(kp, j * Q_BLOCK, 3 * Q_BLOCK, axis=1)
            vb = lax.dynamic_slice_in_dim(vp, j * Q_BLOCK, 3 * Q_BLOCK, axis=1)
            qpos = j * Q_BLOCK + qi
            kpos = j * Q_BLOCK + offs
            valid = (jnp.abs(qpos[:, None] - kpos[None, :]) <= WINDOW) & ((kpos >= 0) & (kpos < t))[None, :]
            s_band = jnp.where(valid, gqa_scores(qb, kb), NEG_INF)
            s_ctx = gqa_scores(qb, ctx_k)
            p = softmax_with_sink(jnp.concatenate([s_ctx, s_band], axis=-1), sink)
            return gqa_values(p[..., :n_ctx], ctx_v) + gqa_values(p[..., n_ctx:], vb)

        o = sweep_query_blocks(block, q)
    return o.reshape(b, t, D_MODEL) @ w_o, k, v


def mixer_axial_gqa(h, w_qkv, w_o, q_gain, k_gain, rope, ctx_k, ctx_v):
    b, t, _ = h.shape
    q, k, v = project_gqa(h, w_qkv, q_gain, k_gain, rope)
    keys, vals = (k, v) if ctx_k is None else (jnp.concatenate([ctx_k, k], axis=1), jnp.concatenate([ctx_v, v], axis=1))
    o = dense_gqa(q, keys, vals, None)
    return o.reshape(b, t, D_MODEL) @ w_o, k, v


def chunk_gla(q, k, v, log_f, s0):
    b, t, h, dk = q.shape
    n = t // CHUNK
    rs = lambda a: a.reshape(b, n, CHUNK, h, a.shape[-1]).astype(jnp.float32)
    q, k, v, log_f = rs(q), rs(k), rs(v), rs(log_f)
    cum = jnp.cumsum(log_f, axis=2)
    last = cum[:, :, -1:]
    q_dec = q * jnp.exp(cum)
    k_in = k * jnp.exp(-cum)
    k_out = k * jnp.exp(last - cum)
    mask = jnp.tril(jnp.ones((CHUNK, CHUNK), jnp.float32))
    att = jnp.einsum('bnchk,bnshk->bnhcs', q_dec, k_in) * mask
    o_intra = jnp.einsum('bnhcs,bnshv->bnchv', att, v)
    d_state = jnp.einsum('bnshk,bnshv->bnhkv', k_out, v)
    decay = jnp.exp(last[:, :, 0])

    def step(s, inp):
        dec, ds = inp
        return dec[..., None] * s + ds, s

    s_final, s_prev = lax.scan(step, s0.astype(jnp.float32), (jnp.moveaxis(decay, 1, 0), jnp.moveaxis(d_state, 1, 0)))
    s_prev = jnp.moveaxis(s_prev, 0, 1)
    o_inter = jnp.einsum('bnchk,bnhkv->bnchv', q_dec, s_prev)
    return (o_intra + o_inter).reshape(b, t, h, v.shape[-1]), s_final


def mixer_hgrn2(h, w_in, w_o, g_gain, lb, s0):
    b, t, _ = h.shape
    q, f_fwd, f_bwd, i, g = jnp.split(h @ w_in, [D_FDIM, 2 * D_FDIM, 3 * D_FDIM, 3 * D_FDIM + D_MODEL], axis=-1)
    q = jax.nn.silu(q).reshape(b, t, D_HEADS, D_KDIM)
    v = i.reshape(b, t, D_HEADS, D_VDIM)
    outs, finals = [], []
    for d, f_logit in enumerate((f_fwd, f_bwd)):
        lbd = lb[d].astype(jnp.float32)
        f = lbd + (1.0 - lbd) * jax.nn.sigmoid(f_logit.astype(jnp.float32))
        k = (1.0 - f).reshape(b, t, D_HEADS, D_KDIM)
        log_f = jnp.log(f).reshape(b, t, D_HEADS, D_KDIM)
        qd, vd = q, v
        if d == 1:
            qd, k, vd, log_f = (jnp.flip(a, axis=1) for a in (qd, k, vd, log_f))
        o, s_fin = chunk_gla(qd, k, vd, log_f, s0[:, d])
        if d == 1:
            o = jnp.flip(o, axis=1)
        outs.append(o)
        finals.append(s_fin)
    o = (outs[0] + outs[1]).astype(h.dtype)
    o = rms_norm(o, g_gain) * jax.nn.silu(g.reshape(b, t, D_HEADS, D_VDIM))
    return o.reshape(b, t, D_MODEL) @ w_o, jnp.stack(finals, axis=1)


def swiglu(h, w_gate, w_up, w_down):
    return (jax.nn.silu(h @ w_gate) * (h @ w_up)) @ w_down


def setup_inputs(seed: int = 0) -> dict:
    key = jax.random.key(seed)
    ks = iter(jax.random.split(key, 48))

    def normal(shape, scale):
        return jax.random.normal(next(ks), shape, jnp.float32) * scale

    def gain(shape):
        return 1.0 + normal(shape, 0.05)

    D = D_MODEL
    gqa_in = (GQA_HEADS + 2 * GQA_KV_HEADS) * HEAD_DIM
    return {
        'x_prompt': normal((BATCH, SEQ, D), 1.0),
        'x_sample': normal((DEC_BATCH, DEC_SEQ, D), 1.0),
        'c': normal((DEC_BATCH, D), 1.0),
        'cache_a_k': normal((DEC_BATCH, N_LAYERS_A, PAST_LEN, 2 * A_HEADS, HEAD_DIM), 1.0),
        'cache_a_v': normal((DEC_BATCH, N_LAYERS_A, PAST_LEN, A_HEADS, A_VDIM), 1.0),
        'cache_b_k': normal((DEC_BATCH, N_LAYERS_B, PAST_LEN, GQA_KV_HEADS, HEAD_DIM), 1.0),
        'cache_b_v': normal((DEC_BATCH, N_LAYERS_B, PAST_LEN, GQA_KV_HEADS, HEAD_DIM), 1.0),
        'cache_c_k': normal((DEC_BATCH, N_LAYERS_C, PAST_LEN, GQA_KV_HEADS, HEAD_DIM), 1.0),
        'cache_c_v': normal((DEC_BATCH, N_LAYERS_C, PAST_LEN, GQA_KV_HEADS, HEAD_DIM), 1.0),
        'state_d': normal((DEC_BATCH, N_LAYERS_D, 2, D_HEADS, D_KDIM, D_VDIM), 0.3),
        'c_ctx': normal((D,), 1.0),
        'norm_mix': gain((DEPTH, D)),
        'norm_ffn': gain((DEPTH, D)),
        'w_ada': normal((DEPTH, D, 6 * D), 0.5 * D ** -0.5),
        'b_ada': normal((DEPTH, 6 * D), 0.01),
        'w_ffn_gate': normal((DEPTH, D, D_FF), D ** -0.5),
        'w_ffn_up': normal((DEPTH, D, D_FF), D ** -0.5),
        'w_ffn_down': normal((DEPTH, D_FF, D), D_FF ** -0.5),
        'w_qkv_a': normal((N_LAYERS_A, D, 3 * D), D ** -0.5),
        'w_o_a': normal((N_LAYERS_A, D, D), D ** -0.5),
        'qn_a': gain((N_LAYERS_A, HEAD_DIM)),
        'kn_a': gain((N_LAYERS_A, HEAD_DIM)),
        'subln_a': gain((N_LAYERS_A, A_VDIM)),
        'lam_q1_a': normal((N_LAYERS_A, HEAD_DIM), 0.1),
        'lam_k1_a': normal((N_LAYERS_A, HEAD_DIM), 0.1),
        'lam_q2_a': normal((N_LAYERS_A, HEAD_DIM), 0.1),
        'lam_k2_a': normal((N_LAYERS_A, HEAD_DIM), 0.1),
        'w_qkv_b': normal((N_LAYERS_B, D, gqa_in), D ** -0.5),
        'w_o_b': normal((N_LAYERS_B, D, D), D ** -0.5),
        'qn_b': gain((N_LAYERS_B, HEAD_DIM)),
        'kn_b': gain((N_LAYERS_B, HEAD_DIM)),
        'sink_b': normal((N_LAYERS_B, GQA_HEADS), 0.5),
        'w_qkv_c': normal((N_LAYERS_C, D, gqa_in), D ** -0.5),
        'w_o_c': normal((N_LAYERS_C, D, D), D ** -0.5),
        'qn_c': gain((N_LAYERS_C, HEAD_DIM)),
        'kn_c': gain((N_LAYERS_C, HEAD_DIM)),
        'w_in_d': normal((N_LAYERS_D, D, 3 * D_FDIM + 2 * D), D ** -0.5),
        'w_o_d': normal((N_LAYERS_D, D, D), D ** -0.5),
        'gn_d': gain((N_LAYERS_D, D_VDIM)),
        'lb_logits_d': normal((2, DEPTH, D_FDIM), 0.5),
    }


def reference(x_prompt, x_sample, c, cache_a_k, cache_a_v, cache_b_k, cache_b_v, cache_c_k, cache_c_v,
              state_d, c_ctx, norm_mix, norm_ffn, w_ada, b_ada, w_ffn_gate, w_ffn_up, w_ffn_down,
              w_qkv_a, w_o_a, qn_a, kn_a, subln_a, lam_q1_a, lam_k1_a, lam_q2_a, lam_k2_a,
              w_qkv_b, w_o_b, qn_b, kn_b, sink_b, w_qkv_c, w_o_c, qn_c, kn_c,
              w_in_d, w_o_d, gn_d, lb_logits_d):
    rope = axial_rope_tables(x_sample.shape[1])
    cond_ctx = jax.nn.silu(c_ctx)[None]
    cond_lat = jax.nn.silu(c)
    p_lb = jax.nn.softmax(lb_logits_d.astype(jnp.float32), axis=1)
    lb_all = jnp.cumsum(p_lb, axis=1) - p_lb[:, :1]
    zero_state = jnp.zeros((x_prompt.shape[0], 2, D_HEADS, D_KDIM, D_VDIM), jnp.float32)

    xp, xs = x_prompt, x_sample
    new_a_k, new_a_v, new_b_k, new_b_v, new_c_k, new_c_v, new_d = [], [], [], [], [], [], []
    for li in range(DEPTH):
        kind, j = li % N_MIXERS, li // N_MIXERS
        mod_p = jnp.split(cond_ctx @ w_ada[li] + b_ada[li], 6, axis=-1)
        mod_s = jnp.split(cond_lat @ w_ada[li] + b_ada[li], 6, axis=-1)
        hp = modulate(rms_norm(xp, norm_mix[li]), mod_p[0], mod_p[1])
        hs = modulate(rms_norm(xs, norm_mix[li]), mod_s[0], mod_s[1])
        if kind == 0:
            lam_init = 0.8 - 0.6 * math.exp(-0.3 * li)
            lam = (jnp.exp(jnp.sum(lam_q1_a[j] * lam_k1_a[j])) - jnp.exp(jnp.sum(lam_q2_a[j] * lam_k2_a[j]))
                   + lam_init).astype(jnp.float32)
            op, kc, vc = mixer_diff_attn(hp, w_qkv_a[j], w_o_a[j], qn_a[j], kn_a[j], subln_a[j], lam, lam_init,
                                         None, None, None)
            os_, _, _ = mixer_diff_attn(hs, w_qkv_a[j], w_o_a[j], qn_a[j], kn_a[j], subln_a[j], lam, lam_init,
                                        rope, cache_a_k[:, j], cache_a_v[:, j])
            new_a_k.append(kc)
            new_a_v.append(vc)
        elif kind == 1:
            op, kc, vc = mixer_window_sink(hp, w_qkv_b[j], w_o_b[j], qn_b[j], kn_b[j], sink_b[j], None, None, None)
            os_, _, _ = mixer_window_sink(hs, w_qkv_b[j], w_o_b[j], qn_b[j], kn_b[j], sink_b[j],
                                          rope, cache_b_k[:, j], cache_b_v[:, j])
            new_b_k.append(kc)
            new_b_v.append(vc)
        elif kind == 2:
            op, kc, vc = mixer_axial_gqa(hp, w_qkv_c[j], w_o_c[j], qn_c[j], kn_c[j], None, None, None)
            os_, _, _ = mixer_axial_gqa(hs, w_qkv_c[j], w_o_c[j], qn_c[j], kn_c[j],
                                        rope, cache_c_k[:, j], cache_c_v[:, j])
            new_c_k.append(kc)
            new_c_v.append(vc)
        else:
            op, sc = mixer_hgrn2(hp, w_in_d[j], w_o_d[j], gn_d[j], lb_all[:, li], zero_state)
            os_, _ = mixer_hgrn2(hs, w_in_d[j], w_o_d[j], gn_d[j], lb_all[:, li], state_d[:, j])
            new_d.append(sc)
        xp = xp + mod_p[2][:, None] * op
        xs = xs + mod_s[2][:, None] * os_
        hp = modulate(rms_norm(xp, norm_ffn[li]), mod_p[3], mod_p[4])
        hs = modulate(rms_norm(xs, norm_ffn[li]), mod_s[3], mod_s[4])
        xp = xp + mod_p[5][:, None] * swiglu(hp, w_ffn_gate[li], w_ffn_up[li], w_ffn_down[li])
        xs = xs + mod_s[5][:, None] * swiglu(hs, w_ffn_gate[li], w_ffn_up[li], w_ffn_down[li])

    return (xp, xs, jnp.stack(new_a_k, axis=1), jnp.stack(new_a_v, axis=1), jnp.stack(new_b_k, axis=1),
            jnp.stack(new_b_v, axis=1), jnp.stack(new_c_k, axis=1), jnp.stack(new_c_v, axis=1),
            jnp.stack(new_d, axis=1))
```

```python
import math, os
from contextlib import ExitStack
import numpy as np
import concourse.bass as bass
import concourse.mybir as mybir
from concourse.bass_utils import run_bass_kernel_spmd

F32 = mybir.dt.float32
BF16 = mybir.dt.bfloat16
AF = mybir.ActivationFunctionType
ALU = mybir.AluOpType
AX = mybir.AxisListType

ENGS = ("pe", "act", "dve", "pool", "sp")
EPS = 1e-6


class Res:
    __slots__ = ("name", "w", "r", "excl")

    def __init__(self, name="", excl=False):
        self.name = name
        self.w = None
        self.r = {}
        self.excl = excl


class Op:
    __slots__ = ("eng", "key", "idx", "fn", "waits", "needs_inc", "inc_val", "is_dma", "epoch")

    def __init__(self, eng, key, idx, fn, is_dma=False):
        self.eng = eng
        self.key = key
        self.idx = idx
        self.fn = fn
        self.waits = []
        self.needs_inc = False
        self.inc_val = None
        self.is_dma = is_dma
        self.epoch = 0


class Prog:
    def __init__(self, nc):
        self.nc = nc
        self.ops = {e: [] for e in ENGS}
        self.streams = {e: [] for e in ENGS}
        self.seen = {e: {} for e in ENGS}
        self.epoch = 0
        self.out_dma_keys = set()
        self.pending_fence = {e: [] for e in ENGS}

    def new_epoch(self):
        self.epoch += 1

    def fence(self, engines=("pe", "act", "dve", "sp"), skip=lambda key: False):
        lasts = [lst[-1] for key, lst in self.streams.items() if lst and not skip(key)]
        for e in engines:
            self.pending_fence[e] = list(lasts)

    def add(self, eng, fn, reads=(), writes=(), dma_key=None, is_out=False):
        is_dma = dma_key is not None
        key = ("dma", dma_key) if is_dma else eng
        if key not in self.streams:
            self.streams[key] = []
        op = Op(eng, key, len(self.streams[key]), fn, is_dma)
        op.epoch = 0 if is_dma else self.epoch
        deps = {}

        def put(d):
            if d is None:
                return
            cur = deps.get(d.key)
            if cur is None or cur.idx < d.idx:
                deps[d.key] = d

        for r in reads:
            put(r.w)
            if r.excl:
                for d in r.r.values():
                    if d.key != key:
                        put(d)
        for w in writes:
            put(w.w)
            for d in w.r.values():
                put(d)
        fence_keys = set()
        if self.pending_fence[eng]:
            for d in self.pending_fence[eng]:
                put(d)
                fence_keys.add(d.key)
            self.pending_fence[eng] = []
        seen = self.seen[eng]
        for k, d in deps.items():
            if k == "pe" and eng == "pe" and not is_dma and k not in fence_keys:
                continue
            if k == eng and not is_dma and d is op:
                continue
            if seen.get(k, -1) >= d.idx:
                continue
            seen[k] = d.idx
            d.needs_inc = True
            op.waits.append(d)
        self.streams[key].append(op)
        self.ops[eng].append(op)
        for r in reads:
            r.r[key] = op
        for w in writes:
            w.w = op
            w.r = {}
        if is_out:
            self.out_dma_keys.add(key)
        return op

    def emit(self, sem_alloc):
        sems = {}
        for key, lst in self.streams.items():
            cnt = {}
            for op in lst:
                if op.is_dma:
                    op.needs_inc = True
                if op.needs_inc:
                    sk = (key, op.epoch)
                    cnt[sk] = cnt.get(sk, 0) + (16 if op.is_dma else 1)
                    op.inc_val = cnt[sk]
                    if sk not in sems:
                        sems[sk] = sem_alloc("s%d" % len(sems))
        self.sems = sems
        final_out = []
        for key in self.out_dma_keys:
            last = self.streams[key][-1]
            final_out.append((sems[(key, last.epoch)], last.inc_val))

        def run(engname, eng):
            for op in self.ops[engname]:
                for d in op.waits:
                    eng.wait_ge(sems[(d.key, d.epoch)], d.inc_val)
                ins = op.fn(eng)
                if op.needs_inc:
                    ins.then_inc(sems[(op.key, op.epoch)], 16 if op.is_dma else 1)
            if engname == "sp":
                for s, v in final_out:
                    eng.wait_ge(s, v)

        return run


class Rot:
    def __init__(self, items):
        self.items = items
        self.i = 0

    def next(self):
        it = self.items[self.i % len(self.items)]
        self.i += 1
        return it


def build(nlayers=4, phases="SP"):
    nc = bass.Bass("TRN2", target_bir_lowering=False)
    P = Prog(nc)
    es = ExitStack()

    def din(name, shape):
        return nc.dram_tensor(name, list(shape), F32, kind="ExternalInput").ap()

    def dout(name, shape):
        return nc.dram_tensor(name, list(shape), F32, kind="ExternalOutput").ap()

    xs_d = din("xs", [2048, 1024])
    xp_d = din("xp", [1024, 1024])
    vecs_d = din("vecs", [384, 128])
    cak_d = din("cak", [512, 1024])
    cav_d = din("cav", [512, 1024])
    cbk_d = din("cbk", [512, 256])
    cbv_d = din("cbv", [512, 256])
    cck_d = din("cck", [512, 256])
    ccv_d = din("ccv", [512, 256])
    sd_d = din("sd", [2, 8, 128, 128])
    w_ada_d = din("w_ada", [4, 1024, 6144])
    wg_d = din("w_ffn_gate", [4, 1024, 2816])
    wu_d = din("w_ffn_up", [4, 1024, 2816])
    wd_d = din("w_ffn_down", [4, 2816, 1024])
    wqkv_a_d = din("w_qkv_a", [1024, 3072])
    wo_a_d = din("w_o_a", [1024, 1024])
    wqkv_b_d = din("w_qkv_b", [1024, 1536])
    wo_b_d = din("w_o_b", [1024, 1024])
    wqkv_c_d = din("w_qkv_c", [1024, 1536])
    wo_c_d = din("w_o_c", [1024, 1024])
    win_d_d = din("w_in_d", [1024, 5120])
    wo_d_d = din("w_o_d", [1024, 1024])
    small_d = din("small", [16, 128])
    c_ident_d = din("c_ident", [128, 128])
    c_prot_d = din("c_prot", [128, 128])
    c_maskb_d = din("c_maskb", [6, 128, 512])
    c_maskh_d = din("c_maskh", [2, 64, 64])
    c_reset_d = din("c_reset", [128, 512])
    c_cos_d = din("c_cos", [128, 2048])
    c_sin_d = din("c_sin", [128, 2048])

    ys_d = dout("ys", [2048, 1024])
    yp_d = dout("yp", [1024, 1024])
    nak_d = dout("nak", [1024, 1024])
    nav_d = dout("nav", [1024, 1024])
    nbk_d = dout("nbk", [1024, 256])
    nbv_d = dout("nbv", [1024, 256])
    nck_d = dout("nck", [1024, 256])
    ncv_d = dout("ncv", [1024, 256])
    nsd_d = dout("nsd", [4, 2, 8, 128, 128])

    def sb(name, shape, dt):
        return es.enter_context(nc.sbuf_tensor(name, shape, dt))

    xT_t = sb("xT", [128, 8 * 2048], F32)
    hT_t = sb("hT", [128, 8 * 2048], BF16)
    oT_t = sb("oT", [128, 8 * 2048], BF16)
    xT = xT_t[:, :].rearrange("p (c t) -> p c t", c=8)
    hT = hT_t[:, :].rearrange("p (c t) -> p c t", c=8)
    oT = oT_t[:, :].rearrange("p (c t) -> p c t", c=8)
    Wt = [sb("W%d" % i, [128, 6144], BF16) for i in range(2)]
    scrB = sb("scrB", [128, 13312], BF16)
    scrF = sb("scrF", [128, 5120], F32)
    ident32 = sb("ident32", [128, 128], F32)
    prot32 = sb("prot32", [128, 128], F32)
    onesb = sb("onesb", [128, 128], BF16)
    mean1024 = sb("mean1024", [128, 128], BF16)
    blk64 = sb("blk64", [128, 128], BF16)
    mean128 = sb("mean128", [128, 128], BF16)
    vT = sb("vT", [128, 384], F32)
    modt = sb("modt", [128, 4 * 48 * 2], F32)
    mod = modt[:, :].rearrange("p (l j r) -> p l j r", l=4, j=48)
    dert = sb("dert", [128, 4 * 2 * 8 * 2], F32)
    der = dert[:, :].rearrange("p (l k c r) -> p l k c r", l=4, k=2, c=8)
    smallT = sb("smallT", [128, 16], F32)
    misc = sb("misc", [128, 64], F32)
    esink = sb("esink", [128, 16], F32)
    lbt = sb("lbt", [128, 64], F32)
    lbv = sb("lbv", [128, 32], F32)
    condS = sb("condS", [128, 16], F32)
    maskhf = sb("maskhf", [64, 128], F32)
    resetm = sb("resetm", [128, 512], F32)
    ps = [es.enter_context(nc.psum_tensor("ps%d" % i, [128, 512], F32)) for i in range(8)]
    psR = [Res("ps%d" % i, excl=True) for i in range(8)]

    constR = Res("const")
    modR = [Res("mod%d" % i) for i in range(4)]
    xR = [[Res() for t in range(4)] for c in range(8)]
    hR = [[Res() for t in range(4)] for c in range(8)]
    oR = [[Res() for t in range(4)] for c in range(8)]
    WR = [[Res() for i in range(6)] for s in range(2)]
    SLOT = [Res(), Res()]

    eps_ap = misc[:, 0:1]
    neglam_ap = misc[:, 1:2]
    subg_ap = misc[:, 2:3]
    one_ap = misc[:, 3:4]

    def MM(out, lhsT, rhs, start, stop, R, Wr):
        P.add("pe", lambda e: e.matmul(out, lhsT, rhs, start=start, stop=stop), R, Wr)

    def TR(out, in_, ident, R, Wr):
        P.add("pe", lambda e: e.transpose(out, in_, ident), R, Wr)

    def ACT(out, in_, func, R, Wr, bias=None, scale=None):
        kw = {}
        if bias is not None:
            kw["bias"] = bias
        if scale is not None:
            kw["scale"] = scale
        P.add("act", lambda e: e.activation(out, in_, func, **kw), R, Wr)

    def TT(out, in0, in1, op, R, Wr, eng="dve"):
        P.add(eng, lambda e: e.tensor_tensor(out, in0, in1, op=op), R, Wr)

    def STT(out, in0, scalar, in1, op0, op1, R, Wr, eng="dve"):
        P.add(eng, lambda e: e.scalar_tensor_tensor(out, in0, scalar, in1, op0=op0, op1=op1), R, Wr)

    def TS(out, in0, s1, s2, op0, op1, R, Wr, eng="dve"):
        if s2 is None:
            P.add(eng, lambda e: e.tensor_scalar(out, in0, s1, None, op0=op0), R, Wr)
        else:
            P.add(eng, lambda e: e.tensor_scalar(out, in0, s1, s2, op0=op0, op1=op1), R, Wr)

    def CP(out, in_, R, Wr, eng="dve"):
        if eng == "act":
            P.add("act", lambda e: e.activation(out, in_, AF.Copy), R, Wr)
        else:
            P.add(eng, lambda e: e.tensor_copy(out, in_), R, Wr)

    def RECIP(out, in_, R, Wr):
        P.add("dve", lambda e: e.reciprocal(out, in_), R, Wr)

    def MEMSET(out, val, R, Wr, eng="dve"):
        P.add(eng, lambda e: e.memset(out, val), R, Wr)

    def DMA(q, out, in_, key, R, Wr, is_out=False):
        P.add(q, lambda e: e.dma_start(out=out, in_=in_), R, Wr, dma_key=key, is_out=is_out)

    pwi = [0]

    def pw():
        i = pwi[0] % 4
        pwi[0] += 1
        return ps[i], psR[i]

    evi = [0]

    def EVAC(out, in_, R, Wr):
        evi[0] += 1
        CP(out, in_, R, Wr, eng="act" if evi[0] % 2 else "dve")

    wjob = [0]

    def wload(parts):
        s = wjob[0] % 2
        wjob[0] += 1
        for i, (dstf, src) in enumerate(parts):
            wr = [WR[s][i]] + ([SLOT[s]] if i == 0 else [])
            DMA("pool", dstf(Wt[s]), src, ("w", s, i), [], wr)
        return s

    def wview(s_t, off, ncol):
        return s_t[:, off:off + 8 * ncol].rearrange("p (c n) -> p c n", c=8)

    def wsrc(w2d, col0, ncol):
        return w2d.rearrange("(c p) n -> p c n", p=128)[:, :, col0:col0 + ncol]

    def prologue():
        lamt = scrF[:, 384:640]
        DMA("sp", ident32[:, :], c_ident_d, "c0", [], [constR])
        DMA("sp", prot32[:, :], c_prot_d, "c1", [], [constR])
        DMA("sp", resetm[:, :], c_reset_d, "c2", [], [constR])
        DMA("sp", maskhf[:, :].rearrange("p (d c) -> p d c", d=2), c_maskh_d.rearrange("d s c -> s d c"), "c3", [], [constR])
        DMA("sp", smallT[:, :], small_d.rearrange("r p -> p r"), "c4", [], [constR])
        DMA("sp", esink[:, :], small_d[9, 0:16].partition_broadcast(128), "c5", [], [constR])
        for i in range(4):
            DMA("sp", lamt[:, i * 64:(i + 1) * 64], small_d[3 + i, 0:64].partition_broadcast(128), "c6", [], [constR])
        MEMSET(onesb[:, :], 1.0, [], [constR])
        MEMSET(mean1024[:, :], 1.0 / 1024, [], [constR])
        MEMSET(mean128[:, :], 1.0 / 128, [], [constR])
        MEMSET(blk64[:, :], 0.0, [], [constR])
        MEMSET(blk64[0:64, 0:64], 1.0 / 64, [], [constR])
        MEMSET(blk64[64:128, 64:128], 1.0 / 64, [], [constR])
        MEMSET(misc[:, :], 0.0, [], [constR])
        MEMSET(misc[:, 0:1], EPS, [], [constR])
        MEMSET(misc[:, 3:4], 1.0, [], [constR])
        stg = scrF[:, 0:384].rearrange("p (k f) -> p k f", k=3)
        stgR = Res()
        DMA("sp", stg, vecs_d.rearrange("(k p) f -> p k f", p=128), "c7", [], [stgR])
        for k in range(3):
            TR(ps[0][:, k * 128:(k + 1) * 128], stg[:, k, :], ident32[:, :], [stgR, constR], [psR[0]])
        CP(vT[:, :], ps[0][:, 0:384], [psR[0]], [constR])
        cS = condS[:, :].rearrange("p (c r) -> p c r", r=2)
        for r in range(2):
            ACT(cS[:, :, r], vT[:, 320 + 8 * r:328 + 8 * r], AF.Silu, [constR], [constR])
        ACT(esink[:, :], esink[:, :], AF.Exp, [constR], [constR])
        lam_init = 0.8 - 0.6 * math.exp(-0.3 * 0)
        TT(lamt[:, 0:64], lamt[:, 0:64], lamt[:, 64:128], ALU.mult, [constR], [constR])
        TT(lamt[:, 128:192], lamt[:, 128:192], lamt[:, 192:256], ALU.mult, [constR], [constR])
        P.add("dve", lambda e: e.reduce_sum(misc[:, 8:9], lamt[:, 0:64], axis=AX.X), [constR], [constR])
        P.add("dve", lambda e: e.reduce_sum(misc[:, 9:10], lamt[:, 128:192], axis=AX.X), [constR], [constR])
        ACT(misc[:, 8:10], misc[:, 8:10], AF.Exp, [constR], [constR])
        TT(misc[:, 10:11], misc[:, 9:10], misc[:, 8:9], ALU.subtract, [constR], [constR])
        TS(misc[:, 1:2], misc[:, 10:11], -lam_init, None, ALU.add, None, [constR], [constR])
        TS(misc[:, 2:3], smallT[:, 2:3], 1.0 - lam_init, None, ALU.mult, None, [constR], [constR])
        ACT(lbt[:, :], vT[:, 256:320], AF.Exp, [constR], [constR])
        lb4 = lbt[:, :].rearrange("p (d l c) -> p d l c", d=2, l=4)
        lv = lbv[:, :].rearrange("p (k d c) -> p k d c", k=2, d=2)
        li_d = 3
        for d in range(2):
            TT(lv[:, 0, d, :], lb4[:, d, 1, :], lb4[:, d, 2, :], ALU.add, [constR], [constR])
            TT(lv[:, 0, d, :], lv[:, 0, d, :], lb4[:, d, 3, :], ALU.add, [constR], [constR])
            TT(lv[:, 1, d, :], lv[:, 0, d, :], lb4[:, d, 0, :], ALU.add, [constR], [constR])
            RECIP(lv[:, 1, d, :], lv[:, 1, d, :], [constR], [constR])
            TT(lv[:, 0, d, :], lv[:, 0, d, :], lv[:, 1, d, :], ALU.mult, [constR], [constR])
            TS(lv[:, 1, d, :], lv[:, 0, d, :], -1.0, 1.0, ALU.mult, ALU.add, [constR], [constR])
        for step in ada_steps(0, [(scrF[:, 1024 + i * 2048:1024 + (i + 1) * 2048].rearrange("p (c n) -> p c n", c=8), Res()) for i in range(2)], 256, ps[4], psR[4]):
            pass

    def ada_steps(li, slots, ncol, acc, accR):
        cS = condS[:, :].rearrange("p (c r) -> p c r", r=2)
        rot = Rot(slots)
        ngrp = 6144 // ncol
        for jg in range(ngrp):
            wsl, wslR = rot.next()
            DMA("sp", wsl, wsrc(w_ada_d[li], jg * ncol, ncol), ("ada", rot.i % len(slots)), [], [wslR])
            for jj in range(ncol // 128):
                j = jg * (ncol // 128) + jj
                for kc in range(8):
                    MM(acc[:, 2 * j:2 * j + 2], wsl[:, kc, jj * 128:(jj + 1) * 128], cS[:, kc, :], kc == 0, kc == 7,
                       [wslR, constR], [accR])
            yield
        TT(mod[:, li, :, :], acc[:, 0:96].rearrange("p (j r) -> p j r", r=2),
           vT[:, li * 48:(li + 1) * 48].unsqueeze(2).to_broadcast([128, 48, 2]), ALU.add, [accR, constR], [modR[li]])
        STT(der[:, li, 0, :, :], mod[:, li, 8:16, :], 1.0, vT[:, 192 + li * 8:200 + li * 8].unsqueeze(2).to_broadcast([128, 8, 2]),
            ALU.add, ALU.mult, [constR, modR[li]], [modR[li]])
        STT(der[:, li, 1, :, :], mod[:, li, 32:40, :], 1.0, vT[:, 224 + li * 8:232 + li * 8].unsqueeze(2).to_broadcast([128, 8, 2]),
            ALU.add, ALU.mult, [constR, modR[li]], [modR[li]])
        yield

    def modv(li, m, c, r):
        return mod[:, li, m * 8 + c, r:r + 1]

    class Dense:
        pass

    def dense_layout():
        L = Dense()
        L.sq = Rot([(scrB[:, i * 512:(i + 1) * 512], Res()) for i in range(2)])
        L.a = Rot([(scrB[:, 1024 + i * 1024:1024 + (i + 1) * 1024].rearrange("p (j n) -> p j n", j=2), Res()) for i in range(2)])
        L.sd = Rot([(scrF[:, i * 512:(i + 1) * 512], Res()) for i in range(2)])
        L.tmp = Rot([(scrF[:, 1024 + i * 512:1024 + (i + 1) * 512], Res()) for i in range(2)])
        L.s = Rot([(scrF[:, 2048 + i * 512:2048 + (i + 1) * 512], Res()) for i in range(2)])
        L.stg = Rot([(scrF[:, 3072 + i * 1024:3072 + (i + 1) * 1024], Res()) for i in range(2)])
        return L

    DL = dense_layout()

    def load_x(x_d, T):
        L = DL
        for t in range(T // 512):
            for tb in range(4):
                stg, stgR = L.stg.next()
                r0 = t * 512 + tb * 128
                DMA("sp", stg, x_d[r0:r0 + 128, :], ("xin", L.stg.i % 2), [], [stgR])
                for c in range(8):
                    TR(ps[c][:, tb * 128:(tb + 1) * 128], stg[:, c * 128:(c + 1) * 128], ident32[:, :], [stgR, constR], [psR[c]])
            for c in range(8):
                EVAC(xT[:, c, t * 512:(t + 1) * 512], ps[c][:, :], [psR[c]], [xR[c][t]])

    def store_y(y_d, T):
        L = DL
        for tb in range(T // 128):
            t = tb // 4
            stg, stgR = L.stg.next()
            for c in range(8):
                b = 4 + (tb % 2) * 2 + c // 4
                TR(ps[b][:, (c % 4) * 128:(c % 4 + 1) * 128], xT[:, c, tb * 128:(tb + 1) * 128], ident32[:, :],
                   [xR[c][t], constR], [psR[b]])
            for hf in range(2):
                b = 4 + (tb % 2) * 2 + hf
                EVAC(stg[:, hf * 512:(hf + 1) * 512], ps[b][:, :], [psR[b]], [stgR])
            DMA("sp", y_d[tb * 128:(tb + 1) * 128, :], stg, ("yout", L.stg.i % 2), [stgR], [], is_out=True)

    def norm_mod(L, NT, li, which, r):
        for t in range(NT):
            norm_tile(L, t, li, which, r)

    def norm_tile(L, t, li, which, r):
        if True:
            sl = slice(t * 512, (t + 1) * 512)
            pt, pr = pw()
            for c in range(8):
                sq, sqr = L.sq.next()
                ACT(sq, xT[:, c, sl], AF.Square, [xR[c][t]], [sqr])
                MM(pt[:, :], mean1024[:, :], sq, c == 0, c == 7, [sqr, constR], [pr])
            sd, sdr = L.sd.next()
            ACT(sd, pt[:, :], AF.Ln, [pr, constR], [sdr], bias=eps_ap)
            ACT(sd, sd, AF.Exp, [sdr], [sdr], scale=-0.5)
            for c in range(8):
                tmp, tr = L.tmp.next()
                TT(tmp, xT[:, c, sl], sd, ALU.mult, [xR[c][t], sdr], [tr])
                ACT(hT[:, c, sl], tmp, AF.Identity, [tr, modR[li]], [hR[c][t]],
                    bias=modv(li, which * 3 + 0, c, r), scale=der[:, li, which, c, r:r + 1])

    def ffn(L, NT, li, r, side=None, after=None):
        fdefer, fflush = make_defer(1)
        fin_after = []
        nb = 3 if side is not None else 4
        for g in range(11):
            s = wload([
                (lambda w: wview(w, 0, 256), wsrc(wg_d[li], g * 256, 256)),
                (lambda w: wview(w, 2048, 256), wsrc(wu_d[li], g * 256, 256)),
                (lambda w: w[:, 4096:6144].rearrange("p (j n) -> p j n", j=2),
                 wd_d[li][g * 256:(g + 1) * 256, :].rearrange("(j p) n -> p j n", p=128)),
            ])
            Wg = wview(Wt[s], 0, 256)
            Wu = wview(Wt[s], 2048, 256)
            Wd = Wt[s][:, 4096:6144].rearrange("p (j n) -> p j n", j=2)
            for t in range(NT):
                sl = slice(t * 512, (t + 1) * 512)
                a, aR = L.a.next()
                for j in range(2):
                    pg, pgr = pw()
                    for c in range(8):
                        MM(pg[:, :], Wg[:, c, j * 128:(j + 1) * 128], hT[:, c, sl], c == 0, c == 7, [WR[s][0], SLOT[s], hR[c][t]], [pgr])
                    pu, pur = pw()
                    for c in range(8):
                        MM(pu[:, :], Wu[:, c, j * 128:(j + 1) * 128], hT[:, c, sl], c == 0, c == 7, [WR[s][1], SLOT[s], hR[c][t]], [pur])
                    sg, sgr = L.s.next()
                    ACT(sg, pg[:, :], AF.Silu, [pgr], [sgr])
                    TT(a[:, j, :], sg, pu[:, :], ALU.mult, [sgr, pur], [aR])

                if side is not None:
                    for _ in range(2):
                        next(side, None)

                def down(s=s, Wd=Wd, a=a, aR=aR, t=t, sl=sl, g=g):
                    if g == 10 and after is not None:
                        fin_after.append(t)
                    for cp in range(8):
                        b = 4 + cp % nb
                        for j in range(2):
                            MM(ps[b][:, :], Wd[:, j, cp * 128:(cp + 1) * 128], a[:, j, :], j == 0, j == 1, [WR[s][2], SLOT[s], aR], [psR[b]])
                        STT(xT[:, cp, sl], ps[b][:, :], modv(li, 5, cp, r), xT[:, cp, sl], ALU.mult, ALU.add,
                            [psR[b], xR[cp][t], modR[li]], [xR[cp][t]])
                    while fin_after:
                        after(fin_after.pop(0))
                fdefer(down)
        fflush()
        if side is not None:
            for _ in side:
                pass

    def wo_proj(NT, li, r, wo_d, after=None):
        for hf in range(2):
            s = wload([(lambda w: wview(w, 0, 512), wsrc(wo_d, hf * 512, 512))])
            Wo = wview(Wt[s], 0, 512)
            for t in range(NT):
                sl = slice(t * 512, (t + 1) * 512)
                for cl in range(4):
                    cp = hf * 4 + cl
                    b = 4 + cl
                    for c in range(8):
                        MM(ps[b][:, :], Wo[:, c, cl * 128:(cl + 1) * 128], oT[:, c, sl], c == 0, c == 7, [WR[s][0], SLOT[s], oR[c][t]], [psR[b]])
                    STT(xT[:, cp, sl], ps[b][:, :], modv(li, 2, cp, r), xT[:, cp, sl], ALU.mult, ALU.add,
                        [psR[b], xR[cp][t], modR[li]], [xR[cp][t]])
                if hf == 1 and after is not None:
                    after(t)

    def qk_proj(A, lhs_fn, lhsR, M, t, gain_ap, rope, dst, dstR):
        sl = slice(t * 512, (t + 1) * 512)
        pt, pr = pw()
        for c in range(8):
            MM(pt[0:M, :], lhs_fn(c), hT[:, c, sl], c == 0, c == 7, lhsR + [hR[c][t]], [pr])
        sq, sqr = A.sq.next()
        ACT(sq[0:M, :], pt[0:M, :], AF.Square, [pr], [sqr])
        pm, pmr = pw()
        MM(pm[0:M, :], blk64[0:M, 0:M], sq[0:M, :], True, True, [sqr, constR], [pmr])
        t1, t1r = A.tmp.next()
        ACT(t1[0:M, :], pm[0:M, :], AF.Ln, [pmr, constR], [t1r], bias=eps_ap[0:M, :])
        ACT(t1[0:M, :], t1[0:M, :], AF.Exp, [t1r], [t1r], scale=-0.5)
        t2, t2r = A.tmp.next()
        STT(t2[0:M, :], pt[0:M, :], gain_ap, t1[0:M, :], ALU.mult, ALU.mult, [pr, t1r, constR], [t2r])
        if rope is not None:
            cos_ap, sin_ap, ropeR = rope
            pq, pqr = pw()
            MM(pq[0:M, :], prot32[0:M, 0:M], t2[0:M, :], True, True, [t2r, constR], [pqr])
            t3, t3r = A.tmp.next()
            TT(t3[0:M, :], t2[0:M, :], cos_ap[0:M, :], ALU.mult, [t2r, ropeR], [t3r])
            TT(t1[0:M, :], pq[0:M, :], sin_ap[0:M, :], ALU.mult, [pqr, ropeR, t1r], [t1r])
            TT(dst, t3[0:M, :], t1[0:M, :], ALU.add, [t3r, t1r], [dstR])
        else:
            CP(dst, t2[0:M, :], [t2r], [dstR], eng="act")
        return t2, t2r

    def qk_chain(A, lane, lhs_fn, lhsR, t, gain_ap, use_rope, dst, dstR, after=None):
        bX, bY = 2 * lane, 2 * lane + 1
        pX, pXr, pY, pYr = ps[bX], psR[bX], ps[bY], psR[bY]
        (t1, t1r), (t2, t2r) = A.ltmp[lane]
        sq, sqr = A.lsq[lane]
        sl = slice(t * 512, (t + 1) * 512)
        if use_rope:
            slot, ropeR = A.lrope[lane]
            cos_ap, sin_ap = slot[:, 0:512], slot[:, 512:1024]
            DMA("sp", cos_ap, c_cos_d[:, t * 512:(t + 1) * 512], ("rope", lane, 0), [], [ropeR])
            DMA("sp", sin_ap, c_sin_d[:, t * 512:(t + 1) * 512], ("rope", lane, 1), [], [ropeR])
        for c in range(8):
            MM(pX[:, :], lhs_fn(c), hT[:, c, sl], c == 0, c == 7, lhsR + [hR[c][t]], [pXr])
        ACT(sq, pX[:, :], AF.Square, [pXr], [sqr])
        yield
        MM(pY[:, :], blk64[:, :], sq, True, True, [sqr, constR], [pYr])
        ACT(t1, pY[:, :], AF.Ln, [pYr, constR], [t1r], bias=eps_ap)
        yield
        ACT(t1, t1, AF.Exp, [t1r], [t1r], scale=-0.5)
        yield
        STT(t2, pX[:, :], gain_ap, t1, ALU.mult, ALU.mult, [pXr, t1r, constR], [t2r])
        yield
        if use_rope:
            MM(pX[:, :], prot32[:, :], t2, True, True, [t2r, constR], [pXr])
            yield
            TT(t1, pX[:, :], sin_ap, ALU.mult, [pXr, ropeR, t1r], [t1r])
            yield
            TT(t2, t2, cos_ap, ALU.mult, [t2r, ropeR], [t2r])
            yield
            TT(dst, t2, t1, ALU.add, [t2r, t1r], [dstR])
        else:
            CP(dst, t2, [t2r], [dstR], eng="act")
            if after is not None:
                after(t2, t2r, bY)
        yield

    def run_lanes(chains, fillers):
        active = [None, None]
        it = iter(chains)
        while True:
            progressed = False
            for lane in range(2):
                if active[lane] is None:
                    nxt = next(it, None)
                    if nxt is not None:
                        active[lane] = nxt(lane)
                if active[lane] is not None:
                    if next(active[lane], "END") == "END":
                        active[lane] = None
                    progressed = True
            if fillers:
                fillers.pop(0)()
                progressed = True
            if not progressed:
                break

    def lane_setup(A, S):
        tmpslots = [(scrF[:, i * 512:(i + 1) * 512], Res()) for i in range(4)]
        A.tmp = Rot(tmpslots)
        A.ltmp = [[tmpslots[0], tmpslots[1]], [tmpslots[2], tmpslots[3]]]
        if S:
            A.lrope = [(scrF[:, 3072 + i * 1024:4096 + i * 1024], Res()) for i in range(2)]

    def load_rope(A, t):
        slot, slotR = A.rope.next()
        cos_ap = slot[:, 0:512]
        sin_ap = slot[:, 512:1024]
        DMA("sp", cos_ap, c_cos_d[:, t * 512:(t + 1) * 512], ("rope", A.rope.i % 2, 0), [], [slotR])
        DMA("sp", sin_ap, c_sin_d[:, t * 512:(t + 1) * 512], ("rope", A.rope.i % 2, 1), [], [slotR])
        return cos_ap, sin_ap, slotR

    def out_tokmajor(A, src, srcR, M, out_view, ncol=None, bank=None):
        pt, pr = pw() if bank is None else (ps[bank], psR[bank])
        for tb in range(4):
            TR(pt[:, tb * M:(tb + 1) * M], src[0:M, tb * 128:(tb + 1) * 128], ident32[0:M, 0:M], [srcR, constR], [pr])
        stg, stgR = A.ostg.next()
        EVAC(stg[:, 0:4 * M], pt[:, 0:4 * M], [pr], [stgR])
        sv = stg[:, 0:4 * M].rearrange("p (tb f) -> p tb f", tb=4)
        if ncol is not None:
            sv = sv[:, :, 0:ncol]
        DMA("pool", out_view, sv, ("okv", A.ostg.i % 4), [stgR], [], is_out=True)

    class Lay:
        pass

    def make_defer(look):
        q = []

        def defer(fn):
            q.append(fn)
            while len(q) > look:
                q.pop(0)()

        def flush():
            while q:
                q.pop(0)()

        return defer, flush

    def mixer_A(ph):
        S = ph == "S"
        T = 2048 if S else 1024
        NT = T // 512
        koff = 512 if S else 0
        voff = 4 if S else 0
        A = Lay()
        kpair = scrB[:, 0:2560]
        qslots = [(scrB[:, 2560 + i * 512:3072 + i * 512], Res()) for i in range(4)]
        Vh = scrB[:, 5120:7680].rearrange("p (k v) -> p k v", v=128)
        kR = [Res() for _ in range(5)]
        vR = [Res() for _ in range(5)]
        A.pt = Rot([(scrB[:, 8704 + i * 512:9216 + i * 512], Res()) for i in range(6)])
        sqslots = [(scrB[:, 11776 + i * 512:12288 + i * 512], Res()) for i in range(2)]
        A.sq = Rot(sqslots)
        A.lsq = sqslots
        lane_setup(A, S)
        if S:
            stgK = scrF[:, 2048:2560].rearrange("p (t f) -> p t f", t=4)
            stgV = scrF[:, 2560:3072].rearrange("p (t f) -> p t f", t=4)
            stgKR, stgVR = Res(), Res()
        else:
            A.ostg = Rot([(scrF[:, 2048 + i * 512:2560 + i * 512], Res()) for i in range(4)])
        qg = smallT[:, 0:1]
        kg = smallT[:, 1:2]
        defer, flush = make_defer(2)
        vbank = [0]
        for h in range(8):
            flush()
            s = wload([
                (lambda w: wview(w, 0, 128), wsrc(wqkv_a_d, h * 128, 128)),
                (lambda w: wview(w, 1024, 128), wsrc(wqkv_a_d, 1024 + h * 128, 128)),
                (lambda w: wview(w, 2048, 128), wsrc(wqkv_a_d, 2048 + h * 128, 128)),
            ])
            Wq = wview(Wt[s], 0, 128)
            Wk = wview(Wt[s], 1024, 128)
            Wv = wview(Wt[s], 2048, 128)
            if S:
                DMA("sp", stgK, cak_d[:, h * 128:(h + 1) * 128].rearrange("(t p) f -> p t f", p=128), "stgK", [], [stgKR])
                pt, pr = ps[7], psR[7]
                for tb in range(4):
                    TR(pt[:, tb * 128:(tb + 1) * 128], stgK[:, tb, :], ident32[:, :], [stgKR, constR], [pr])
                CP(kpair[:, 0:512], pt[:, :], [pr], [kR[0]], eng="act")
                DMA("sp", stgV, cav_d[:, h * 128:(h + 1) * 128].rearrange("(t p) f -> p t f", p=128), "stgV", [], [stgVR])
                CP(Vh[:, 0:4, :], stgV, [stgVR], [vR[0]], eng="dve")
            chains = []
            for t in range(NT):
                c0 = koff + t * 512
                aft = None
                if not S:
                    aft = (lambda kn, knR, bank, t=t, h=h: out_tokmajor(
                        A, kn, knR, 128, nak_d[t * 512:(t + 1) * 512, h * 128:(h + 1) * 128].rearrange("(tb p) f -> p tb f", p=128), bank=bank))
                chains.append(lambda lane, t=t, c0=c0, aft=aft, s=s, Wk=Wk: qk_chain(
                    A, lane, lambda c: Wk[:, c, :], [WR[s][1], SLOT[s]], t, kg, S, kpair[:, c0:c0 + 512], kR[1 + t], after=aft))
            for t in range(NT):
                qp, qR = qslots[t]
                chains.append(lambda lane, t=t, qp=qp, qR=qR, s=s, Wq=Wq: qk_chain(
                    A, lane, lambda c: Wq[:, c, :], [WR[s][0], SLOT[s]], t, qg, S, qp, qR))
            fillers = []
            for t in range(NT):
                vbank[0] += 1
                b = 4 + vbank[0] % 3
                pv_, pvr = ps[b], psR[b]
                for tb in range(4):
                    def vblk(t=t, tb=tb, pv_=pv_, pvr=pvr, s=s, Wv=Wv):
                        for c in range(8):
                            MM(pv_[:, tb * 128:(tb + 1) * 128], hT[:, c, t * 512 + tb * 128:t * 512 + (tb + 1) * 128], Wv[:, c, :],
                               c == 0, c == 7, [WR[s][2], SLOT[s], hR[c][t]], [pvr])
                    fillers.append(vblk)

                def vev(t=t, pv_=pv_, pvr=pvr, h=h):
                    CP(Vh[:, voff + 4 * t:voff + 4 * t + 4, :], pv_[:, :].rearrange("p (k v) -> p k v", v=128), [pvr], [vR[1 + t]], eng="act")
                    if not S:
                        stg, stgR = A.ostg.next()
                        CP(stg, pv_[:, :], [pvr], [stgR], eng="act")
                        DMA("pool", nav_d[t * 512:(t + 1) * 512, h * 128:(h + 1) * 128].rearrange("(tb p) f -> p tb f", p=128),
                            stg.rearrange("p (tb f) -> p tb f", tb=4), ("okv", A.ostg.i % 4), [stgR], [], is_out=True)
                fillers.append(vev)
            run_lanes(chains, fillers)
            for t in range(NT):
                qp, qR = qslots[t]
                if S:
                    segs = [(0, 512, [(kt * 128, kt, kR[0] if kt < 4 else kR[1 + (kt - 4) // 4], vR[0] if kt < 4 else vR[1 + (kt - 4) // 4]) for kt in range(20)])]
                else:
                    segs = []
                    for sq_ in range(2):
                        p0 = t * 512 + sq_ * 256
                        segs.append((sq_ * 256, 256, [(p0 + kt * 128, (p0 // 128) + kt, kR[1 + t], vR[1 + t]) for kt in range(2)]))
                for (qoff, N, ktiles) in segs:
                    nk = len(ktiles)
                    for i, (kc0, vi, kr, vr) in enumerate(ktiles):
                        sts = []
                        for m in range(2):
                            pst, pstr = pw()
                            MM(pst[:, 0:N], kpair[64 * m:64 * m + 64, kc0:kc0 + 128], qp[64 * m:64 * m + 64, qoff:qoff + N], True, True, [kr, qR], [pstr])
                            sts.append((pst, pstr))
                        for m in range(2):
                            pst, pstr = sts[m]
                            Pt, PtR = A.pt.next()
                            ACT(Pt[:, 0:N], pst[:, 0:N], AF.Exp, [pstr], [PtR], scale=0.125)

                            def pv(m=m, vi=vi, vr=vr, Pt=Pt, PtR=PtR, i=i, nk=nk, N=N, qoff=qoff):
                                MM(ps[4 + 2 * m][:, qoff:qoff + N], Vh[:, vi, :], Pt[:, 0:N], i == 0, i == nk - 1, [vr, PtR], [psR[4 + 2 * m]])
                                MM(ps[5 + 2 * m][:, qoff:qoff + N], onesb[:, :], Pt[:, 0:N], i == 0, i == nk - 1, [constR, PtR], [psR[5 + 2 * m]])
                            defer(pv)

                if True:
                    def fin(N=512, h=h, t=t, qoff=0):
                        ta, tar = A.tmp.next()
                        tb_, tbr = A.tmp.next()
                        tc_, tcr = A.tmp.next()
                        ACT(ta[:, 0:N], ps[5][:, 0:N], AF.Ln, [psR[5]], [tar])
                        ACT(ta[:, 0:N], ta[:, 0:N], AF.Exp, [tar], [tar], scale=-1.0)
                        TT(ta[:, 0:N], ps[4][:, 0:N], ta[:, 0:N], ALU.mult, [psR[4], tar], [tar])
                        ACT(tb_[:, 0:N], ps[7][:, 0:N], AF.Ln, [psR[7]], [tbr])
                        ACT(tb_[:, 0:N], tb_[:, 0:N], AF.Exp, [tbr], [tbr], scale=-1.0)
                        TT(tb_[:, 0:N], ps[6][:, 0:N], tb_[:, 0:N], ALU.mult, [psR[6], tbr], [tbr])
                        STT(tc_[:, 0:N], tb_[:, 0:N], neglam_ap, ta[:, 0:N], ALU.mult, ALU.add, [tar, tbr, constR], [tcr])
                        sq, sqr = A.sq.next()
                        ACT(sq[:, 0:N], tc_[:, 0:N], AF.Square, [tcr], [sqr])
                        pm, pmr = pw()
                        MM(pm[:, 0:N], mean128[:, :], sq[:, 0:N], True, True, [sqr, constR], [pmr])
                        ACT(ta[:, 0:N], pm[:, 0:N], AF.Ln, [pmr, constR, tar], [tar], bias=eps_ap)
                        ACT(ta[:, 0:N], ta[:, 0:N], AF.Exp, [tar], [tar], scale=-0.5)
                        STT(oT[:, h, t * 512 + qoff:t * 512 + qoff + N], tc_[:, 0:N], subg_ap, ta[:, 0:N], ALU.mult, ALU.mult,
                            [tcr, tar, constR], [oR[h][t]])
                    defer(fin)
            flush()

    def mixer_G(ph, kind):
        S = ph == "S"
        T = 2048 if S else 1024
        NT = T // 512
        koff = 512 if S else 0
        voff = 4 if S else 0
        isB = kind == "B"
        wqkv_d = wqkv_b_d if isB else wqkv_c_d
        ck_d, cv_d = (cbk_d, cbv_d) if isB else (cck_d, ccv_d)
        nk_d, nv_d = (nbk_d, nbv_d) if isB else (nck_d, ncv_d)
        qg = smallT[:, 7:8] if isB else smallT[:, 10:11]
        kg = smallT[:, 8:9] if isB else smallT[:, 11:12]
        A = Lay()
        kT2 = scrB[:, 0:2560]
        Va = scrB[:, 2560:5120].rearrange("p (k v) -> p k v", v=128)
        kR = [Res() for _ in range(5)]
        vR = [Res() for _ in range(5)]
        qslots = [(scrB[:, 5120 + i * 512:5632 + i * 512], Res()) for i in range(4)]
        A.pt = Rot([(scrB[:, 7168 + i * 512:7680 + i * 512], Res()) for i in range(4)])
        sqslots = [(scrB[:, 9216 + i * 512:9728 + i * 512], Res()) for i in range(2)]
        A.sq = Rot(sqslots)
        A.lsq = sqslots
        maskb = scrB[:, 10240:13312].rearrange("p (o q) -> p o q", o=6)
        maskR = Res()
        lane_setup(A, S)
        if S:
            stgK = scrF[:, 2048:2560].rearrange("p (t f) -> p t f", t=4)
            stgV = scrF[:, 2560:2816].rearrange("p (t f) -> p t f", t=4)
            stgM = scrF[:, 4096:4608]
            stgKR, stgVR, stgMR = Res(), Res(), Res()
            if isB:
                for o in range(6):
                    DMA("sp", stgM, c_maskb_d[o], "stgM", [], [stgMR])
                    CP(maskb[:, o, :], stgM, [stgMR], [maskR], eng="dve")
                P.fence(skip=skipw)
        else:
            A.ostg = Rot([(scrF[:, 2048 + i * 512:2560 + i * 512], Res()) for i in range(4)])
        defer, flush = make_defer(2)
        MEMSET(Va[:, :, 64:128], 1.0, [], [vR[0]])
        vonesR = vR[0]
        accrot = [0]
        vbank = [0]

        def attend(g, t, qpi):
            qp, qR = qslots[(t % 2) * 2 + qpi]
            if S:
                kts = [(kt * 128, kt, kR[0], vR[0], None) for kt in range(4)]
                if isB:
                    for o in range(-1, 5):
                        kt = 4 * t + o
                        if 0 <= kt < 16:
                            kts.append((512 + kt * 128, 4 + kt, kR[1 + kt // 4], vR[1 + kt // 4], o + 1))
                else:
                    kts += [(512 + kt * 128, 4 + kt, kR[1 + kt // 4], vR[1 + kt // 4], None) for kt in range(16)]
                segs = [(0, 512, kts)]
            else:
                segs = []
                for sq_ in range(2):
                    p0 = t * 512 + sq_ * 256
                    segs.append((sq_ * 256, 256, [(p0 + kt * 128, (p0 // 128) + kt, kR[1 + t], vR[1 + t], None) for kt in range(2)]))
            accs = []
            for hh in range(2):
                bi = 4 + accrot[0] % 4
                accrot[0] += 1
                accs.append((ps[bi], psR[bi]))
            for si, (qoff, N, ktiles) in enumerate(segs):
                nk = len(ktiles)
                for i, (kc0, vi, kr, vr, mo) in enumerate(ktiles):
                    sts = []
                    for hh in range(2):
                        pst, pstr = pw()
                        MM(pst[:, 0:N], kT2[64 * hh:64 * hh + 64, kc0:kc0 + 128], qp[64 * hh:64 * hh + 64, qoff:qoff + N], True, True, [kr, qR], [pstr])
                        sts.append((pst, pstr))
                    for hh in range(2):
                        pst, pstr = sts[hh]
                        acc, accR = accs[hh]
                        Pt, PtR = A.pt.next()
                        ACT(Pt[:, 0:N], pst[:, 0:N], AF.Exp, [pstr], [PtR], scale=0.125)
                        if mo is not None:
                            TT(Pt[:, 0:N], Pt[:, 0:N], maskb[:, mo, 0:N], ALU.mult, [PtR, maskR], [PtR])

                        def pv(acc=acc, accR=accR, vi=vi, vr=vr, Pt=Pt, PtR=PtR, i=i, nk=nk, N=N, qoff=qoff):
                            MM(acc[:, qoff:qoff + N], Va[:, vi, :], Pt[:, 0:N], i == 0, i == nk - 1, [vr, vonesR, PtR], [accR])
                        defer(pv)
            if True:
                for hh in range(2):
                    hq = g * 4 + qpi * 2 + hh
                    acc, accR = accs[hh]

                    def fin(acc=acc, accR=accR, hq=hq, N=512, t=t, qoff=0):
                        ta, tar = A.tmp.next()
                        if isB:
                            ACT(ta[64:128, 0:N], acc[64:128, 0:N], AF.Ln, [accR, constR], [tar], bias=esink[64:128, hq:hq + 1])
                        else:
                            ACT(ta[64:128, 0:N], acc[64:128, 0:N], AF.Ln, [accR], [tar])
                        ACT(ta[64:128, 0:N], ta[64:128, 0:N], AF.Exp, [tar], [tar], scale=-1.0)
                        pd = (hq % 2) * 64
                        TT(oT[pd:pd + 64, hq // 2, t * 512 + qoff:t * 512 + qoff + N], acc[0:64, 0:N], ta[64:128, 0:N], ALU.mult,
                           [accR, tar], [oR[hq // 2][t]])
                    defer(fin)

        for g in range(4):
            flush()
            s = wload([
                (lambda w: wview(w, 0, 256), wsrc(wqkv_d, g * 256, 256)),
                (lambda w: wview(w, 2048, 128)[:, :, 0:64], wsrc(wqkv_d, 1024 + g * 64, 64)),
                (lambda w: wview(w, 2048, 128)[:, :, 64:128], wsrc(wqkv_d, 1024 + g * 64, 64)),
                (lambda w: wview(w, 3072, 64), wsrc(wqkv_d, 1280 + g * 64, 64)),
            ])
            Wq = wview(Wt[s], 0, 256)
            Wk = wview(Wt[s], 2048, 128)
            Wv = wview(Wt[s], 3072, 64)
            if S:
                DMA("sp", stgK[:, :, 0:64], ck_d[:, g * 64:(g + 1) * 64].rearrange("(t p) f -> p t f", p=128), "stgK", [], [stgKR])
                DMA("sp", stgK[:, :, 64:128], ck_d[:, g * 64:(g + 1) * 64].rearrange("(t p) f -> p t f", p=128), "stgK2", [], [stgKR])
                pt, pr = ps[7], psR[7]
                for tb in range(4):
                    TR(pt[:, tb * 128:(tb + 1) * 128], stgK[:, tb, :], ident32[:, :], [stgKR, constR], [pr])
                CP(kT2[:, 0:512], pt[:, :], [pr], [kR[0]], eng="act")
                DMA("sp", stgV, cv_d[:, g * 64:(g + 1) * 64].rearrange("(t p) f -> p t f", p=128), "stgV", [], [stgVR])
                CP(Va[:, 0:4, 0:64], stgV, [stgVR, vonesR], [vR[0]], eng="dve")
            chains = []
            for t in range(NT):
                c0 = koff + t * 512
                aft = None
                if not S:
                    aft = (lambda kn, knR, bank, t=t, g=g: out_tokmajor(
                        A, kn, knR, 128, nk_d[t * 512:(t + 1) * 512, g * 64:(g + 1) * 64].rearrange("(tb p) f -> p tb f", p=128), ncol=64, bank=bank))
                chains.append(lambda lane, t=t, c0=c0, aft=aft, s=s, Wk=Wk: qk_chain(
                    A, lane, lambda c: Wk[:, c, :], [WR[s][1], WR[s][2], SLOT[s]], t, kg, S, kT2[:, c0:c0 + 512], kR[1 + t], after=aft))

            def qchain(t, qpi, s=s, Wq=Wq):
                qp, qR = qslots[(t % 2) * 2 + qpi]
                return lambda lane: qk_chain(A, lane, lambda c: Wq[:, c, qpi * 128:(qpi + 1) * 128], [WR[s][0], SLOT[s]], t, qg, S, qp, qR)
            for t in range(min(NT, 2)):
                for qpi in range(2):
                    chains.append(qchain(t, qpi))
            fillers = []
            for t in range(NT):
                vbank[0] += 1
                b = 4 + vbank[0] % 3
                pv_, pvr = ps[b], psR[b]
                for tb in range(4):
                    def vblk(t=t, tb=tb, pv_=pv_, pvr=pvr, s=s, Wv=Wv):
                        for c in range(8):
                            MM(pv_[:, tb * 64:(tb + 1) * 64], hT[:, c, t * 512 + tb * 128:t * 512 + (tb + 1) * 128], Wv[:, c, :],
                               c == 0, c == 7, [WR[s][3], SLOT[s], hR[c][t]], [pvr])
                    fillers.append(vblk)

                def vev(t=t, pv_=pv_, pvr=pvr, g=g):
                    CP(Va[:, voff + 4 * t:voff + 4 * t + 4, 0:64], pv_[:, 0:256].rearrange("p (k v) -> p k v", v=64), [pvr, vonesR], [vR[1 + t]], eng="act")
                    if not S:
                        stg, stgR = A.ostg.next()
                        CP(stg[:, 0:256], pv_[:, 0:256], [pvr], [stgR], eng="act")
                        DMA("pool", nv_d[t * 512:(t + 1) * 512, g * 64:(g + 1) * 64].rearrange("(tb p) f -> p tb f", p=128),
                            stg[:, 0:256].rearrange("p (tb f) -> p tb f", tb=4), ("okv", A.ostg.i % 4), [stgR], [], is_out=True)
                fillers.append(vev)
            run_lanes(chains, fillers)
            for t in range(min(NT, 2)):
                for qpi in range(2):
                    attend(g, t, qpi)
            if NT == 4:
                flush()
                chains = [qchain(t, qpi) for t in (2, 3) for qpi in range(2)]
                run_lanes(chains, [])
                for t in (2, 3):
                    for qpi in range(2):
                        attend(g, t, qpi)
        flush()

    def mixer_D(ph):
        S = ph == "S"
        T = 2048 if S else 1024
        NT = T // 512
        NCH = T // 64
        A = Lay()
        qTb = scrB[:, 0:2048]
        vtok = scrB[0:64, 2048:6144].rearrange("p (n v) -> p n v", v=128)
        qR_ = [Res() for _ in range(4)]
        vtR = [Res() for _ in range(4)]
        A.qd = Rot([(scrB[:, 6144 + i * 512:6656 + i * 512], Res()) for i in range(2)])
        A.ki = Rot([(scrB[:, 7168 + i * 512:7680 + i * 512], Res()) for i in range(2)])
        A.koT = Rot([(scrB[0:64, 8192 + i * 1024:9216 + i * 1024].rearrange("p (n k) -> p n k", k=128), Res()) for i in range(2)])
        A.att = Rot([(scrB[0:64, 10240 + i * 512:10752 + i * 512], Res()) for i in range(2)])
        A.sbf = Rot([(scrB[:, 11264 + i * 128:11392 + i * 128], Res()) for i in range(3)])
        A.sq = Rot([(scrB[:, 11776 + i * 512:12288 + i * 512], Res()) for i in range(2)])
        oacc = scrF[:, 0:2048]
        oaR = [Res() for _ in range(4)]
        A.tmp = Rot([(scrF[:, 2048 + i * 512:2560 + i * 512], Res()) for i in range(5)])
        A.s32 = Rot([(scrF[:, 4608 + i * 128:4736 + i * 128], Res()) for i in range(3)])
        decs = [(misc[:, 16:24], Res()), (misc[:, 32:40], Res())]
        tots = [(misc[:, 24:32], Res()), (misc[:, 40:48], Res())]
        ptri = [0]

        def ptr():
            i = ptri[0] % 2
            ptri[0] += 1
            return ps[i], psR[i]
        gnd = smallT[:, 12:13]
        lv = lbv[:, :].rearrange("p (k d c) -> p k d c", k=2, d=2)
        poi = [0]
        for h in range(8):
            s = wload([
                (lambda w: wview(w, 0, 128), wsrc(win_d_d, h * 128, 128)),
                (lambda w: wview(w, 1024, 128), wsrc(win_d_d, 1024 + h * 128, 128)),
                (lambda w: wview(w, 2048, 128), wsrc(win_d_d, 2048 + h * 128, 128)),
                (lambda w: wview(w, 3072, 128), wsrc(win_d_d, 3072 + h * 128, 128)),
                (lambda w: wview(w, 4096, 128), wsrc(win_d_d, 4096 + h * 128, 128)),
            ])
            Wq = wview(Wt[s], 0, 128)
            Wf = [wview(Wt[s], 1024, 128), wview(Wt[s], 2048, 128)]
            Wi = wview(Wt[s], 3072, 128)
            Wgt = wview(Wt[s], 4096, 128)
            for t in range(NT):
                MEMSET(oacc[:, t * 512:(t + 1) * 512], 0.0, [], [oaR[t]])
            for t in range(NT):
                sl = slice(t * 512, (t + 1) * 512)
                pt, pr = pw()
                for c in range(8):
                    MM(pt[:, :], Wq[:, c, :], hT[:, c, sl], c == 0, c == 7, [WR[s][0], SLOT[s], hR[c][t]], [pr])
                ACT(qTb[:, sl], pt[:, :], AF.Silu, [pr], [qR_[t]])
                for half in range(2):
                    pv, pvr = pw()
                    for n4 in range(4):
                        n = half * 4 + n4
                        tk = t * 512 + n * 64
                        for c in range(8):
                            MM(pv[0:64, n4 * 128:(n4 + 1) * 128], hT[:, c, tk:tk + 64], Wi[:, c, :], c == 0, c == 7,
                               [WR[s][3], SLOT[s], hR[c][t]], [pvr])
                    EVAC(vtok[:, t * 8 + half * 4:t * 8 + half * 4 + 4, :], pv[0:64, :].rearrange("p (n v) -> p n v", v=128), [pvr], [vtR[t]])
            st = {}

            def prep(d, t, ui, s=s, Wf=Wf, h=h):
                lb_ap = lv[:, 0, d, h:h + 1]
                omlb_ap = lv[:, 1, d, h:h + 1]
                dec_, decR_ = decs[ui % 2]
                tot_, totR_ = tots[ui % 2]
                sl = slice(t * 512, (t + 1) * 512)
                pz, pzr = ptr()
                for c in range(8):
                    MM(pz[:, :], Wf[d][:, c, :], hT[:, c, sl], c == 0, c == 7, [WR[s][1 + d], SLOT[s], hR[c][t]], [pzr])
                f_, fR = A.tmp.next()
                ACT(f_, pz[:, :], AF.Exp, [pzr], [fR], scale=-1.0)
                yield
                ACT(f_, f_, AF.Ln, [fR, constR], [fR], bias=one_ap)
                yield
                ACT(f_, f_, AF.Exp, [fR], [fR], scale=-1.0)
                yield
                TS(f_, f_, omlb_ap, lb_ap, ALU.mult, ALU.add, [fR, constR], [fR])
                yield
                lf, lfR = A.tmp.next()
                ACT(lf, f_, AF.Ln, [fR], [lfR])
                TS(f_, f_, -1.0, 1.0, ALU.mult, ALU.add, [fR], [fR])
                yield
                cum, cumR = A.tmp.next()
                P.add("dve", lambda e, cum=cum, lf=lf: e.tensor_tensor_scan(cum, resetm[:, :], lf, 0.0, op0=ALU.mult, op1=ALU.add),
                      [lfR, constR], [cumR])
                cum3 = cum.rearrange("p (n c) -> p n c", c=64)
                CP(tot_[:, 0:8], cum3[:, :, 63], [cumR], [totR_], eng="dve")
                yield
                if d == 1:
                    TT(cum, lf, cum, ALU.subtract, [lfR, cumR], [cumR])
                    TT(cum3, cum3, tot_[:, 0:8].unsqueeze(2).to_broadcast([128, 8, 64]), ALU.add, [cumR, totR_], [cumR])
                    yield
                ACT(dec_[:, 0:8], tot_[:, 0:8], AF.Exp, [totR_], [decR_])
                e1, e1R = A.tmp.next()
                ACT(e1, cum, AF.Exp, [cumR], [e1R])
                STT(lf.rearrange("p (n c) -> p n c", c=64), cum3, -1.0, tot_[:, 0:8].unsqueeze(2).to_broadcast([128, 8, 64]),
                    ALU.mult, ALU.add, [cumR, totR_, lfR], [lfR])
                yield
                ACT(lf, lf, AF.Exp, [lfR], [lfR])
                qd, qdR = A.qd.next()
                TT(qd, qTb[:, sl], e1, ALU.mult, [qR_[t], e1R], [qdR])
                yield
                e2, e2R = A.tmp.next()
                ACT(e2, cum, AF.Exp, [cumR], [e2R], scale=-1.0)
                TT(lf, lf, f_, ALU.mult, [lfR, fR], [lfR])
                yield
                ki, kiR = A.ki.next()
                TT(ki, f_, e2, ALU.mult, [fR, e2R], [kiR])
                koT, koTR = A.koT.next()
                for half in range(2):
                    pk, pkr = ptr()
                    for n4 in range(4):
                        n = half * 4 + n4
                        TR(pk[0:64, n4 * 128:(n4 + 1) * 128], lf[:, n * 64:(n + 1) * 64], ident32[:, :], [lfR, constR], [pkr])
                    EVAC(koT[:, half * 4:half * 4 + 4, :], pk[0:64, :].rearrange("p (n k) -> p n k", k=128), [pkr], [koTR])
                    yield
                pa_, par = ptr()
                for n in range(8):
                    cs = slice(n * 64, (n + 1) * 64)
                    MM(pa_[0:64, cs], ki[:, cs], qd[:, cs], True, True, [kiR, qdR], [par])
                attm, attmR = A.att.next()
                TT(attm.rearrange("p (n c) -> p n c", c=64), pa_[0:64, :].rearrange("p (n c) -> p n c", c=64),
                   maskhf[:, d * 64:(d + 1) * 64].unsqueeze(1).to_broadcast([64, 8, 64]), ALU.mult, [par, constR], [attmR])
                yield
                pb = (2, 3) if ui % 2 == 0 else (6, 7)
                pds = [(ps[pb[0]], psR[pb[0]]), (ps[pb[1]], psR[pb[1]])]
                for n in range(8):
                    ng = t * 8 + n
                    MM(pds[n // 4][0][:, (n % 4) * 128:(n % 4 + 1) * 128], koT[:, n, :], vtok[:, ng, :], True, True,
                       [koTR, vtR[t]], [pds[n // 4][1]])
                st[ui] = (qd, qdR, attm, attmR, pds, dec_, decR_)
                yield

            def chain(d, t, ui, h=h):
                qd, qdR, attm, attmR, pds, dec_, decR_ = st.pop(ui)
                sl = slice(t * 512, (t + 1) * 512)
                first = (t == 0) if d == 0 else (t == NT - 1)
                if first:
                    s32, s32R = A.s32.next()
                    sbf, sbfR = A.sbf.next()
                    if S:
                        DMA("sp", s32, sd_d[d, h], ("s32", A.s32.i % 3), [], [s32R])
                    else:
                        MEMSET(s32, 0.0, [], [s32R])
                    CP(sbf, s32, [s32R], [sbfR], eng="dve")
                else:
                    s32, s32R, sbf, sbfR = cur["s"]
                po, por = ps[4 + (ui % 2)], psR[4 + (ui % 2)]
                chunks = list(range(8)) if d == 0 else list(range(7, -1, -1))
                for n in chunks:
                    ng = t * 8 + n
                    cs = slice(n * 64, (n + 1) * 64)
                    if not S and ((d == 0 and ng % 4 == 0) or (d == 1 and ng % 4 == 3)) and not (ng == (0 if d == 0 else NCH - 1)):
                        s32, s32R = A.s32.next()
                        sbf, sbfR = A.sbf.next()
                        MEMSET(s32, 0.0, [], [s32R])
                        MEMSET(sbf, 0.0, [], [sbfR])
                    MM(po[:, cs], vtok[:, ng, :], attm[:, cs], True, False, [vtR[t], attmR], [por])
                    MM(po[:, cs], sbf, qd[:, cs], False, True, [sbfR, qdR], [por])
                    n32, n32R = A.s32.next()
                    pdn, pdnR = pds[n // 4]
                    STT(n32, s32, dec_[:, n:n + 1], pdn[:, (n % 4) * 128:(n % 4 + 1) * 128], ALU.mult, ALU.add, [s32R, decR_, pdnR], [n32R])
                    s32, s32R = n32, n32R
                    if not S and ((d == 0 and ng % 4 == 3) or (d == 1 and ng % 4 == 0)):
                        DMA("sp", nsd_d[ng // 4, d, h], s32, ("nsd", A.s32.i % 3), [s32R], [], is_out=True)
                    else:
                        sbf, sbfR = A.sbf.next()
                        CP(sbf, s32, [s32R], [sbfR], eng="dve")
                    yield
                TT(oacc[:, sl], oacc[:, sl], po[:, :], ALU.add, [oaR[t], por], [oaR[t]])
                cur["s"] = (s32, s32R, sbf, sbfR)
                yield

            cur = {}
            units = [(0, t) for t in range(NT)] + [(1, t) for t in range(NT - 1, -1, -1)]
            prev = None
            for ui, (d, t) in enumerate(units):
                pg = prep(d, t, ui)
                if prev is None:
                    for _ in pg:
                        pass
                else:
                    a_done = b_done = False
                    while not (a_done and b_done):
                        if not a_done:
                            a_done = next(pg, "END") == "END"
                        if not b_done:
                            b_done = next(prev, "END") == "END"
                prev = chain(d, t, ui)
            for _ in prev:
                pass
            for t in range(NT):
                sl = slice(t * 512, (t + 1) * 512)
                pg, pgr = pw()
                for c in range(8):
                    MM(pg[:, :], Wgt[:, c, :], hT[:, c, sl], c == 0, c == 7, [WR[s][4], SLOT[s], hR[c][t]], [pgr])
                sg, sgr = A.tmp.next()
                ACT(sg, pg[:, :], AF.Silu, [pgr], [sgr])
                sq, sqr = A.sq.next()
                ACT(sq, oacc[:, sl], AF.Square, [oaR[t]], [sqr])
                pm, pmr = pw()
                MM(pm[:, :], mean128[:, :], sq, True, True, [sqr, constR], [pmr])
                t1, t1r = A.tmp.next()
                ACT(t1, pm[:, :], AF.Ln, [pmr, constR], [t1r], bias=eps_ap)
                ACT(t1, t1, AF.Exp, [t1r], [t1r], scale=-0.5)
                STT(t1, oacc[:, sl], gnd, t1, ALU.mult, ALU.mult, [oaR[t], t1r, constR], [t1r])
                TT(oT[:, h, sl], t1, sg, ALU.mult, [t1r, sgr], [oR[h][t]])

    def skipw(key):
        return isinstance(key, tuple) and key[0] == "dma" and isinstance(key[1], tuple) and key[1][0] == "w"

    prologue()
    for ph in phases:
        S = ph == "S"
        T = 2048 if S else 1024
        NT = T // 512
        r = 1 if S else 0
        P.fence(skip=skipw)
        P.new_epoch()
        load_x(xs_d if S else xp_d, T)
        for li in range(nlayers):
            L = DL
            if li == 0:
                norm_mod(L, NT, li, 0, r)
            P.fence(skip=skipw)
            kind = li % 4
            if kind == 0:
                mixer_A(ph)
                wo_d = wo_a_d
            elif kind == 1:
                mixer_G(ph, "B")
                wo_d = wo_b_d
            elif kind == 2:
                mixer_G(ph, "C")
                wo_d = wo_c_d
            else:
                mixer_D(ph)
                wo_d = wo_d_d
            P.fence(skip=skipw)
            wo_proj(NT, li, r, wo_d, after=lambda t, li=li: norm_tile(L, t, li, 1, r))
            side = None
            if ph == phases[0] and li + 1 < nlayers:
                side = ada_steps(li + 1, [(scrF[:, 3072 + i * 1024:4096 + i * 1024].rearrange("p (c n) -> p c n", c=8), Res()) for i in range(2)],
                                 128, ps[7], psR[7])
            nxt = (lambda t, li=li: norm_tile(L, t, li + 1, 0, r)) if li + 1 < nlayers else None
            ffn(L, NT, li, r, side, after=nxt)
        store_y(ys_d if S else yp_d, T)

    semstack = ExitStack()
    with semstack:
        run = P.emit(lambda name: semstack.enter_context(nc.semaphore(name)))
        with nc.allow_non_contiguous_dma(reason="small strided loads"):
            with nc.Block() as block:
                @block.tensor
                def _(e):
                    run("pe", e)

                @block.scalar
                def _(e):
                    run("act", e)

                @block.vector
                def _(e):
                    run("dve", e)

                @block.gpsimd
                def _(e):
                    run("pool", e)

                @block.sync
                def _(e):
                    run("sp", e)
    es.close()
    return nc


def _consts():
    f = np.float32
    ident = np.eye(128, dtype=f)
    prot = np.zeros((128, 128), f)
    for p in range(128):
        if (p % 32) < 16:
            prot[p + 16, p] = -1.0
        else:
            prot[p - 16, p] = 1.0
    maskb = np.zeros((6, 128, 512), f)
    k = np.arange(128)[:, None]
    q = np.arange(512)[None, :]
    for o in range(-1, 5):
        maskb[o + 1] = (np.abs(q - 128 * o - k) <= 128).astype(f)
    s = np.arange(64)[:, None]
    c = np.arange(64)[None, :]
    maskh = np.stack([(c >= s).astype(f), (c <= s).astype(f)], 0)
    reset = np.ones((128, 512), f)
    reset[:, ::64] = 0.0
    tpos = np.arange(2048)
    row = (tpos // 64).astype(np.float64)
    col = (tpos % 64).astype(np.float64)
    inv_freq = 10000.0 ** (-np.arange(0, 32, 2, dtype=np.float64) / 32.0)
    ang = np.zeros((64, 2048), np.float64)
    for d in range(64):
        a = d // 32
        fi = d % 16
        ang[d] = (row if a == 0 else col) * inv_freq[fi]
    ang32 = np.zeros((64, 2048), np.float32)
    invf32 = (np.float32(10000.0) ** (-np.arange(0, 32, 2, dtype=np.float32) / np.float32(32.0))).astype(np.float32)
    for d in range(64):
        a = d // 32
        fi = d % 16
        ang32[d] = ((row if a == 0 else col).astype(np.float32) * invf32[fi]).astype(np.float32)
    cos = np.cos(ang32.astype(np.float64)).astype(f)
    sin = np.sin(ang32.astype(np.float64)).astype(f)
    cos = np.concatenate([cos, cos], 0)
    sin = np.concatenate([sin, sin], 0)
    return dict(c_ident=ident, c_prot=prot, c_maskb=maskb, c_maskh=maskh, c_reset=reset, c_cos=cos, c_sin=sin)


_NC_CACHE = {}


def _in_maps(inp, cores=range(8)):
    f = np.float32
    A = lambda x: np.ascontiguousarray(np.asarray(x, dtype=f))
    consts = _consts()
    small = np.zeros((16, 128), f)

    def dup(v):
        v = np.asarray(v, f).reshape(-1)
        return np.concatenate([v, v]) if v.size == 64 else v

    small[0] = dup(inp["qn_a"][0]); small[1] = dup(inp["kn_a"][0]); small[2] = np.asarray(inp["subln_a"][0], f)
    small[3, :64] = inp["lam_q1_a"][0]; small[4, :64] = inp["lam_k1_a"][0]
    small[5, :64] = inp["lam_q2_a"][0]; small[6, :64] = inp["lam_k2_a"][0]
    small[7] = dup(inp["qn_b"][0]); small[8] = dup(inp["kn_b"][0]); small[9, :16] = inp["sink_b"][0]
    small[10] = dup(inp["qn_c"][0]); small[11] = dup(inp["kn_c"][0]); small[12] = np.asarray(inp["gn_d"][0], f)
    shared = dict(
        w_ada=A(inp["w_ada"]), w_ffn_gate=A(inp["w_ffn_gate"]), w_ffn_up=A(inp["w_ffn_up"]), w_ffn_down=A(inp["w_ffn_down"]),
        w_qkv_a=A(inp["w_qkv_a"][0]), w_o_a=A(inp["w_o_a"][0]), w_qkv_b=A(inp["w_qkv_b"][0]), w_o_b=A(inp["w_o_b"][0]),
        w_qkv_c=A(inp["w_qkv_c"][0]), w_o_c=A(inp["w_o_c"][0]), w_in_d=A(inp["w_in_d"][0]), w_o_d=A(inp["w_o_d"][0]),
        small=small, **consts)
    maps = []
    for core in cores:
        b = core // 2
        vecs = np.zeros((384, 128), f)
        vecs[0:192] = np.asarray(inp["b_ada"], f).reshape(192, 128)
        vecs[192:224] = np.asarray(inp["norm_mix"], f).reshape(32, 128)
        vecs[224:256] = np.asarray(inp["norm_ffn"], f).reshape(32, 128)
        vecs[256:320] = np.asarray(inp["lb_logits_d"], f).reshape(64, 128)
        vecs[320:328] = np.asarray(inp["c_ctx"], f).reshape(8, 128)
        vecs[328:336] = np.asarray(inp["c"][b], f).reshape(8, 128)
        m = dict(shared)
        m.update(
            xs=A(inp["x_sample"][b]), xp=A(np.asarray(inp["x_prompt"][4 * core:4 * core + 4]).reshape(1024, 1024)), vecs=vecs,
            cak=A(np.asarray(inp["cache_a_k"][b, 0]).reshape(512, 1024)), cav=A(np.asarray(inp["cache_a_v"][b, 0]).reshape(512, 1024)),
            cbk=A(np.asarray(inp["cache_b_k"][b, 0]).reshape(512, 256)), cbv=A(np.asarray(inp["cache_b_v"][b, 0]).reshape(512, 256)),
            cck=A(np.asarray(inp["cache_c_k"][b, 0]).reshape(512, 256)), ccv=A(np.asarray(inp["cache_c_v"][b, 0]).reshape(512, 256)),
            sd=A(inp["state_d"][b, 0]))
        maps.append(m)
    return maps


def kernel(**inp):
    if "nc" not in _NC_CACHE:
        _NC_CACHE["nc"] = build()
    nc = _NC_CACHE["nc"]
    maps = _in_maps(inp)
    res = run_bass_kernel_spmd(nc, maps, core_ids=list(range(8)))
    R = res.results
    f = np.float32
    y_prompt = np.concatenate([R[c]["yp"].reshape(4, 256, 1024) for c in range(8)], 0).astype(f)
    y_sample = np.stack([R[2 * b]["ys"] for b in range(4)], 0).astype(f)
    cat = lambda name, shp: np.concatenate([R[c][name].reshape((4, 1, 256) + shp) for c in range(8)], 0).astype(f)
    new_a_k = cat("nak", (16, 64))
    new_a_v = cat("nav", (8, 128))
    new_b_k = cat("nbk", (4, 64))
    new_b_v = cat("nbv", (4, 64))
    new_c_k = cat("nck", (4, 64))
    new_c_v = cat("ncv", (4, 64))
    new_d = np.concatenate([R[c]["nsd"].reshape(4, 1, 2, 8, 128, 128) for c in range(8)], 0).astype(f)
    return (y_prompt, y_sample, new_a_k, new_a_v, new_b_k, new_b_v, new_c_k, new_c_v, new_d)
```

```python
import math, os
from contextlib import ExitStack
import numpy as np
import concourse.bass as bass
import concourse.mybir as mybir
from concourse.bass_utils import run_bass_kernel_spmd

F32 = mybir.dt.float32
BF16 = mybir.dt.bfloat16
AF = mybir.ActivationFunctionType
ALU = mybir.AluOpType
AX = mybir.AxisListType

ENGS = ("pe", "act", "dve", "pool", "sp")
EPS = 1e-6


class Res:
    __slots__ = ("name", "w", "r", "excl")

    def __init__(self, name="", excl=False):
        self.name = name
        self.w = None
        self.r = {}
        self.excl = excl


class Op:
    __slots__ = ("eng", "key", "idx", "fn", "waits", "needs_inc", "inc_val", "is_dma", "epoch")

    def __init__(self, eng, key, idx, fn, is_dma=False):
        self.eng = eng
        self.key = key
        self.idx = idx
        self.fn = fn
        self.waits = []
        self.needs_inc = False
        self.inc_val = None
        self.is_dma = is_dma
        self.epoch = 0


class Prog:
    def __init__(self, nc):
        self.nc = nc
        self.ops = {e: [] for e in ENGS}
        self.streams = {e: [] for e in ENGS}
        self.seen = {e: {} for e in ENGS}
        self.epoch = 0
        self.out_dma_keys = set()
        self.pending_fence = {e: [] for e in ENGS}

    def new_epoch(self):
        self.epoch += 1

    def fence(self, engines=("pe", "act", "dve", "sp"), skip=lambda key: False):
        lasts = [lst[-1] for key, lst in self.streams.items() if lst and not skip(key)]
        for e in engines:
            self.pending_fence[e] = list(lasts)

    def add(self, eng, fn, reads=(), writes=(), dma_key=None, is_out=False):
        is_dma = dma_key is not None
        key = ("dma", dma_key) if is_dma else eng
        if key not in self.streams:
            self.streams[key] = []
        op = Op(eng, key, len(self.streams[key]), fn, is_dma)
        op.epoch = 0 if is_dma else self.epoch
        deps = {}

        def put(d):
            if d is None:
                return
            cur = deps.get(d.key)
            if cur is None or cur.idx < d.idx:
                deps[d.key] = d

        for r in reads:
            put(r.w)
            if r.excl:
                for d in r.r.values():
                    if d.key != key:
                        put(d)
        for w in writes:
            put(w.w)
            for d in w.r.values():
                put(d)
        fence_keys = set()
        if self.pending_fence[eng]:
            for d in self.pending_fence[eng]:
                put(d)
                fence_keys.add(d.key)
            self.pending_fence[eng] = []
        seen = self.seen[eng]
        for k, d in deps.items():
            if k == "pe" and eng == "pe" and not is_dma and k not in fence_keys:
                continue
            if k == eng and not is_dma and d is op:
                continue
            if seen.get(k, -1) >= d.idx:
                continue
            seen[k] = d.idx
            d.needs_inc = True
            op.waits.append(d)
        self.streams[key].append(op)
        self.ops[eng].append(op)
        for r in reads:
            r.r[key] = op
        for w in writes:
            w.w = op
            w.r = {}
        if is_out:
            self.out_dma_keys.add(key)
        return op

    def emit(self, sem_alloc):
        sems = {}
        for key, lst in self.streams.items():
            cnt = {}
            for op in lst:
                if op.is_dma:
                    op.needs_inc = True
                if op.needs_inc:
                    sk = (key, op.epoch)
                    cnt[sk] = cnt.get(sk, 0) + (16 if op.is_dma else 1)
                    op.inc_val = cnt[sk]
                    if sk not in sems:
                        sems[sk] = sem_alloc("s%d" % len(sems))
        self.sems = sems
        final_out = []
        for key in self.out_dma_keys:
            last = self.streams[key][-1]
            final_out.append((sems[(key, last.epoch)], last.inc_val))

        def run(engname, eng):
            for op in self.ops[engname]:
                for d in op.waits:
                    eng.wait_ge(sems[(d.key, d.epoch)], d.inc_val)
                ins = op.fn(eng)
                if op.needs_inc:
                    ins.then_inc(sems[(op.key, op.epoch)], 16 if op.is_dma else 1)
            if engname == "sp":
                for s, v in final_out:
                    eng.wait_ge(s, v)

        return run


class Rot:
    def __init__(self, items):
        self.items = items
        self.i = 0

    def next(self):
        it = self.items[self.i % len(self.items)]
        self.i += 1
        return it


def build(nlayers=4, phases="SP"):
    nc = bass.Bass("TRN2", target_bir_lowering=False)
    P = Prog(nc)
    es = ExitStack()

    def din(name, shape):
        return nc.dram_tensor(name, list(shape), F32, kind="ExternalInput").ap()

    def dout(name, shape):
        return nc.dram_tensor(name, list(shape), F32, kind="ExternalOutput").ap()

    xs_d = din("xs", [2048, 1024])
    xp_d = din("xp", [1024, 1024])
    vecs_d = din("vecs", [384, 128])
    cak_d = din("cak", [512, 1024])
    cav_d = din("cav", [512, 1024])
    cbk_d = din("cbk", [512, 256])
    cbv_d = din("cbv", [512, 256])
    cck_d = din("cck", [512, 256])
    ccv_d = din("ccv", [512, 256])
    sd_d = din("sd", [2, 8, 128, 128])
    w_ada_d = din("w_ada", [4, 1024, 6144])
    wg_d = din("w_ffn_gate", [4, 1024, 2816])
    wu_d = din("w_ffn_up", [4, 1024, 2816])
    wd_d = din("w_ffn_down", [4, 2816, 1024])
    wqkv_a_d = din("w_qkv_a", [1024, 3072])
    wo_a_d = din("w_o_a", [1024, 1024])
    wqkv_b_d = din("w_qkv_b", [1024, 1536])
    wo_b_d = din("w_o_b", [1024, 1024])
    wqkv_c_d = din("w_qkv_c", [1024, 1536])
    wo_c_d = din("w_o_c", [1024, 1024])
    win_d_d = din("w_in_d", [1024, 5120])
    wo_d_d = din("w_o_d", [1024, 1024])
    small_d = din("small", [16, 128])
    c_ident_d = din("c_ident", [128, 128])
    c_prot_d = din("c_prot", [128, 128])
    c_maskb_d = din("c_maskb", [6, 128, 512])
    c_maskh_d = din("c_maskh", [2, 64, 64])
    c_reset_d = din("c_reset", [128, 512])
    c_cos_d = din("c_cos", [128, 2048])
    c_sin_d = din("c_sin", [128, 2048])

    ys_d = dout("ys", [2048, 1024])
    yp_d = dout("yp", [1024, 1024])
    nak_d = dout("nak", [1024, 1024])
    nav_d = dout("nav", [1024, 1024])
    nbk_d = dout("nbk", [1024, 256])
    nbv_d = dout("nbv", [1024, 256])
    nck_d = dout("nck", [1024, 256])
    ncv_d = dout("ncv", [1024, 256])
    nsd_d = dout("nsd", [4, 2, 8, 128, 128])

    def sb(name, shape, dt):
        return es.enter_context(nc.sbuf_tensor(name, shape, dt))

    xT_t = sb("xT", [128, 8 * 2048], F32)
    hT_t = sb("hT", [128, 8 * 2048], BF16)
    oT_t = sb("oT", [128, 8 * 2048], BF16)
    xT = xT_t[:, :].rearrange("p (c t) -> p c t", c=8)
    hT = hT_t[:, :].rearrange("p (c t) -> p c t", c=8)
    oT = oT_t[:, :].rearrange("p (c t) -> p c t", c=8)
    Wt = [sb("W%d" % i, [128, 6144], BF16) for i in range(2)]
    scrB = sb("scrB", [128, 13312], BF16)
    scrF = sb("scrF", [128, 5120], F32)
    ident32 = sb("ident32", [128, 128], F32)
    prot32 = sb("prot32", [128, 128], F32)
    onesb = sb("onesb", [128, 128], BF16)
    mean1024 = sb("mean1024", [128, 128], BF16)
    blk64 = sb("blk64", [128, 128], BF16)
    mean128 = sb("mean128", [128, 128], BF16)
    vT = sb("vT", [128, 384], F32)
    modt = sb("modt", [128, 4 * 48 * 2], F32)
    mod = modt[:, :].rearrange("p (l j r) -> p l j r", l=4, j=48)
    dert = sb("dert", [128, 4 * 2 * 8 * 2], F32)
    der = dert[:, :].rearrange("p (l k c r) -> p l k c r", l=4, k=2, c=8)
    smallT = sb("smallT", [128, 16], F32)
    misc = sb("misc", [128, 64], F32)
    esink = sb("esink", [128, 16], F32)
    lbt = sb("lbt", [128, 64], F32)
    lbv = sb("lbv", [128, 32], F32)
    condS = sb("condS", [128, 16], F32)
    maskhf = sb("maskhf", [64, 128], F32)
    resetm = sb("resetm", [128, 512], F32)
    ps = [es.enter_context(nc.psum_tensor("ps%d" % i, [128, 512], F32)) for i in range(8)]
    psR = [Res("ps%d" % i, excl=True) for i in range(8)]

    constR = Res("const")
    modR = [Res("mod%d" % i) for i in range(4)]
    xR = [[Res() for t in range(4)] for c in range(8)]
    hR = [[Res() for t in range(4)] for c in range(8)]
    oR = [[Res() for t in range(4)] for c in range(8)]
    WR = [[Res() for i in range(6)] for s in range(2)]
    SLOT = [Res(), Res()]

    eps_ap = misc[:, 0:1]
    neglam_ap = misc[:, 1:2]
    subg_ap = misc[:, 2:3]
    one_ap = misc[:, 3:4]

    def MM(out, lhsT, rhs, start, stop, R, Wr):
        P.add("pe", lambda e: e.matmul(out, lhsT, rhs, start=start, stop=stop), R, Wr)

    def TR(out, in_, ident, R, Wr):
        P.add("pe", lambda e: e.transpose(out, in_, ident), R, Wr)

    def ACT(out, in_, func, R, Wr, bias=None, scale=None):
        kw = {}
        if bias is not None:
            kw["bias"] = bias
        if scale is not None:
            kw["scale"] = scale
        P.add("act", lambda e: e.activation(out, in_, func, **kw), R, Wr)

    def TT(out, in0, in1, op, R, Wr, eng="dve"):
        P.add(eng, lambda e: e.tensor_tensor(out, in0, in1, op=op), R, Wr)

    def STT(out, in0, scalar, in1, op0, op1, R, Wr, eng="dve"):
        P.add(eng, lambda e: e.scalar_tensor_tensor(out, in0, scalar, in1, op0=op0, op1=op1), R, Wr)

    def TS(out, in0, s1, s2, op0, op1, R, Wr, eng="dve"):
        if s2 is None:
            P.add(eng, lambda e: e.tensor_scalar(out, in0, s1, None, op0=op0), R, Wr)
        else:
            P.add(eng, lambda e: e.tensor_scalar(out, in0, s1, s2, op0=op0, op1=op1), R, Wr)

    def CP(out, in_, R, Wr, eng="dve"):
        if eng == "act":
            P.add("act", lambda e: e.activation(out, in_, AF.Copy), R, Wr)
        else:
            P.add(eng, lambda e: e.tensor_copy(out, in_), R, Wr)

    def RECIP(out, in_, R, Wr):
        P.add("dve", lambda e: e.reciprocal(out, in_), R, Wr)

    def MEMSET(out, val, R, Wr, eng="dve"):
        P.add(eng, lambda e: e.memset(out, val), R, Wr)

    def DMA(q, out, in_, key, R, Wr, is_out=False):
        P.add(q, lambda e: e.dma_start(out=out, in_=in_), R, Wr, dma_key=key, is_out=is_out)

    pwi = [0]

    def pw():
        i = pwi[0] % 4
        pwi[0] += 1
        return ps[i], psR[i]

    evi = [0]

    def EVAC(out, in_, R, Wr):
        evi[0] += 1
        CP(out, in_, R, Wr, eng="act" if evi[0] % 2 else "dve")

    wjob = [0]

    def wload(parts):
        s = wjob[0] % 2
        wjob[0] += 1
        for i, (dstf, src) in enumerate(parts):
            wr = [WR[s][i]] + ([SLOT[s]] if i == 0 else [])
            DMA("pool", dstf(Wt[s]), src, ("w", s, i), [], wr)
        return s

    def wview(s_t, off, ncol):
        return s_t[:, off:off + 8 * ncol].rearrange("p (c n) -> p c n", c=8)

    def wsrc(w2d, col0, ncol):
        return w2d.rearrange("(c p) n -> p c n", p=128)[:, :, col0:col0 + ncol]

    def prologue():
        lamt = scrF[:, 384:640]
        DMA("sp", ident32[:, :], c_ident_d, "c0", [], [constR])
        DMA("sp", prot32[:, :], c_prot_d, "c1", [], [constR])
        DMA("sp", resetm[:, :], c_reset_d, "c2", [], [constR])
        DMA("sp", maskhf[:, :].rearrange("p (d c) -> p d c", d=2), c_maskh_d.rearrange("d s c -> s d c"), "c3", [], [constR])
        DMA("sp", smallT[:, :], small_d.rearrange("r p -> p r"), "c4", [], [constR])
        DMA("sp", esink[:, :], small_d[9, 0:16].partition_broadcast(128), "c5", [], [constR])
        for i in range(4):
            DMA("sp", lamt[:, i * 64:(i + 1) * 64], small_d[3 + i, 0:64].partition_broadcast(128), "c6", [], [constR])
        MEMSET(onesb[:, :], 1.0, [], [constR])
        MEMSET(mean1024[:, :], 1.0 / 1024, [], [constR])
        MEMSET(mean128[:, :], 1.0 / 128, [], [constR])
        MEMSET(blk64[:, :], 0.0, [], [constR])
        MEMSET(blk64[0:64, 0:64], 1.0 / 64, [], [constR])
        MEMSET(blk64[64:128, 64:128], 1.0 / 64, [], [constR])
        MEMSET(misc[:, :], 0.0, [], [constR])
        MEMSET(misc[:, 0:1], EPS, [], [constR])
        MEMSET(misc[:, 3:4], 1.0, [], [constR])
        stg = scrF[:, 0:384].rearrange("p (k f) -> p k f", k=3)
        stgR = Res()
        DMA("sp", stg, vecs_d.rearrange("(k p) f -> p k f", p=128), "c7", [], [stgR])
        for k in range(3):
            TR(ps[0][:, k * 128:(k + 1) * 128], stg[:, k, :], ident32[:, :], [stgR, constR], [psR[0]])
        CP(vT[:, :], ps[0][:, 0:384], [psR[0]], [constR])
        cS = condS[:, :].rearrange("p (c r) -> p c r", r=2)
        for r in range(2):
            ACT(cS[:, :, r], vT[:, 320 + 8 * r:328 + 8 * r], AF.Silu, [constR], [constR])
        ACT(esink[:, :], esink[:, :], AF.Exp, [constR], [constR])
        lam_init = 0.8 - 0.6 * math.exp(-0.3 * 0)
        TT(lamt[:, 0:64], lamt[:, 0:64], lamt[:, 64:128], ALU.mult, [constR], [constR])
        TT(lamt[:, 128:192], lamt[:, 128:192], lamt[:, 192:256], ALU.mult, [constR], [constR])
        P.add("dve", lambda e: e.reduce_sum(misc[:, 8:9], lamt[:, 0:64], axis=AX.X), [constR], [constR])
        P.add("dve", lambda e: e.reduce_sum(misc[:, 9:10], lamt[:, 128:192], axis=AX.X), [constR], [constR])
        ACT(misc[:, 8:10], misc[:, 8:10], AF.Exp, [constR], [constR])
        TT(misc[:, 10:11], misc[:, 9:10], misc[:, 8:9], ALU.subtract, [constR], [constR])
        TS(misc[:, 1:2], misc[:, 10:11], -lam_init, None, ALU.add, None, [constR], [constR])
        TS(misc[:, 2:3], smallT[:, 2:3], 1.0 - lam_init, None, ALU.mult, None, [constR], [constR])
        ACT(lbt[:, :], vT[:, 256:320], AF.Exp, [constR], [constR])
        lb4 = lbt[:, :].rearrange("p (d l c) -> p d l c", d=2, l=4)
        lv = lbv[:, :].rearrange("p (k d c) -> p k d c", k=2, d=2)
        li_d = 3
        for d in range(2):
            TT(lv[:, 0, d, :], lb4[:, d, 1, :], lb4[:, d, 2, :], ALU.add, [constR], [constR])
            TT(lv[:, 0, d, :], lv[:, 0, d, :], lb4[:, d, 3, :], ALU.add, [constR], [constR])
            TT(lv[:, 1, d, :], lv[:, 0, d, :], lb4[:, d, 0, :], ALU.add, [constR], [constR])
            RECIP(lv[:, 1, d, :], lv[:, 1, d, :], [constR], [constR])
            TT(lv[:, 0, d, :], lv[:, 0, d, :], lv[:, 1, d, :], ALU.mult, [constR], [constR])
            TS(lv[:, 1, d, :], lv[:, 0, d, :], -1.0, 1.0, ALU.mult, ALU.add, [constR], [constR])
        for step in ada_steps(0, [(scrF[:, 1024 + i * 2048:1024 + (i + 1) * 2048].rearrange("p (c n) -> p c n", c=8), Res()) for i in range(2)], 256, ps[4], psR[4]):
            pass

    def ada_steps(li, slots, ncol, acc, accR):
        cS = condS[:, :].rearrange("p (c r) -> p c r", r=2)
        rot = Rot(slots)
        ngrp = 6144 // ncol
        for jg in range(ngrp):
            wsl, wslR = rot.next()
            DMA("sp", wsl, wsrc(w_ada_d[li], jg * ncol, ncol), ("ada", rot.i % len(slots)), [], [wslR])
            for jj in range(ncol // 128):
                j = jg * (ncol // 128) + jj
                for kc in range(8):
                    MM(acc[:, 2 * j:2 * j + 2], wsl[:, kc, jj * 128:(jj + 1) * 128], cS[:, kc, :], kc == 0, kc == 7,
                       [wslR, constR], [accR])
            yield
        TT(mod[:, li, :, :], acc[:, 0:96].rearrange("p (j r) -> p j r", r=2),
           vT[:, li * 48:(li + 1) * 48].unsqueeze(2).to_broadcast([128, 48, 2]), ALU.add, [accR, constR], [modR[li]])
        STT(der[:, li, 0, :, :], mod[:, li, 8:16, :], 1.0, vT[:, 192 + li * 8:200 + li * 8].unsqueeze(2).to_broadcast([128, 8, 2]),
            ALU.add, ALU.mult, [constR, modR[li]], [modR[li]])
        STT(der[:, li, 1, :, :], mod[:, li, 32:40, :], 1.0, vT[:, 224 + li * 8:232 + li * 8].unsqueeze(2).to_broadcast([128, 8, 2]),
            ALU.add, ALU.mult, [constR, modR[li]], [modR[li]])
        yield

    def modv(li, m, c, r):
        return mod[:, li, m * 8 + c, r:r + 1]

    class Dense:
        pass

    def dense_layout():
        L = Dense()
        L.sq = Rot([(scrB[:, i * 512:(i + 1) * 512], Res()) for i in range(2)])
        L.a = Rot([(scrB[:, 1024 + i * 1024:1024 + (i + 1) * 1024].rearrange("p (j n) -> p j n", j=2), Res()) for i in range(2)])
        L.sd = Rot([(scrF[:, i * 512:(i + 1) * 512], Res()) for i in range(2)])
        L.tmp = Rot([(scrF[:, 1024 + i * 512:1024 + (i + 1) * 512], Res()) for i in range(2)])
        L.s = Rot([(scrF[:, 2048 + i * 512:2048 + (i + 1) * 512], Res()) for i in range(2)])
        L.stg = Rot([(scrF[:, 3072 + i * 1024:3072 + (i + 1) * 1024], Res()) for i in range(2)])
        return L

    DL = dense_layout()

    def load_x(x_d, T):
        L = DL
        for t in range(T // 512):
            for tb in range(4):
                stg, stgR = L.stg.next()
                r0 = t * 512 + tb * 128
                DMA("sp", stg, x_d[r0:r0 + 128, :], ("xin", L.stg.i % 2), [], [stgR])
                for c in range(8):
                    TR(ps[c][:, tb * 128:(tb + 1) * 128], stg[:, c * 128:(c + 1) * 128], ident32[:, :], [stgR, constR], [psR[c]])
            for c in range(8):
                EVAC(xT[:, c, t * 512:(t + 1) * 512], ps[c][:, :], [psR[c]], [xR[c][t]])

    def store_y(y_d, T):
        L = DL
        for tb in range(T // 128):
            t = tb // 4
            stg, stgR = L.stg.next()
            for c in range(8):
                b = 4 + (tb % 2) * 2 + c // 4
                TR(ps[b][:, (c % 4) * 128:(c % 4 + 1) * 128], xT[:, c, tb * 128:(tb + 1) * 128], ident32[:, :],
                   [xR[c][t], constR], [psR[b]])
            for hf in range(2):
                b = 4 + (tb % 2) * 2 + hf
                EVAC(stg[:, hf * 512:(hf + 1) * 512], ps[b][:, :], [psR[b]], [stgR])
            DMA("sp", y_d[tb * 128:(tb + 1) * 128, :], stg, ("yout", L.stg.i % 2), [stgR], [], is_out=True)

    def norm_mod(L, NT, li, which, r):
        for t in range(NT):
            norm_tile(L, t, li, which, r)

    def norm_tile(L, t, li, which, r):
        if True:
            sl = slice(t * 512, (t + 1) * 512)
            pt, pr = pw()
            for c in range(8):
                sq, sqr = L.sq.next()
                ACT(sq, xT[:, c, sl], AF.Square, [xR[c][t]], [sqr])
                MM(pt[:, :], mean1024[:, :], sq, c == 0, c == 7, [sqr, constR], [pr])
            sd, sdr = L.sd.next()
            ACT(sd, pt[:, :], AF.Ln, [pr, constR], [sdr], bias=eps_ap)
            ACT(sd, sd, AF.Exp, [sdr], [sdr], scale=-0.5)
            for c in range(8):
                tmp, tr = L.tmp.next()
                TT(tmp, xT[:, c, sl], sd, ALU.mult, [xR[c][t], sdr], [tr])
                ACT(hT[:, c, sl], tmp, AF.Identity, [tr, modR[li]], [hR[c][t]],
                    bias=modv(li, which * 3 + 0, c, r), scale=der[:, li, which, c, r:r + 1])

    def ffn(L, NT, li, r, side=None, after=None):
        fdefer, fflush = make_defer(1)
        fin_after = []
        nb = 3 if side is not None else 4
        for g in range(11):
            s = wload([
                (lambda w: wview(w, 0, 256), wsrc(wg_d[li], g * 256, 256)),
                (lambda w: wview(w, 2048, 256), wsrc(wu_d[li], g * 256, 256)),
                (lambda w: w[:, 4096:6144].rearrange("p (j n) -> p j n", j=2),
                 wd_d[li][g * 256:(g + 1) * 256, :].rearrange("(j p) n -> p j n", p=128)),
            ])
            Wg = wview(Wt[s], 0, 256)
            Wu = wview(Wt[s], 2048, 256)
            Wd = Wt[s][:, 4096:6144].rearrange("p (j n) -> p j n", j=2)
            for t in range(NT):
                sl = slice(t * 512, (t + 1) * 512)
                a, aR = L.a.next()
                for j in range(2):
                    pg, pgr = pw()
                    for c in range(8):
                        MM(pg[:, :], Wg[:, c, j * 128:(j + 1) * 128], hT[:, c, sl], c == 0, c == 7, [WR[s][0], SLOT[s], hR[c][t]], [pgr])
                    pu, pur = pw()
                    for c in range(8):
                        MM(pu[:, :], Wu[:, c, j * 128:(j + 1) * 128], hT[:, c, sl], c == 0, c == 7, [WR[s][1], SLOT[s], hR[c][t]], [pur])
                    sg, sgr = L.s.next()
                    ACT(sg, pg[:, :], AF.Silu, [pgr], [sgr])
                    TT(a[:, j, :], sg, pu[:, :], ALU.mult, [sgr, pur], [aR])

                if side is not None:
                    for _ in range(2):
                        next(side, None)

                def down(s=s, Wd=Wd, a=a, aR=aR, t=t, sl=sl, g=g):
                    if g == 10 and after is not None:
                        fin_after.append(t)
                    for cp in range(8):
                        b = 4 + cp % nb
                        for j in range(2):
                            MM(ps[b][:, :], Wd[:, j, cp * 128:(cp + 1) * 128], a[:, j, :], j == 0, j == 1, [WR[s][2], SLOT[s], aR], [psR[b]])
                        STT(xT[:, cp, sl], ps[b][:, :], modv(li, 5, cp, r), xT[:, cp, sl], ALU.mult, ALU.add,
                            [psR[b], xR[cp][t], modR[li]], [xR[cp][t]])
                    while fin_after:
                        after(fin_after.pop(0))
                fdefer(down)
        fflush()
        if side is not None:
            for _ in side:
                pass

    def wo_proj(NT, li, r, wo_d, after=None):
        for hf in range(2):
            s = wload([(lambda w: wview(w, 0, 512), wsrc(wo_d, hf * 512, 512))])
            Wo = wview(Wt[s], 0, 512)
            for t in range(NT):
                sl = slice(t * 512, (t + 1) * 512)
                for cl in range(4):
                    cp = hf * 4 + cl
                    b = 4 + cl
                    for c in range(8):
                        MM(ps[b][:, :], Wo[:, c, cl * 128:(cl + 1) * 128], oT[:, c, sl], c == 0, c == 7, [WR[s][0], SLOT[s], oR[c][t]], [psR[b]])
                    STT(xT[:, cp, sl], ps[b][:, :], modv(li, 2, cp, r), xT[:, cp, sl], ALU.mult, ALU.add,
                        [psR[b], xR[cp][t], modR[li]], [xR[cp][t]])
                if hf == 1 and after is not None:
                    after(t)

    def qk_proj(A, lhs_fn, lhsR, M, t, gain_ap, rope, dst, dstR):
        sl = slice(t * 512, (t + 1) * 512)
        pt, pr = pw()
        for c in range(8):
            MM(pt[0:M, :], lhs_fn(c), hT[:, c, sl], c == 0, c == 7, lhsR + [hR[c][t]], [pr])
        sq, sqr = A.sq.next()
        ACT(sq[0:M, :], pt[0:M, :], AF.Square, [pr], [sqr])
        pm, pmr = pw()
        MM(pm[0:M, :], blk64[0:M, 0:M], sq[0:M, :], True, True, [sqr, constR], [pmr])
        t1, t1r = A.tmp.next()
        ACT(t1[0:M, :], pm[0:M, :], AF.Ln, [pmr, constR], [t1r], bias=eps_ap[0:M, :])
        ACT(t1[0:M, :], t1[0:M, :], AF.Exp, [t1r], [t1r], scale=-0.5)
        t2, t2r = A.tmp.next()
        STT(t2[0:M, :], pt[0:M, :], gain_ap, t1[0:M, :], ALU.mult, ALU.mult, [pr, t1r, constR], [t2r])
        if rope is not None:
            cos_ap, sin_ap, ropeR = rope
            pq, pqr = pw()
            MM(pq[0:M, :], prot32[0:M, 0:M], t2[0:M, :], True, True, [t2r, constR], [pqr])
            t3, t3r = A.tmp.next()
            TT(t3[0:M, :], t2[0:M, :], cos_ap[0:M, :], ALU.mult, [t2r, ropeR], [t3r])
            TT(t1[0:M, :], pq[0:M, :], sin_ap[0:M, :], ALU.mult, [pqr, ropeR, t1r], [t1r])
            TT(dst, t3[0:M, :], t1[0:M, :], ALU.add, [t3r, t1r], [dstR])
        else:
            CP(dst, t2[0:M, :], [t2r], [dstR], eng="act")
        return t2, t2r

    def qk_chain(A, lane, lhs_fn, lhsR, t, gain_ap, use_rope, dst, dstR, after=None):
        bX, bY = 2 * lane, 2 * lane + 1
        pX, pXr, pY, pYr = ps[bX], psR[bX], ps[bY], psR[bY]
        (t1, t1r), (t2, t2r) = A.ltmp[lane]
        sq, sqr = A.lsq[lane]
        sl = slice(t * 512, (t + 1) * 512)
        if use_rope:
            slot, ropeR = A.lrope[lane]
            cos_ap, sin_ap = slot[:, 0:512], slot[:, 512:1024]
            DMA("sp", cos_ap, c_cos_d[:, t * 512:(t + 1) * 512], ("rope", lane, 0), [], [ropeR])
            DMA("sp", sin_ap, c_sin_d[:, t * 512:(t + 1) * 512], ("rope", lane, 1), [], [ropeR])
        for c in range(8):
            MM(pX[:, :], lhs_fn(c), hT[:, c, sl], c == 0, c == 7, lhsR + [hR[c][t]], [pXr])
        ACT(sq, pX[:, :], AF.Square, [pXr], [sqr])
        yield
        MM(pY[:, :], blk64[:, :], sq, True, True, [sqr, constR], [pYr])
        ACT(t1, pY[:, :], AF.Ln, [pYr, constR], [t1r], bias=eps_ap)
        yield
        ACT(t1, t1, AF.Exp, [t1r], [t1r], scale=-0.5)
        yield
        STT(t2, pX[:, :], gain_ap, t1, ALU.mult, ALU.mult, [pXr, t1r, constR], [t2r])
        yield
        if use_rope:
            MM(pX[:, :], prot32[:, :], t2, True, True, [t2r, constR], [pXr])
            yield
            TT(t1, pX[:, :], sin_ap, ALU.mult, [pXr, ropeR, t1r], [t1r])
            yield
            TT(t2, t2, cos_ap, ALU.mult, [t2r, ropeR], [t2r])
            yield
            TT(dst, t2, t1, ALU.add, [t2r, t1r], [dstR])
        else:
            CP(dst, t2, [t2r], [dstR], eng="act")
            if after is not None:
                after(t2, t2r, bY)
        yield

    def run_lanes(chains, fillers):
        active = [None, None]
        it = iter(chains)
        while True:
            progressed = False
            for lane in range(2):
                if active[lane] is None:
                    nxt = next(it, None)
                    if nxt is not None:
                        active[lane] = nxt(lane)
                if active[lane] is not None:
                    if next(active[lane], "END") == "END":
                        active[lane] = None
                    progressed = True
            if fillers:
                fillers.pop(0)()
                progressed = True
            if not progressed:
                break

    def lane_setup(A, S):
        tmpslots = [(scrF[:, i * 512:(i + 1) * 512], Res()) for i in range(4)]
        A.tmp = Rot(tmpslots)
        A.ltmp = [[tmpslots[0], tmpslots[1]], [tmpslots[2], tmpslots[3]]]
        if S:
            A.lrope = [(scrF[:, 3072 + i * 1024:4096 + i * 1024], Res()) for i in range(2)]

    def load_rope(A, t):
        slot, slotR = A.rope.next()
        cos_ap = slot[:, 0:512]
        sin_ap = slot[:, 512:1024]
        DMA("sp", cos_ap, c_cos_d[:, t * 512:(t + 1) * 512], ("rope", A.rope.i % 2, 0), [], [slotR])
        DMA("sp", sin_ap, c_sin_d[:, t * 512:(t + 1) * 512], ("rope", A.rope.i % 2, 1), [], [slotR])
        return cos_ap, sin_ap, slotR

    def out_tokmajor(A, src, srcR, M, out_view, ncol=None, bank=None):
        pt, pr = pw() if bank is None else (ps[bank], psR[bank])
        for tb in range(4):
            TR(pt[:, tb * M:(tb + 1) * M], src[0:M, tb * 128:(tb + 1) * 128], ident32[0:M, 0:M], [srcR, constR], [pr])
        stg, stgR = A.ostg.next()
        EVAC(stg[:, 0:4 * M], pt[:, 0:4 * M], [pr], [stgR])
        sv = stg[:, 0:4 * M].rearrange("p (tb f) -> p tb f", tb=4)
        if ncol is not None:
            sv = sv[:, :, 0:ncol]
        DMA("pool", out_view, sv, ("okv", A.ostg.i % 4), [stgR], [], is_out=True)

    class Lay:
        pass

    def make_defer(look):
        q = []

        def defer(fn):
            q.append(fn)
            while len(q) > look:
                q.pop(0)()

        def flush():
            while q:
                q.pop(0)()

        return defer, flush

    def mixer_A(ph):
        S = ph == "S"
        T = 2048 if S else 1024
        NT = T // 512
        koff = 512 if S else 0
        voff = 4 if S else 0
        A = Lay()
        kpair = scrB[:, 0:2560]
        qslots = [(scrB[:, 2560 + i * 512:3072 + i * 512], Res()) for i in range(4)]
        Vh = scrB[:, 5120:7680].rearrange("p (k v) -> p k v", v=128)
        kR = [Res() for _ in range(5)]
        vR = [Res() for _ in range(5)]
        A.pt = Rot([(scrB[:, 8704 + i * 512:9216 + i * 512], Res()) for i in range(6)])
        sqslots = [(scrB[:, 11776 + i * 512:12288 + i * 512], Res()) for i in range(2)]
        A.sq = Rot(sqslots)
        A.lsq = sqslots
        lane_setup(A, S)
        if S:
            stgK = scrF[:, 2048:2560].rearrange("p (t f) -> p t f", t=4)
            stgV = scrF[:, 2560:3072].rearrange("p (t f) -> p t f", t=4)
            stgKR, stgVR = Res(), Res()
        else:
            A.ostg = Rot([(scrF[:, 2048 + i * 512:2560 + i * 512], Res()) for i in range(4)])
        qg = smallT[:, 0:1]
        kg = smallT[:, 1:2]
        defer, flush = make_defer(4)
        vbank = [0]
        for h in range(8):
            flush()
            s = wload([
                (lambda w: wview(w, 0, 128), wsrc(wqkv_a_d, h * 128, 128)),
                (lambda w: wview(w, 1024, 128), wsrc(wqkv_a_d, 1024 + h * 128, 128)),
                (lambda w: wview(w, 2048, 128), wsrc(wqkv_a_d, 2048 + h * 128, 128)),
            ])
            Wq = wview(Wt[s], 0, 128)
            Wk = wview(Wt[s], 1024, 128)
            Wv = wview(Wt[s], 2048, 128)
            if S:
                DMA("sp", stgK, cak_d[:, h * 128:(h + 1) * 128].rearrange("(t p) f -> p t f", p=128), "stgK", [], [stgKR])
                pt, pr = ps[7], psR[7]
                for tb in range(4):
                    TR(pt[:, tb * 128:(tb + 1) * 128], stgK[:, tb, :], ident32[:, :], [stgKR, constR], [pr])
                CP(kpair[:, 0:512], pt[:, :], [pr], [kR[0]], eng="act")
                DMA("sp", stgV, cav_d[:, h * 128:(h + 1) * 128].rearrange("(t p) f -> p t f", p=128), "stgV", [], [stgVR])
                CP(Vh[:, 0:4, :], stgV, [stgVR], [vR[0]], eng="dve")
            chains = []
            for t in range(NT):
                c0 = koff + t * 512
                aft = None
                if not S:
                    aft = (lambda kn, knR, bank, t=t, h=h: out_tokmajor(
                        A, kn, knR, 128, nak_d[t * 512:(t + 1) * 512, h * 128:(h + 1) * 128].rearrange("(tb p) f -> p tb f", p=128), bank=bank))
                chains.append(lambda lane, t=t, c0=c0, aft=aft, s=s, Wk=Wk: qk_chain(
                    A, lane, lambda c: Wk[:, c, :], [WR[s][1], SLOT[s]], t, kg, S, kpair[:, c0:c0 + 512], kR[1 + t], after=aft))
            for t in range(NT):
                qp, qR = qslots[t]
                chains.append(lambda lane, t=t, qp=qp, qR=qR, s=s, Wq=Wq: qk_chain(
                    A, lane, lambda c: Wq[:, c, :], [WR[s][0], SLOT[s]], t, qg, S, qp, qR))
            fillers = []
            for t in range(NT):
                vbank[0] += 1
                b = 4 + vbank[0] % 3
                pv_, pvr = ps[b], psR[b]
                for tb in range(4):
                    def vblk(t=t, tb=tb, pv_=pv_, pvr=pvr, s=s, Wv=Wv):
                        for c in range(8):
                            MM(pv_[:, tb * 128:(tb + 1) * 128], hT[:, c, t * 512 + tb * 128:t * 512 + (tb + 1) * 128], Wv[:, c, :],
                               c == 0, c == 7, [WR[s][2], SLOT[s], hR[c][t]], [pvr])
                    fillers.append(vblk)

                def vev(t=t, pv_=pv_, pvr=pvr, h=h):
                    CP(Vh[:, voff + 4 * t:voff + 4 * t + 4, :], pv_[:, :].rearrange("p (k v) -> p k v", v=128), [pvr], [vR[1 + t]], eng="dve")
                    if not S:
                        stg, stgR = A.ostg.next()
                        CP(stg, pv_[:, :], [pvr], [stgR], eng="act")
                        DMA("pool", nav_d[t * 512:(t + 1) * 512, h * 128:(h + 1) * 128].rearrange("(tb p) f -> p tb f", p=128),
                            stg.rearrange("p (tb f) -> p tb f", tb=4), ("okv", A.ostg.i % 4), [stgR], [], is_out=True)
                fillers.append(vev)
            run_lanes(chains, fillers)
            for t in range(NT):
                qp, qR = qslots[t]
                if S:
                    segs = [(0, 512, [(kt * 128, kt, kR[0] if kt < 4 else kR[1 + (kt - 4) // 4], vR[0] if kt < 4 else vR[1 + (kt - 4) // 4]) for kt in range(20)])]
                else:
                    segs = []
                    for sq_ in range(2):
                        p0 = t * 512 + sq_ * 256
                        segs.append((sq_ * 256, 256, [(p0 + kt * 128, (p0 // 128) + kt, kR[1 + t], vR[1 + t]) for kt in range(2)]))
                for (qoff, N, ktiles) in segs:
                    nk = len(ktiles)
                    for i, (kc0, vi, kr, vr) in enumerate(ktiles):
                        sts = []
                        for m in range(2):
                            pst, pstr = pw()
                            MM(pst[:, 0:N], kpair[64 * m:64 * m + 64, kc0:kc0 + 128], qp[64 * m:64 * m + 64, qoff:qoff + N], True, True, [kr, qR], [pstr])
                            sts.append((pst, pstr))
                        for m in range(2):
                            pst, pstr = sts[m]
                            Pt, PtR = A.pt.next()
                            ACT(Pt[:, 0:N], pst[:, 0:N], AF.Exp, [pstr], [PtR], scale=0.125)

                            def pv(m=m, vi=vi, vr=vr, Pt=Pt, PtR=PtR, i=i, nk=nk, N=N, qoff=qoff):
                                MM(ps[4 + 2 * m][:, qoff:qoff + N], Vh[:, vi, :], Pt[:, 0:N], i == 0, i == nk - 1, [vr, PtR], [psR[4 + 2 * m]])
                                MM(ps[5 + 2 * m][:, qoff:qoff + N], onesb[:, :], Pt[:, 0:N], i == 0, i == nk - 1, [constR, PtR], [psR[5 + 2 * m]])
                            defer(pv)

                if True:
                    def fin(N=512, h=h, t=t, qoff=0):
                        ta, tar = A.tmp.next()
                        tb_, tbr = A.tmp.next()
                        tc_, tcr = A.tmp.next()
                        ACT(ta[:, 0:N], ps[5][:, 0:N], AF.Ln, [psR[5]], [tar])
                        ACT(ta[:, 0:N], ta[:, 0:N], AF.Exp, [tar], [tar], scale=-1.0)
                        TT(ta[:, 0:N], ps[4][:, 0:N], ta[:, 0:N], ALU.mult, [psR[4], tar], [tar])
                        ACT(tb_[:, 0:N], ps[7][:, 0:N], AF.Ln, [psR[7]], [tbr])
                        ACT(tb_[:, 0:N], tb_[:, 0:N], AF.Exp, [tbr], [tbr], scale=-1.0)
                        TT(tb_[:, 0:N], ps[6][:, 0:N], tb_[:, 0:N], ALU.mult, [psR[6], tbr], [tbr])
                        STT(tc_[:, 0:N], tb_[:, 0:N], neglam_ap, ta[:, 0:N], ALU.mult, ALU.add, [tar, tbr, constR], [tcr])
                        sq, sqr = A.sq.next()
                        ACT(sq[:, 0:N], tc_[:, 0:N], AF.Square, [tcr], [sqr])
                        pm, pmr = pw()
                        MM(pm[:, 0:N], mean128[:, :], sq[:, 0:N], True, True, [sqr, constR], [pmr])
                        ACT(ta[:, 0:N], pm[:, 0:N], AF.Ln, [pmr, constR, tar], [tar], bias=eps_ap)
                        ACT(ta[:, 0:N], ta[:, 0:N], AF.Exp, [tar], [tar], scale=-0.5)
                        STT(oT[:, h, t * 512 + qoff:t * 512 + qoff + N], tc_[:, 0:N], subg_ap, ta[:, 0:N], ALU.mult, ALU.mult,
                            [tcr, tar, constR], [oR[h][t]])
                    defer(fin)
            flush()

    def mixer_G(ph, kind):
        S = ph == "S"
        T = 2048 if S else 1024
        NT = T // 512
        koff = 512 if S else 0
        voff = 4 if S else 0
        isB = kind == "B"
        wqkv_d = wqkv_b_d if isB else wqkv_c_d
        ck_d, cv_d = (cbk_d, cbv_d) if isB else (cck_d, ccv_d)
        nk_d, nv_d = (nbk_d, nbv_d) if isB else (nck_d, ncv_d)
        qg = smallT[:, 7:8] if isB else smallT[:, 10:11]
        kg = smallT[:, 8:9] if isB else smallT[:, 11:12]
        A = Lay()
        kT2 = scrB[:, 0:2560]
        Va = scrB[:, 2560:5120].rearrange("p (k v) -> p k v", v=128)
        kR = [Res() for _ in range(5)]
        vR = [Res() for _ in range(5)]
        qslots = [(scrB[:, 5120 + i * 512:5632 + i * 512], Res()) for i in range(4)]
        A.pt = Rot([(scrB[:, 7168 + i * 512:7680 + i * 512], Res()) for i in range(4)])
        sqslots = [(scrB[:, 9216 + i * 512:9728 + i * 512], Res()) for i in range(2)]
        A.sq = Rot(sqslots)
        A.lsq = sqslots
        maskb = scrB[:, 10240:13312].rearrange("p (o q) -> p o q", o=6)
        maskR = Res()
        lane_setup(A, S)
        if S:
            stgK = scrF[:, 2048:2560].rearrange("p (t f) -> p t f", t=4)
            stgV = scrF[:, 2560:2816].rearrange("p (t f) -> p t f", t=4)
            stgM = scrF[:, 4096:4608]
            stgKR, stgVR, stgMR = Res(), Res(), Res()
            if isB:
                for o in range(6):
                    DMA("sp", stgM, c_maskb_d[o], "stgM", [], [stgMR])
                    CP(maskb[:, o, :], stgM, [stgMR], [maskR], eng="dve")
                P.fence(skip=skipw)
        else:
            A.ostg = Rot([(scrF[:, 2048 + i * 512:2560 + i * 512], Res()) for i in range(4)])
        defer, flush = make_defer(2)
        MEMSET(Va[:, :, 64:128], 1.0, [], [vR[0]])
        vonesR = vR[0]
        accrot = [0]
        vbank = [0]

        def attend(g, t, qpi):
            qp, qR = qslots[(t % 2) * 2 + qpi]
            if S:
                kts = [(kt * 128, kt, kR[0], vR[0], None) for kt in range(4)]
                if isB:
                    for o in range(-1, 5):
                        kt = 4 * t + o
                        if 0 <= kt < 16:
                            kts.append((512 + kt * 128, 4 + kt, kR[1 + kt // 4], vR[1 + kt // 4], o + 1))
                else:
                    kts += [(512 + kt * 128, 4 + kt, kR[1 + kt // 4], vR[1 + kt // 4], None) for kt in range(16)]
                segs = [(0, 512, kts)]
            else:
                segs = []
                for sq_ in range(2):
                    p0 = t * 512 + sq_ * 256
                    segs.append((sq_ * 256, 256, [(p0 + kt * 128, (p0 // 128) + kt, kR[1 + t], vR[1 + t], None) for kt in range(2)]))
            accs = []
            for hh in range(2):
                bi = 4 + accrot[0] % 4
                accrot[0] += 1
                accs.append((ps[bi], psR[bi]))
            for si, (qoff, N, ktiles) in enumerate(segs):
                nk = len(ktiles)
                for i, (kc0, vi, kr, vr, mo) in enumerate(ktiles):
                    sts = []
                    for hh in range(2):
                        pst, pstr = pw()
                        MM(pst[:, 0:N], kT2[64 * hh:64 * hh + 64, kc0:kc0 + 128], qp[64 * hh:64 * hh + 64, qoff:qoff + N], True, True, [kr, qR], [pstr])
                        sts.append((pst, pstr))
                    for hh in range(2):
                        pst, pstr = sts[hh]
                        acc, accR = accs[hh]
                        Pt, PtR = A.pt.next()
                        ACT(Pt[:, 0:N], pst[:, 0:N], AF.Exp, [pstr], [PtR], scale=0.125)
                        if mo is not None:
                            TT(Pt[:, 0:N], Pt[:, 0:N], maskb[:, mo, 0:N], ALU.mult, [PtR, maskR], [PtR])

                        def pv(acc=acc, accR=accR, vi=vi, vr=vr, Pt=Pt, PtR=PtR, i=i, nk=nk, N=N, qoff=qoff):
                            MM(acc[:, qoff:qoff + N], Va[:, vi, :], Pt[:, 0:N], i == 0, i == nk - 1, [vr, vonesR, PtR], [accR])
                        defer(pv)
            if True:
                for hh in range(2):
                    hq = g * 4 + qpi * 2 + hh
                    acc, accR = accs[hh]

                    def fin(acc=acc, accR=accR, hq=hq, N=512, t=t, qoff=0):
                        ta, tar = A.tmp.next()
                        if isB:
                            ACT(ta[64:128, 0:N], acc[64:128, 0:N], AF.Ln, [accR, constR], [tar], bias=esink[64:128, hq:hq + 1])
                        else:
                            ACT(ta[64:128, 0:N], acc[64:128, 0:N], AF.Ln, [accR], [tar])
                        ACT(ta[64:128, 0:N], ta[64:128, 0:N], AF.Exp, [tar], [tar], scale=-1.0)
                        pd = (hq % 2) * 64
                        TT(oT[pd:pd + 64, hq // 2, t * 512 + qoff:t * 512 + qoff + N], acc[0:64, 0:N], ta[64:128, 0:N], ALU.mult,
                           [accR, tar], [oR[hq // 2][t]])
                    defer(fin)

        for g in range(4):
            flush()
            s = wload([
                (lambda w: wview(w, 0, 256), wsrc(wqkv_d, g * 256, 256)),
                (lambda w: wview(w, 2048, 128)[:, :, 0:64], wsrc(wqkv_d, 1024 + g * 64, 64)),
                (lambda w: wview(w, 2048, 128)[:, :, 64:128], wsrc(wqkv_d, 1024 + g * 64, 64)),
                (lambda w: wview(w, 3072, 64), wsrc(wqkv_d, 1280 + g * 64, 64)),
            ])
            Wq = wview(Wt[s], 0, 256)
            Wk = wview(Wt[s], 2048, 128)
            Wv = wview(Wt[s], 3072, 64)
            if S:
                DMA("sp", stgK[:, :, 0:64], ck_d[:, g * 64:(g + 1) * 64].rearrange("(t p) f -> p t f", p=128), "stgK", [], [stgKR])
                DMA("sp", stgK[:, :, 64:128], ck_d[:, g * 64:(g + 1) * 64].rearrange("(t p) f -> p t f", p=128), "stgK2", [], [stgKR])
                pt, pr = ps[7], psR[7]
                for tb in range(4):
                    TR(pt[:, tb * 128:(tb + 1) * 128], stgK[:, tb, :], ident32[:, :], [stgKR, constR], [pr])
                CP(kT2[:, 0:512], pt[:, :], [pr], [kR[0]], eng="act")
                DMA("sp", stgV, cv_d[:, g * 64:(g + 1) * 64].rearrange("(t p) f -> p t f", p=128), "stgV", [], [stgVR])
                CP(Va[:, 0:4, 0:64], stgV, [stgVR, vonesR], [vR[0]], eng="dve")
            chains = []
            for t in range(NT):
                c0 = koff + t * 512
                aft = None
                if not S:
                    aft = (lambda kn, knR, bank, t=t, g=g: out_tokmajor(
                        A, kn, knR, 128, nk_d[t * 512:(t + 1) * 512, g * 64:(g + 1) * 64].rearrange("(tb p) f -> p tb f", p=128), ncol=64, bank=bank))
                chains.append(lambda lane, t=t, c0=c0, aft=aft, s=s, Wk=Wk: qk_chain(
                    A, lane, lambda c: Wk[:, c, :], [WR[s][1], WR[s][2], SLOT[s]], t, kg, S, kT2[:, c0:c0 + 512], kR[1 + t], after=aft))

            def qchain(t, qpi, s=s, Wq=Wq):
                qp, qR = qslots[(t % 2) * 2 + qpi]
                return lambda lane: qk_chain(A, lane, lambda c: Wq[:, c, qpi * 128:(qpi + 1) * 128], [WR[s][0], SLOT[s]], t, qg, S, qp, qR)
            for t in range(min(NT, 2)):
                for qpi in range(2):
                    chains.append(qchain(t, qpi))
            fillers = []
            for t in range(NT):
                vbank[0] += 1
                b = 4 + vbank[0] % 3
                pv_, pvr = ps[b], psR[b]
                for tb in range(4):
                    def vblk(t=t, tb=tb, pv_=pv_, pvr=pvr, s=s, Wv=Wv):
                        for c in range(8):
                            MM(pv_[:, tb * 64:(tb + 1) * 64], hT[:, c, t * 512 + tb * 128:t * 512 + (tb + 1) * 128], Wv[:, c, :],
                               c == 0, c == 7, [WR[s][3], SLOT[s], hR[c][t]], [pvr])
                    fillers.append(vblk)

                def vev(t=t, pv_=pv_, pvr=pvr, g=g):
                    CP(Va[:, voff + 4 * t:voff + 4 * t + 4, 0:64], pv_[:, 0:256].rearrange("p (k v) -> p k v", v=64), [pvr, vonesR], [vR[1 + t]], eng="dve")
                    if not S:
                        stg, stgR = A.ostg.next()
                        CP(stg[:, 0:256], pv_[:, 0:256], [pvr], [stgR], eng="act")
                        DMA("pool", nv_d[t * 512:(t + 1) * 512, g * 64:(g + 1) * 64].rearrange("(tb p) f -> p tb f", p=128),
                            stg[:, 0:256].rearrange("p (tb f) -> p tb f", tb=4), ("okv", A.ostg.i % 4), [stgR], [], is_out=True)
                fillers.append(vev)
            run_lanes(chains, fillers)
            for t in range(min(NT, 2)):
                for qpi in range(2):
                    attend(g, t, qpi)
            if NT == 4:
                flush()
                chains = [qchain(t, qpi) for t in (2, 3) for qpi in range(2)]
                run_lanes(chains, [])
                for t in (2, 3):
                    for qpi in range(2):
                        attend(g, t, qpi)
        flush()

    def mixer_D(ph):
        S = ph == "S"
        T = 2048 if S else 1024
        NT = T // 512
        NCH = T // 64
        A = Lay()
        qTb = scrB[:, 0:2048]
        vtok = scrB[0:64, 2048:6144].rearrange("p (n v) -> p n v", v=128)
        qR_ = [Res() for _ in range(4)]
        vtR = [Res() for _ in range(4)]
        A.qd = Rot([(scrB[:, 6144 + i * 512:6656 + i * 512], Res()) for i in range(2)])
        A.ki = Rot([(scrB[:, 7168 + i * 512:7680 + i * 512], Res()) for i in range(2)])
        A.koT = Rot([(scrB[0:64, 8192 + i * 1024:9216 + i * 1024].rearrange("p (n k) -> p n k", k=128), Res()) for i in range(2)])
        A.att = Rot([(scrB[0:64, 10240 + i * 512:10752 + i * 512], Res()) for i in range(2)])
        A.sbf = Rot([(scrB[:, 11264 + i * 128:11392 + i * 128], Res()) for i in range(3)])
        A.sq = Rot([(scrB[:, 11776 + i * 512:12288 + i * 512], Res()) for i in range(2)])
        oacc = scrF[:, 0:2048]
        oaR = [Res() for _ in range(4)]
        A.tmp = Rot([(scrF[:, 2048 + i * 512:2560 + i * 512], Res()) for i in range(5)])
        A.s32 = Rot([(scrF[:, 4608 + i * 128:4736 + i * 128], Res()) for i in range(3)])
        decs = [(misc[:, 16:24], Res()), (misc[:, 32:40], Res())]
        tots = [(misc[:, 24:32], Res()), (misc[:, 40:48], Res())]
        ptri = [0]

        def ptr():
            i = ptri[0] % 2
            ptri[0] += 1
            return ps[i], psR[i]
        gnd = smallT[:, 12:13]
        lv = lbv[:, :].rearrange("p (k d c) -> p k d c", k=2, d=2)
        poi = [0]
        for h in range(8):
            s = wload([
                (lambda w: wview(w, 0, 128), wsrc(win_d_d, h * 128, 128)),
                (lambda w: wview(w, 1024, 128), wsrc(win_d_d, 1024 + h * 128, 128)),
                (lambda w: wview(w, 2048, 128), wsrc(win_d_d, 2048 + h * 128, 128)),
                (lambda w: wview(w, 3072, 128), wsrc(win_d_d, 3072 + h * 128, 128)),
                (lambda w: wview(w, 4096, 128), wsrc(win_d_d, 4096 + h * 128, 128)),
            ])
            Wq = wview(Wt[s], 0, 128)
            Wf = [wview(Wt[s], 1024, 128), wview(Wt[s], 2048, 128)]
            Wi = wview(Wt[s], 3072, 128)
            Wgt = wview(Wt[s], 4096, 128)
            for t in range(NT):
                MEMSET(oacc[:, t * 512:(t + 1) * 512], 0.0, [], [oaR[t]])
            for t in range(NT):
                sl = slice(t * 512, (t + 1) * 512)
                pt, pr = pw()
                for c in range(8):
                    MM(pt[:, :], Wq[:, c, :], hT[:, c, sl], c == 0, c == 7, [WR[s][0], SLOT[s], hR[c][t]], [pr])
                ACT(qTb[:, sl], pt[:, :], AF.Silu, [pr], [qR_[t]])
                for half in range(2):
                    pv, pvr = pw()
                    for n4 in range(4):
                        n = half * 4 + n4
                        tk = t * 512 + n * 64
                        for c in range(8):
                            MM(pv[0:64, n4 * 128:(n4 + 1) * 128], hT[:, c, tk:tk + 64], Wi[:, c, :], c == 0, c == 7,
                               [WR[s][3], SLOT[s], hR[c][t]], [pvr])
                    EVAC(vtok[:, t * 8 + half * 4:t * 8 + half * 4 + 4, :], pv[0:64, :].rearrange("p (n v) -> p n v", v=128), [pvr], [vtR[t]])
            st = {}

            def prep(d, t, ui, s=s, Wf=Wf, h=h):
                lb_ap = lv[:, 0, d, h:h + 1]
                omlb_ap = lv[:, 1, d, h:h + 1]
                dec_, decR_ = decs[ui % 2]
                tot_, totR_ = tots[ui % 2]
                sl = slice(t * 512, (t + 1) * 512)
                pz, pzr = ptr()
                for c in range(8):
                    MM(pz[:, :], Wf[d][:, c, :], hT[:, c, sl], c == 0, c == 7, [WR[s][1 + d], SLOT[s], hR[c][t]], [pzr])
                f_, fR = A.tmp.next()
                ACT(f_, pz[:, :], AF.Exp, [pzr], [fR], scale=-1.0)
                yield
                ACT(f_, f_, AF.Ln, [fR, constR], [fR], bias=one_ap)
                yield
                ACT(f_, f_, AF.Exp, [fR], [fR], scale=-1.0)
                yield
                TS(f_, f_, omlb_ap, lb_ap, ALU.mult, ALU.add, [fR, constR], [fR])
                yield
                lf, lfR = A.tmp.next()
                ACT(lf, f_, AF.Ln, [fR], [lfR])
                TS(f_, f_, -1.0, 1.0, ALU.mult, ALU.add, [fR], [fR])
                yield
                cum, cumR = A.tmp.next()
                P.add("dve", lambda e, cum=cum, lf=lf: e.tensor_tensor_scan(cum, resetm[:, :], lf, 0.0, op0=ALU.mult, op1=ALU.add),
                      [lfR, constR], [cumR])
                cum3 = cum.rearrange("p (n c) -> p n c", c=64)
                CP(tot_[:, 0:8], cum3[:, :, 63], [cumR], [totR_], eng="dve")
                yield
                if d == 1:
                    TT(cum, lf, cum, ALU.subtract, [lfR, cumR], [cumR])
                    TT(cum3, cum3, tot_[:, 0:8].unsqueeze(2).to_broadcast([128, 8, 64]), ALU.add, [cumR, totR_], [cumR])
                    yield
                ACT(dec_[:, 0:8], tot_[:, 0:8], AF.Exp, [totR_], [decR_])
                e1, e1R = A.tmp.next()
                ACT(e1, cum, AF.Exp, [cumR], [e1R])
                STT(lf.rearrange("p (n c) -> p n c", c=64), cum3, -1.0, tot_[:, 0:8].unsqueeze(2).to_broadcast([128, 8, 64]),
                    ALU.mult, ALU.add, [cumR, totR_, lfR], [lfR])
                yield
                ACT(lf, lf, AF.Exp, [lfR], [lfR])
                qd, qdR = A.qd.next()
                TT(qd, qTb[:, sl], e1, ALU.mult, [qR_[t], e1R], [qdR])
                yield
                e2, e2R = A.tmp.next()
                ACT(e2, cum, AF.Exp, [cumR], [e2R], scale=-1.0)
                TT(lf, lf, f_, ALU.mult, [lfR, fR], [lfR])
                yield
                ki, kiR = A.ki.next()
                TT(ki, f_, e2, ALU.mult, [fR, e2R], [kiR])
                koT, koTR = A.koT.next()
                for half in range(2):
                    pk, pkr = ptr()
                    for n4 in range(4):
                        n = half * 4 + n4
                        TR(pk[0:64, n4 * 128:(n4 + 1) * 128], lf[:, n * 64:(n + 1) * 64], ident32[:, :], [lfR, constR], [pkr])
                    EVAC(koT[:, half * 4:half * 4 + 4, :], pk[0:64, :].rearrange("p (n k) -> p n k", k=128), [pkr], [koTR])
                    yield
                pa_, par = ptr()
                for n in range(8):
                    cs = slice(n * 64, (n + 1) * 64)
                    MM(pa_[0:64, cs], ki[:, cs], qd[:, cs], True, True, [kiR, qdR], [par])
                attm, attmR = A.att.next()
                TT(attm.rearrange("p (n c) -> p n c", c=64), pa_[0:64, :].rearrange("p (n c) -> p n c", c=64),
                   maskhf[:, d * 64:(d + 1) * 64].unsqueeze(1).to_broadcast([64, 8, 64]), ALU.mult, [par, constR], [attmR])
                yield
                pb = (2, 3) if ui % 2 == 0 else (6, 7)
                pds = [(ps[pb[0]], psR[pb[0]]), (ps[pb[1]], psR[pb[1]])]
                for n in range(8):
                    ng = t * 8 + n
                    MM(pds[n // 4][0][:, (n % 4) * 128:(n % 4 + 1) * 128], koT[:, n, :], vtok[:, ng, :], True, True,
                       [koTR, vtR[t]], [pds[n // 4][1]])
                st[ui] = (qd, qdR, attm, attmR, pds, dec_, decR_)
                yield

            def chain(d, t, ui, h=h):
                qd, qdR, attm, attmR, pds, dec_, decR_ = st.pop(ui)
                sl = slice(t * 512, (t + 1) * 512)
                first = (t == 0) if d == 0 else (t == NT - 1)
                if first:
                    s32, s32R = A.s32.next()
                    sbf, sbfR = A.sbf.next()
                    if S:
                        DMA("sp", s32, sd_d[d, h], ("s32", A.s32.i % 3), [], [s32R])
                    else:
                        MEMSET(s32, 0.0, [], [s32R])
                    CP(sbf, s32, [s32R], [sbfR], eng="dve")
                else:
                    s32, s32R, sbf, sbfR = cur["s"]
                po, por = ps[4 + (ui % 2)], psR[4 + (ui % 2)]
                chunks = list(range(8)) if d == 0 else list(range(7, -1, -1))
                for n in chunks:
                    ng = t * 8 + n
                    cs = slice(n * 64, (n + 1) * 64)
                    if not S and ((d == 0 and ng % 4 == 0) or (d == 1 and ng % 4 == 3)) and not (ng == (0 if d == 0 else NCH - 1)):
                        s32, s32R = A.s32.next()
                        sbf, sbfR = A.sbf.next()
                        MEMSET(s32, 0.0, [], [s32R])
                        MEMSET(sbf, 0.0, [], [sbfR])
                    MM(po[:, cs], vtok[:, ng, :], attm[:, cs], True, False, [vtR[t], attmR], [por])
                    MM(po[:, cs], sbf, qd[:, cs], False, True, [sbfR, qdR], [por])
                    n32, n32R = A.s32.next()
                    pdn, pdnR = pds[n // 4]
                    STT(n32, s32, dec_[:, n:n + 1], pdn[:, (n % 4) * 128:(n % 4 + 1) * 128], ALU.mult, ALU.add, [s32R, decR_, pdnR], [n32R])
                    s32, s32R = n32, n32R
                    if not S and ((d == 0 and ng % 4 == 3) or (d == 1 and ng % 4 == 0)):
                        DMA("sp", nsd_d[ng // 4, d, h], s32, ("nsd", A.s32.i % 3), [s32R], [], is_out=True)
                    else:
                        sbf, sbfR = A.sbf.next()
                        CP(sbf, s32, [s32R], [sbfR], eng="dve")
                    yield
                TT(oacc[:, sl], oacc[:, sl], po[:, :], ALU.add, [oaR[t], por], [oaR[t]])
                cur["s"] = (s32, s32R, sbf, sbfR)
                yield

            cur = {}
            units = [(0, t) for t in range(NT)] + [(1, t) for t in range(NT - 1, -1, -1)]
            prev = None
            for ui, (d, t) in enumerate(units):
                pg = prep(d, t, ui)
                if prev is None:
                    for _ in pg:
                        pass
                else:
                    a_done = b_done = False
                    while not (a_done and b_done):
                        if not a_done:
                            a_done = next(pg, "END") == "END"
                        if not b_done:
                            b_done = next(prev, "END") == "END"
                prev = chain(d, t, ui)
            for _ in prev:
                pass
            for t in range(NT):
                sl = slice(t * 512, (t + 1) * 512)
                pg, pgr = pw()
                for c in range(8):
                    MM(pg[:, :], Wgt[:, c, :], hT[:, c, sl], c == 0, c == 7, [WR[s][4], SLOT[s], hR[c][t]], [pgr])
                sg, sgr = A.tmp.next()
                ACT(sg, pg[:, :], AF.Silu, [pgr], [sgr])
                sq, sqr = A.sq.next()
                ACT(sq, oacc[:, sl], AF.Square, [oaR[t]], [sqr])
                pm, pmr = pw()
                MM(pm[:, :], mean128[:, :], sq, True, True, [sqr, constR], [pmr])
                t1, t1r = A.tmp.next()
                ACT(t1, pm[:, :], AF.Ln, [pmr, constR], [t1r], bias=eps_ap)
                ACT(t1, t1, AF.Exp, [t1r], [t1r], scale=-0.5)
                STT(t1, oacc[:, sl], gnd, t1, ALU.mult, ALU.mult, [oaR[t], t1r, constR], [t1r])
                TT(oT[:, h, sl], t1, sg, ALU.mult, [t1r, sgr], [oR[h][t]])

    def skipw(key):
        return isinstance(key, tuple) and key[0] == "dma" and isinstance(key[1], tuple) and key[1][0] == "w"

    prologue()
    for ph in phases:
        S = ph == "S"
        T = 2048 if S else 1024
        NT = T // 512
        r = 1 if S else 0
        P.fence(skip=skipw)
        P.new_epoch()
        load_x(xs_d if S else xp_d, T)
        for li in range(nlayers):
            L = DL
            if li == 0:
                norm_mod(L, NT, li, 0, r)
            P.fence(skip=skipw)
            kind = li % 4
            if kind == 0:
                mixer_A(ph)
                wo_d = wo_a_d
            elif kind == 1:
                mixer_G(ph, "B")
                wo_d = wo_b_d
            elif kind == 2:
                mixer_G(ph, "C")
                wo_d = wo_c_d
            else:
                mixer_D(ph)
                wo_d = wo_d_d
            P.fence(skip=skipw)
            wo_proj(NT, li, r, wo_d, after=lambda t, li=li: norm_tile(L, t, li, 1, r))
            side = None
            if ph == phases[0] and li + 1 < nlayers:
                side = ada_steps(li + 1, [(scrF[:, 3072 + i * 1024:4096 + i * 1024].rearrange("p (c n) -> p c n", c=8), Res()) for i in range(2)],
                                 128, ps[7], psR[7])
            nxt = (lambda t, li=li: norm_tile(L, t, li + 1, 0, r)) if li + 1 < nlayers else None
            ffn(L, NT, li, r, side, after=nxt)
        store_y(ys_d if S else yp_d, T)

    semstack = ExitStack()
    with semstack:
        run = P.emit(lambda name: semstack.enter_context(nc.semaphore(name)))
        with nc.allow_non_contiguous_dma(reason="small strided loads"):
            with nc.Block() as block:
                @block.tensor
                def _(e):
                    run("pe", e)

                @block.scalar
                def _(e):
                    run("act", e)

                @block.vector
                def _(e):
                    run("dve", e)

                @block.gpsimd
                def _(e):
                    run("pool", e)

                @block.sync
                def _(e):
                    run("sp", e)
    es.close()
    return nc


def _consts():
    f = np.float32
    ident = np.eye(128, dtype=f)
    prot = np.zeros((128, 128), f)
    for p in range(128):
        if (p % 32) < 16:
            prot[p + 16, p] = -1.0
        else:
            prot[p - 16, p] = 1.0
    maskb = np.zeros((6, 128, 512), f)
    k = np.arange(128)[:, None]
    q = np.arange(512)[None, :]
    for o in range(-1, 5):
        maskb[o + 1] = (np.abs(q - 128 * o - k) <= 128).astype(f)
    s = np.arange(64)[:, None]
    c = np.arange(64)[None, :]
    maskh = np.stack([(c >= s).astype(f), (c <= s).astype(f)], 0)
    reset = np.ones((128, 512), f)
    reset[:, ::64] = 0.0
    tpos = np.arange(2048)
    row = (tpos // 64).astype(np.float64)
    col = (tpos % 64).astype(np.float64)
    inv_freq = 10000.0 ** (-np.arange(0, 32, 2, dtype=np.float64) / 32.0)
    ang = np.zeros((64, 2048), np.float64)
    for d in range(64):
        a = d // 32
        fi = d % 16
        ang[d] = (row if a == 0 else col) * inv_freq[fi]
    ang32 = np.zeros((64, 2048), np.float32)
    invf32 = (np.float32(10000.0) ** (-np.arange(0, 32, 2, dtype=np.float32) / np.float32(32.0))).astype(np.float32)
    for d in range(64):
        a = d // 32
        fi = d % 16
        ang32[d] = ((row if a == 0 else col).astype(np.float32) * invf32[fi]).astype(np.float32)
    cos = np.cos(ang32.astype(np.float64)).astype(f)
    sin = np.sin(ang32.astype(np.float64)).astype(f)
    cos = np.concatenate([cos, cos], 0)
    sin = np.concatenate([sin, sin], 0)
    return dict(c_ident=ident, c_prot=prot, c_maskb=maskb, c_maskh=maskh, c_reset=reset, c_cos=cos, c_sin=sin)


_NC_CACHE = {}


def _in_maps(inp, cores=range(8)):
    f = np.float32
    A = lambda x: np.ascontiguousarray(np.asarray(x, dtype=f))
    consts = _consts()
    small = np.zeros((16, 128), f)

    def dup(v):
        v = np.asarray(v, f).reshape(-1)
        return np.concatenate([v, v]) if v.size == 64 else v

    small[0] = dup(inp["qn_a"][0]); small[1] = dup(inp["kn_a"][0]); small[2] = np.asarray(inp["subln_a"][0], f)
    small[3, :64] = inp["lam_q1_a"][0]; small[4, :64] = inp["lam_k1_a"][0]
    small[5, :64] = inp["lam_q2_a"][0]; small[6, :64] = inp["lam_k2_a"][0]
    small[7] = dup(inp["qn_b"][0]); small[8] = dup(inp["kn_b"][0]); small[9, :16] = inp["sink_b"][0]
    small[10] = dup(inp["qn_c"][0]); small[11] = dup(inp["kn_c"][0]); small[12] = np.asarray(inp["gn_d"][0], f)
    shared = dict(
        w_ada=A(inp["w_ada"]), w_ffn_gate=A(inp["w_ffn_gate"]), w_ffn_up=A(inp["w_ffn_up"]), w_ffn_down=A(inp["w_ffn_down"]),
        w_qkv_a=A(inp["w_qkv_a"][0]), w_o_a=A(inp["w_o_a"][0]), w_qkv_b=A(inp["w_qkv_b"][0]), w_o_b=A(inp["w_o_b"][0]),
        w_qkv_c=A(inp["w_qkv_c"][0]), w_o_c=A(inp["w_o_c"][0]), w_in_d=A(inp["w_in_d"][0]), w_o_d=A(inp["w_o_d"][0]),
        small=small, **consts)
    maps = []
    for core in cores:
        b = core // 2
        vecs = np.zeros((384, 128), f)
        vecs[0:192] = np.asarray(inp["b_ada"], f).reshape(192, 128)
        vecs[192:224] = np.asarray(inp["norm_mix"], f).reshape(32, 128)
        vecs[224:256] = np.asarray(inp["norm_ffn"], f).reshape(32, 128)
        vecs[256:320] = np.asarray(inp["lb_logits_d"], f).reshape(64, 128)
        vecs[320:328] = np.asarray(inp["c_ctx"], f).reshape(8, 128)
        vecs[328:336] = np.asarray(inp["c"][b], f).reshape(8, 128)
        m = dict(shared)
        m.update(
            xs=A(inp["x_sample"][b]), xp=A(np.asarray(inp["x_prompt"][4 * core:4 * core + 4]).reshape(1024, 1024)), vecs=vecs,
            cak=A(np.asarray(inp["cache_a_k"][b, 0]).reshape(512, 1024)), cav=A(np.asarray(inp["cache_a_v"][b, 0]).reshape(512, 1024)),
            cbk=A(np.asarray(inp["cache_b_k"][b, 0]).reshape(512, 256)), cbv=A(np.asarray(inp["cache_b_v"][b, 0]).reshape(512, 256)),
            cck=A(np.asarray(inp["cache_c_k"][b, 0]).reshape(512, 256)), ccv=A(np.asarray(inp["cache_c_v"][b, 0]).reshape(512, 256)),
            sd=A(inp["state_d"][b, 0]))
        maps.append(m)
    return maps


def kernel(**inp):
    if "nc" not in _NC_CACHE:
        _NC_CACHE["nc"] = build()
    nc = _NC_CACHE["nc"]
    maps = _in_maps(inp)
    res = run_bass_kernel_spmd(nc, maps, core_ids=list(range(8)))
    R = res.results
    f = np.float32
    y_prompt = np.concatenate([R[c]["yp"].reshape(4, 256, 1024) for c in range(8)], 0).astype(f)
    y_sample = np.stack([R[2 * b]["ys"] for b in range(4)], 0).astype(f)
    cat = lambda name, shp: np.concatenate([R[c][name].reshape((4, 1, 256) + shp) for c in range(8)], 0).astype(f)
    new_a_k = cat("nak", (16, 64))
    new_a_v = cat("nav", (8, 128))
    new_b_k = cat("nbk", (4, 64))
    new_b_v = cat("nbv", (4, 64))
    new_c_k = cat("nck", (4, 64))
    new_c_v = cat("ncv", (4, 64))
    new_d = np.concatenate([R[c]["nsd"].reshape(4, 1, 2, 8, 128, 128) for c in range(8)], 0).astype(f)
    return (y_prompt, y_sample, new_a_k, new_a_v, new_b_k, new_b_v, new_c_k, new_c_v, new_d)
```

```python
import math, os
from contextlib import ExitStack
import numpy as np
import concourse.bass as bass
import concourse.mybir as mybir
from concourse.bass_utils import run_bass_kernel_spmd

F32 = mybir.dt.float32
BF16 = mybir.dt.bfloat16
AF = mybir.ActivationFunctionType
ALU = mybir.AluOpType
AX = mybir.AxisListType

ENGS = ("pe", "act", "dve", "pool", "sp")
EPS = 1e-6


class Res:
    __slots__ = ("name", "w", "r", "excl")

    def __init__(self, name="", excl=False):
        self.name = name
        self.w = None
        self.r = {}
        self.excl = excl


class Op:
    __slots__ = ("eng", "key", "idx", "fn", "waits", "needs_inc", "inc_val", "is_dma", "epoch")

    def __init__(self, eng, key, idx, fn, is_dma=False):
        self.eng = eng
        self.key = key
        self.idx = idx
        self.fn = fn
        self.waits = []
        self.needs_inc = False
        self.inc_val = None
        self.is_dma = is_dma
        self.epoch = 0


class Prog:
    def __init__(self, nc):
        self.nc = nc
        self.ops = {e: [] for e in ENGS}
        self.streams = {e: [] for e in ENGS}
        self.seen = {e: {} for e in ENGS}
        self.epoch = 0
        self.out_dma_keys = set()
        self.pending_fence = {e: [] for e in ENGS}

    def new_epoch(self):
        self.epoch += 1

    def fence(self, engines=("pe", "act", "dve", "sp"), skip=lambda key: False):
        lasts = [lst[-1] for key, lst in self.streams.items() if lst and not skip(key)]
        for e in engines:
            self.pending_fence[e] = list(lasts)

    def add(self, eng, fn, reads=(), writes=(), dma_key=None, is_out=False):
        is_dma = dma_key is not None
        key = ("dma", dma_key) if is_dma else eng
        if key not in self.streams:
            self.streams[key] = []
        op = Op(eng, key, len(self.streams[key]), fn, is_dma)
        op.epoch = 0 if is_dma else self.epoch
        deps = {}

        def put(d):
            if d is None:
                return
            cur = deps.get(d.key)
            if cur is None or cur.idx < d.idx:
                deps[d.key] = d

        for r in reads:
            put(r.w)
            if r.excl:
                for d in r.r.values():
                    if d.key != key:
                        put(d)
        for w in writes:
            put(w.w)
            for d in w.r.values():
                put(d)
        fence_keys = set()
        if self.pending_fence[eng]:
            for d in self.pending_fence[eng]:
                put(d)
                fence_keys.add(d.key)
            self.pending_fence[eng] = []
        seen = self.seen[eng]
        for k, d in deps.items():
            if k == "pe" and eng == "pe" and not is_dma and k not in fence_keys:
                continue
            if k == eng and not is_dma and d is op:
                continue
            if seen.get(k, -1) >= d.idx:
                continue
            seen[k] = d.idx
            d.needs_inc = True
            op.waits.append(d)
        self.streams[key].append(op)
        self.ops[eng].append(op)
        for r in reads:
            r.r[key] = op
        for w in writes:
            w.w = op
            w.r = {}
        if is_out:
            self.out_dma_keys.add(key)
        return op

    def emit(self, sem_alloc):
        sems = {}
        for key, lst in self.streams.items():
            cnt = {}
            for op in lst:
                if op.is_dma:
                    op.needs_inc = True
                if op.needs_inc:
                    sk = (key, op.epoch)
                    cnt[sk] = cnt.get(sk, 0) + (16 if op.is_dma else 1)
                    op.inc_val = cnt[sk]
                    if sk not in sems:
                        sems[sk] = sem_alloc("s%d" % len(sems))
        self.sems = sems
        final_out = []
        for key in self.out_dma_keys:
            last = self.streams[key][-1]
            final_out.append((sems[(key, last.epoch)], last.inc_val))

        def run(engname, eng):
            for op in self.ops[engname]:
                for d in op.waits:
                    eng.wait_ge(sems[(d.key, d.epoch)], d.inc_val)
                ins = op.fn(eng)
                if op.needs_inc:
                    ins.then_inc(sems[(op.key, op.epoch)], 16 if op.is_dma else 1)
            if engname == "sp":
                for s, v in final_out:
                    eng.wait_ge(s, v)

        return run


class Rot:
    def __init__(self, items):
        self.items = items
        self.i = 0

    def next(self):
        it = self.items[self.i % len(self.items)]
        self.i += 1
        return it


def build(nlayers=4, phases="SP"):
    nc = bass.Bass("TRN2", target_bir_lowering=False)
    P = Prog(nc)
    es = ExitStack()

    def din(name, shape):
        return nc.dram_tensor(name, list(shape), F32, kind="ExternalInput").ap()

    def dout(name, shape):
        return nc.dram_tensor(name, list(shape), F32, kind="ExternalOutput").ap()

    xs_d = din("xs", [2048, 1024])
    xp_d = din("xp", [1024, 1024])
    vecs_d = din("vecs", [384, 128])
    cak_d = din("cak", [512, 1024])
    cav_d = din("cav", [512, 1024])
    cbk_d = din("cbk", [512, 256])
    cbv_d = din("cbv", [512, 256])
    cck_d = din("cck", [512, 256])
    ccv_d = din("ccv", [512, 256])
    sd_d = din("sd", [2, 8, 128, 128])
    w_ada_d = din("w_ada", [4, 1024, 6144])
    wg_d = din("w_ffn_gate", [4, 1024, 2816])
    wu_d = din("w_ffn_up", [4, 1024, 2816])
    wd_d = din("w_ffn_down", [4, 2816, 1024])
    wqkv_a_d = din("w_qkv_a", [1024, 3072])
    wo_a_d = din("w_o_a", [1024, 1024])
    wqkv_b_d = din("w_qkv_b", [1024, 1536])
    wo_b_d = din("w_o_b", [1024, 1024])
    wqkv_c_d = din("w_qkv_c", [1024, 1536])
    wo_c_d = din("w_o_c", [1024, 1024])
    win_d_d = din("w_in_d", [1024, 5120])
    wo_d_d = din("w_o_d", [1024, 1024])
    small_d = din("small", [16, 128])
    c_ident_d = din("c_ident", [128, 128])
    c_prot_d = din("c_prot", [128, 128])
    c_maskb_d = din("c_maskb", [6, 128, 512])
    c_maskh_d = din("c_maskh", [2, 64, 64])
    c_reset_d = din("c_reset", [128, 512])
    c_cos_d = din("c_cos", [128, 2048])
    c_sin_d = din("c_sin", [128, 2048])

    ys_d = dout("ys", [2048, 1024])
    yp_d = dout("yp", [1024, 1024])
    nak_d = dout("nak", [1024, 1024])
    nav_d = dout("nav", [1024, 1024])
    nbk_d = dout("nbk", [1024, 256])
    nbv_d = dout("nbv", [1024, 256])
    nck_d = dout("nck", [1024, 256])
    ncv_d = dout("ncv", [1024, 256])
    nsd_d = dout("nsd", [4, 2, 8, 128, 128])

    def sb(name, shape, dt):
        return es.enter_context(nc.sbuf_tensor(name, shape, dt))

    xT_t = sb("xT", [128, 8 * 2048], F32)
    hT_t = sb("hT", [128, 8 * 2048], BF16)
    oT_t = sb("oT", [128, 8 * 2048], BF16)
    xT = xT_t[:, :].rearrange("p (c t) -> p c t", c=8)
    hT = hT_t[:, :].rearrange("p (c t) -> p c t", c=8)
    oT = oT_t[:, :].rearrange("p (c t) -> p c t", c=8)
    Wt = [sb("W%d" % i, [128, 6144], BF16) for i in range(2)]
    scrB = sb("scrB", [128, 13312], BF16)
    scrF = sb("scrF", [128, 5120], F32)
    ident32 = sb("ident32", [128, 128], F32)
    prot32 = sb("prot32", [128, 128], F32)
    onesb = sb("onesb", [128, 128], BF16)
    mean1024 = sb("mean1024", [128, 128], BF16)
    blk64 = sb("blk64", [128, 128], BF16)
    mean128 = sb("mean128", [128, 128], BF16)
    vT = sb("vT", [128, 384], F32)
    modt = sb("modt", [128, 4 * 48 * 2], F32)
    mod = modt[:, :].rearrange("p (l j r) -> p l j r", l=4, j=48)
    dert = sb("dert", [128, 4 * 2 * 8 * 2], F32)
    der = dert[:, :].rearrange("p (l k c r) -> p l k c r", l=4, k=2, c=8)
    smallT = sb("smallT", [128, 16], F32)
    misc = sb("misc", [128, 64], F32)
    esink = sb("esink", [128, 16], F32)
    lbt = sb("lbt", [128, 64], F32)
    lbv = sb("lbv", [128, 32], F32)
    condS = sb("condS", [128, 16], F32)
    maskhf = sb("maskhf", [64, 128], F32)
    resetm = sb("resetm", [128, 512], F32)
    ps = [es.enter_context(nc.psum_tensor("ps%d" % i, [128, 512], F32)) for i in range(8)]
    psR = [Res("ps%d" % i, excl=True) for i in range(8)]

    constR = Res("const")
    modR = [Res("mod%d" % i) for i in range(4)]
    xR = [[Res() for t in range(4)] for c in range(8)]
    hR = [[Res() for t in range(4)] for c in range(8)]
    oR = [[Res() for t in range(4)] for c in range(8)]
    WR = [[Res() for i in range(6)] for s in range(2)]
    SLOT = [Res(), Res()]

    eps_ap = misc[:, 0:1]
    neglam_ap = misc[:, 1:2]
    subg_ap = misc[:, 2:3]
    one_ap = misc[:, 3:4]

    def MM(out, lhsT, rhs, start, stop, R, Wr):
        P.add("pe", lambda e: e.matmul(out, lhsT, rhs, start=start, stop=stop), R, Wr)

    def TR(out, in_, ident, R, Wr):
        P.add("pe", lambda e: e.transpose(out, in_, ident), R, Wr)

    def ACT(out, in_, func, R, Wr, bias=None, scale=None):
        kw = {}
        if bias is not None:
            kw["bias"] = bias
        if scale is not None:
            kw["scale"] = scale
        P.add("act", lambda e: e.activation(out, in_, func, **kw), R, Wr)

    def TT(out, in0, in1, op, R, Wr, eng="dve"):
        P.add(eng, lambda e: e.tensor_tensor(out, in0, in1, op=op), R, Wr)

    def STT(out, in0, scalar, in1, op0, op1, R, Wr, eng="dve"):
        P.add(eng, lambda e: e.scalar_tensor_tensor(out, in0, scalar, in1, op0=op0, op1=op1), R, Wr)

    def TS(out, in0, s1, s2, op0, op1, R, Wr, eng="dve"):
        if s2 is None:
            P.add(eng, lambda e: e.tensor_scalar(out, in0, s1, None, op0=op0), R, Wr)
        else:
            P.add(eng, lambda e: e.tensor_scalar(out, in0, s1, s2, op0=op0, op1=op1), R, Wr)

    def CP(out, in_, R, Wr, eng="dve"):
        if eng == "act":
            P.add("act", lambda e: e.activation(out, in_, AF.Copy), R, Wr)
        else:
            P.add(eng, lambda e: e.tensor_copy(out, in_), R, Wr)

    def RECIP(out, in_, R, Wr):
        P.add("dve", lambda e: e.reciprocal(out, in_), R, Wr)

    def MEMSET(out, val, R, Wr, eng="dve"):
        P.add(eng, lambda e: e.memset(out, val), R, Wr)

    def DMA(q, out, in_, key, R, Wr, is_out=False):
        P.add(q, lambda e: e.dma_start(out=out, in_=in_), R, Wr, dma_key=key, is_out=is_out)

    pwi = [0]

    def pw():
        i = pwi[0] % 4
        pwi[0] += 1
        return ps[i], psR[i]

    evi = [0]

    def EVAC(out, in_, R, Wr):
        evi[0] += 1
        CP(out, in_, R, Wr, eng="act" if evi[0] % 2 else "dve")

    wjob = [0]

    def wload(parts):
        s = wjob[0] % 2
        wjob[0] += 1
        for i, (dstf, src) in enumerate(parts):
            wr = [WR[s][i]] + ([SLOT[s]] if i == 0 else [])
            DMA("pool", dstf(Wt[s]), src, ("w", s, i), [], wr)
        return s

    def wview(s_t, off, ncol):
        return s_t[:, off:off + 8 * ncol].rearrange("p (c n) -> p c n", c=8)

    def wsrc(w2d, col0, ncol):
        return w2d.rearrange("(c p) n -> p c n", p=128)[:, :, col0:col0 + ncol]

    def prologue():
        lamt = scrF[:, 384:640]
        DMA("sp", ident32[:, :], c_ident_d, "c0", [], [constR])
        DMA("sp", prot32[:, :], c_prot_d, "c1", [], [constR])
        DMA("sp", resetm[:, :], c_reset_d, "c2", [], [constR])
        DMA("sp", maskhf[:, :].rearrange("p (d c) -> p d c", d=2), c_maskh_d.rearrange("d s c -> s d c"), "c3", [], [constR])
        DMA("sp", smallT[:, :], small_d.rearrange("r p -> p r"), "c4", [], [constR])
        DMA("sp", esink[:, :], small_d[9, 0:16].partition_broadcast(128), "c5", [], [constR])
        for i in range(4):
            DMA("sp", lamt[:, i * 64:(i + 1) * 64], small_d[3 + i, 0:64].partition_broadcast(128), "c6", [], [constR])
        MEMSET(onesb[:, :], 1.0, [], [constR])
        MEMSET(mean1024[:, :], 1.0 / 1024, [], [constR])
        MEMSET(mean128[:, :], 1.0 / 128, [], [constR])
        MEMSET(blk64[:, :], 0.0, [], [constR])
        MEMSET(blk64[0:64, 0:64], 1.0 / 64, [], [constR])
        MEMSET(blk64[64:128, 64:128], 1.0 / 64, [], [constR])
        MEMSET(misc[:, :], 0.0, [], [constR])
        MEMSET(misc[:, 0:1], EPS, [], [constR])
        MEMSET(misc[:, 3:4], 1.0, [], [constR])
        stg = scrF[:, 0:384].rearrange("p (k f) -> p k f", k=3)
        stgR = Res()
        DMA("sp", stg, vecs_d.rearrange("(k p) f -> p k f", p=128), "c7", [], [stgR])
        for k in range(3):
            TR(ps[0][:, k * 128:(k + 1) * 128], stg[:, k, :], ident32[:, :], [stgR, constR], [psR[0]])
        CP(vT[:, :], ps[0][:, 0:384], [psR[0]], [constR])
        cS = condS[:, :].rearrange("p (c r) -> p c r", r=2)
        for r in range(2):
            ACT(cS[:, :, r], vT[:, 320 + 8 * r:328 + 8 * r], AF.Silu, [constR], [constR])
        ACT(esink[:, :], esink[:, :], AF.Exp, [constR], [constR])
        lam_init = 0.8 - 0.6 * math.exp(-0.3 * 0)
        TT(lamt[:, 0:64], lamt[:, 0:64], lamt[:, 64:128], ALU.mult, [constR], [constR])
        TT(lamt[:, 128:192], lamt[:, 128:192], lamt[:, 192:256], ALU.mult, [constR], [constR])
        P.add("dve", lambda e: e.reduce_sum(misc[:, 8:9], lamt[:, 0:64], axis=AX.X), [constR], [constR])
        P.add("dve", lambda e: e.reduce_sum(misc[:, 9:10], lamt[:, 128:192], axis=AX.X), [constR], [constR])
        ACT(misc[:, 8:10], misc[:, 8:10], AF.Exp, [constR], [constR])
        TT(misc[:, 10:11], misc[:, 9:10], misc[:, 8:9], ALU.subtract, [constR], [constR])
        TS(misc[:, 1:2], misc[:, 10:11], -lam_init, None, ALU.add, None, [constR], [constR])
        TS(misc[:, 2:3], smallT[:, 2:3], 1.0 - lam_init, None, ALU.mult, None, [constR], [constR])
        ACT(lbt[:, :], vT[:, 256:320], AF.Exp, [constR], [constR])
        lb4 = lbt[:, :].rearrange("p (d l c) -> p d l c", d=2, l=4)
        lv = lbv[:, :].rearrange("p (k d c) -> p k d c", k=2, d=2)
        li_d = 3
        for d in range(2):
            TT(lv[:, 0, d, :], lb4[:, d, 1, :], lb4[:, d, 2, :], ALU.add, [constR], [constR])
            TT(lv[:, 0, d, :], lv[:, 0, d, :], lb4[:, d, 3, :], ALU.add, [constR], [constR])
            TT(lv[:, 1, d, :], lv[:, 0, d, :], lb4[:, d, 0, :], ALU.add, [constR], [constR])
            RECIP(lv[:, 1, d, :], lv[:, 1, d, :], [constR], [constR])
            TT(lv[:, 0, d, :], lv[:, 0, d, :], lv[:, 1, d, :], ALU.mult, [constR], [constR])
            TS(lv[:, 1, d, :], lv[:, 0, d, :], -1.0, 1.0, ALU.mult, ALU.add, [constR], [constR])
        for step in ada_steps(0, [(scrF[:, 1024 + i * 2048:1024 + (i + 1) * 2048].rearrange("p (c n) -> p c n", c=8), Res()) for i in range(2)], 256, ps[4], psR[4]):
            pass

    def ada_steps(li, slots, ncol, acc, accR):
        cS = condS[:, :].rearrange("p (c r) -> p c r", r=2)
        rot = Rot(slots)
        ngrp = 6144 // ncol
        for jg in range(ngrp):
            wsl, wslR = rot.next()
            DMA("sp", wsl, wsrc(w_ada_d[li], jg * ncol, ncol), ("ada", rot.i % len(slots)), [], [wslR])
            for jj in range(ncol // 128):
                j = jg * (ncol // 128) + jj
                for kc in range(8):
                    MM(acc[:, 2 * j:2 * j + 2], wsl[:, kc, jj * 128:(jj + 1) * 128], cS[:, kc, :], kc == 0, kc == 7,
                       [wslR, constR], [accR])
            yield
        TT(mod[:, li, :, :], acc[:, 0:96].rearrange("p (j r) -> p j r", r=2),
           vT[:, li * 48:(li + 1) * 48].unsqueeze(2).to_broadcast([128, 48, 2]), ALU.add, [accR, constR], [modR[li]])
        STT(der[:, li, 0, :, :], mod[:, li, 8:16, :], 1.0, vT[:, 192 + li * 8:200 + li * 8].unsqueeze(2).to_broadcast([128, 8, 2]),
            ALU.add, ALU.mult, [constR, modR[li]], [modR[li]])
        STT(der[:, li, 1, :, :], mod[:, li, 32:40, :], 1.0, vT[:, 224 + li * 8:232 + li * 8].unsqueeze(2).to_broadcast([128, 8, 2]),
            ALU.add, ALU.mult, [constR, modR[li]], [modR[li]])
        yield

    def modv(li, m, c, r):
        return mod[:, li, m * 8 + c, r:r + 1]

    class Dense:
        pass

    def dense_layout():
        L = Dense()
        L.sq = Rot([(scrB[:, i * 512:(i + 1) * 512], Res()) for i in range(2)])
        L.a = Rot([(scrB[:, 1024 + i * 1024:1024 + (i + 1) * 1024].rearrange("p (j n) -> p j n", j=2), Res()) for i in range(2)])
        L.sd = Rot([(scrF[:, i * 512:(i + 1) * 512], Res()) for i in range(2)])
        L.tmp = Rot([(scrF[:, 1024 + i * 512:1024 + (i + 1) * 512], Res()) for i in range(2)])
        L.s = Rot([(scrF[:, 2048 + i * 512:2048 + (i + 1) * 512], Res()) for i in range(2)])
        L.stg = Rot([(scrF[:, 3072 + i * 1024:3072 + (i + 1) * 1024], Res()) for i in range(2)])
        return L

    DL = dense_layout()

    def load_x(x_d, T):
        L = DL
        for t in range(T // 512):
            for tb in range(4):
                stg, stgR = L.stg.next()
                r0 = t * 512 + tb * 128
                DMA("sp", stg, x_d[r0:r0 + 128, :], ("xin", L.stg.i % 2), [], [stgR])
                for c in range(8):
                    TR(ps[c][:, tb * 128:(tb + 1) * 128], stg[:, c * 128:(c + 1) * 128], ident32[:, :], [stgR, constR], [psR[c]])
            for c in range(8):
                EVAC(xT[:, c, t * 512:(t + 1) * 512], ps[c][:, :], [psR[c]], [xR[c][t]])

    def store_y(y_d, T):
        L = DL
        for tb in range(T // 128):
            t = tb // 4
            stg, stgR = L.stg.next()
            for c in range(8):
                b = 4 + (tb % 2) * 2 + c // 4
                TR(ps[b][:, (c % 4) * 128:(c % 4 + 1) * 128], xT[:, c, tb * 128:(tb + 1) * 128], ident32[:, :],
                   [xR[c][t], constR], [psR[b]])
            for hf in range(2):
                b = 4 + (tb % 2) * 2 + hf
                EVAC(stg[:, hf * 512:(hf + 1) * 512], ps[b][:, :], [psR[b]], [stgR])
            DMA("sp", y_d[tb * 128:(tb + 1) * 128, :], stg, ("yout", L.stg.i % 2), [stgR], [], is_out=True)

    def norm_mod(L, NT, li, which, r):
        for t in range(NT):
            norm_tile(L, t, li, which, r)

    def norm_tile(L, t, li, which, r):
        if True:
            sl = slice(t * 512, (t + 1) * 512)
            pt, pr = pw()
            for c in range(8):
                sq, sqr = L.sq.next()
                ACT(sq, xT[:, c, sl], AF.Square, [xR[c][t]], [sqr])
                MM(pt[:, :], mean1024[:, :], sq, c == 0, c == 7, [sqr, constR], [pr])
            sd, sdr = L.sd.next()
            ACT(sd, pt[:, :], AF.Ln, [pr, constR], [sdr], bias=eps_ap)
            ACT(sd, sd, AF.Exp, [sdr], [sdr], scale=-0.5)
            for c in range(8):
                tmp, tr = L.tmp.next()
                TT(tmp, xT[:, c, sl], sd, ALU.mult, [xR[c][t], sdr], [tr])
                ACT(hT[:, c, sl], tmp, AF.Identity, [tr, modR[li]], [hR[c][t]],
                    bias=modv(li, which * 3 + 0, c, r), scale=der[:, li, which, c, r:r + 1])

    def ffn(L, NT, li, r, side=None, after=None):
        fdefer, fflush = make_defer(1)
        fin_after = []
        nb = 3 if side is not None else 4
        for g in range(11):
            s = wload([
                (lambda w: wview(w, 0, 256), wsrc(wg_d[li], g * 256, 256)),
                (lambda w: wview(w, 2048, 256), wsrc(wu_d[li], g * 256, 256)),
                (lambda w: w[:, 4096:6144].rearrange("p (j n) -> p j n", j=2),
                 wd_d[li][g * 256:(g + 1) * 256, :].rearrange("(j p) n -> p j n", p=128)),
            ])
            Wg = wview(Wt[s], 0, 256)
            Wu = wview(Wt[s], 2048, 256)
            Wd = Wt[s][:, 4096:6144].rearrange("p (j n) -> p j n", j=2)
            for t in range(NT):
                sl = slice(t * 512, (t + 1) * 512)
                a, aR = L.a.next()
                for j in range(2):
                    pg, pgr = pw()
                    for c in range(8):
                        MM(pg[:, :], Wg[:, c, j * 128:(j + 1) * 128], hT[:, c, sl], c == 0, c == 7, [WR[s][0], SLOT[s], hR[c][t]], [pgr])
                    pu, pur = pw()
                    for c in range(8):
                        MM(pu[:, :], Wu[:, c, j * 128:(j + 1) * 128], hT[:, c, sl], c == 0, c == 7, [WR[s][1], SLOT[s], hR[c][t]], [pur])
                    sg, sgr = L.s.next()
                    ACT(sg, pg[:, :], AF.Silu, [pgr], [sgr])
                    TT(a[:, j, :], sg, pu[:, :], ALU.mult, [sgr, pur], [aR])

                if side is not None:
                    for _ in range(2):
                        next(side, None)

                def down(s=s, Wd=Wd, a=a, aR=aR, t=t, sl=sl, g=g):
                    if g == 10 and after is not None:
                        fin_after.append(t)
                    for cp in range(8):
                        b = 4 + cp % nb
                        for j in range(2):
                            MM(ps[b][:, :], Wd[:, j, cp * 128:(cp + 1) * 128], a[:, j, :], j == 0, j == 1, [WR[s][2], SLOT[s], aR], [psR[b]])
                        STT(xT[:, cp, sl], ps[b][:, :], modv(li, 5, cp, r), xT[:, cp, sl], ALU.mult, ALU.add,
                            [psR[b], xR[cp][t], modR[li]], [xR[cp][t]])
                    while fin_after:
                        after(fin_after.pop(0))
                fdefer(down)
        fflush()
        if side is not None:
            for _ in side:
                pass

    def wo_proj(NT, li, r, wo_d, after=None):
        for hf in range(2):
            s = wload([(lambda w: wview(w, 0, 512), wsrc(wo_d, hf * 512, 512))])
            Wo = wview(Wt[s], 0, 512)
            for t in range(NT):
                sl = slice(t * 512, (t + 1) * 512)
                for cl in range(4):
                    cp = hf * 4 + cl
                    b = 4 + cl
                    for c in range(8):
                        MM(ps[b][:, :], Wo[:, c, cl * 128:(cl + 1) * 128], oT[:, c, sl], c == 0, c == 7, [WR[s][0], SLOT[s], oR[c][t]], [psR[b]])
                    STT(xT[:, cp, sl], ps[b][:, :], modv(li, 2, cp, r), xT[:, cp, sl], ALU.mult, ALU.add,
                        [psR[b], xR[cp][t], modR[li]], [xR[cp][t]])
                if hf == 1 and after is not None:
                    after(t)

    def qk_proj(A, lhs_fn, lhsR, M, t, gain_ap, rope, dst, dstR):
        sl = slice(t * 512, (t + 1) * 512)
        pt, pr = pw()
        for c in range(8):
            MM(pt[0:M, :], lhs_fn(c), hT[:, c, sl], c == 0, c == 7, lhsR + [hR[c][t]], [pr])
        sq, sqr = A.sq.next()
        ACT(sq[0:M, :], pt[0:M, :], AF.Square, [pr], [sqr])
        pm, pmr = pw()
        MM(pm[0:M, :], blk64[0:M, 0:M], sq[0:M, :], True, True, [sqr, constR], [pmr])
        t1, t1r = A.tmp.next()
        ACT(t1[0:M, :], pm[0:M, :], AF.Ln, [pmr, constR], [t1r], bias=eps_ap[0:M, :])
        ACT(t1[0:M, :], t1[0:M, :], AF.Exp, [t1r], [t1r], scale=-0.5)
        t2, t2r = A.tmp.next()
        STT(t2[0:M, :], pt[0:M, :], gain_ap, t1[0:M, :], ALU.mult, ALU.mult, [pr, t1r, constR], [t2r])
        if rope is not None:
            cos_ap, sin_ap, ropeR = rope
            pq, pqr = pw()
            MM(pq[0:M, :], prot32[0:M, 0:M], t2[0:M, :], True, True, [t2r, constR], [pqr])
            t3, t3r = A.tmp.next()
            TT(t3[0:M, :], t2[0:M, :], cos_ap[0:M, :], ALU.mult, [t2r, ropeR], [t3r])
            TT(t1[0:M, :], pq[0:M, :], sin_ap[0:M, :], ALU.mult, [pqr, ropeR, t1r], [t1r])
            TT(dst, t3[0:M, :], t1[0:M, :], ALU.add, [t3r, t1r], [dstR])
        else:
            CP(dst, t2[0:M, :], [t2r], [dstR], eng="act")
        return t2, t2r

    def qk_chain(A, lane, lhs_fn, lhsR, t, gain_ap, use_rope, dst, dstR, after=None):
        bX, bY = 2 * lane, 2 * lane + 1
        pX, pXr, pY, pYr = ps[bX], psR[bX], ps[bY], psR[bY]
        (t1, t1r), (t2, t2r) = A.ltmp[lane]
        sq, sqr = A.lsq[lane]
        sl = slice(t * 512, (t + 1) * 512)
        if use_rope:
            slot, ropeR = A.lrope[lane]
            cos_ap, sin_ap = slot[:, 0:512], slot[:, 512:1024]
            DMA("sp", cos_ap, c_cos_d[:, t * 512:(t + 1) * 512], ("rope", lane, 0), [], [ropeR])
            DMA("sp", sin_ap, c_sin_d[:, t * 512:(t + 1) * 512], ("rope", lane, 1), [], [ropeR])
        for c in range(8):
            MM(pX[:, :], lhs_fn(c), hT[:, c, sl], c == 0, c == 7, lhsR + [hR[c][t]], [pXr])
        ACT(sq, pX[:, :], AF.Square, [pXr], [sqr])
        yield
        MM(pY[:, :], blk64[:, :], sq, True, True, [sqr, constR], [pYr])
        ACT(t1, pY[:, :], AF.Ln, [pYr, constR], [t1r], bias=eps_ap)
        yield
        ACT(t1, t1, AF.Exp, [t1r], [t1r], scale=-0.5)
        yield
        STT(t2, pX[:, :], gain_ap, t1, ALU.mult, ALU.mult, [pXr, t1r, constR], [t2r])
        yield
        if use_rope:
            MM(pX[:, :], prot32[:, :], t2, True, True, [t2r, constR], [pXr])
            yield
            TT(t1, pX[:, :], sin_ap, ALU.mult, [pXr, ropeR, t1r], [t1r])
            yield
            TT(t2, t2, cos_ap, ALU.mult, [t2r, ropeR], [t2r])
            yield
            TT(dst, t2, t1, ALU.add, [t2r, t1r], [dstR])
        else:
            CP(dst, t2, [t2r], [dstR], eng="act")
            if after is not None:
                after(t2, t2r, bY)
        yield

    def run_lanes(chains, fillers):
        active = [None, None]
        it = iter(chains)
        while True:
            progressed = False
            for lane in range(2):
                if active[lane] is None:
                    nxt = next(it, None)
                    if nxt is not None:
                        active[lane] = nxt(lane)
                if active[lane] is not None:
                    if next(active[lane], "END") == "END":
                        active[lane] = None
                    progressed = True
            if fillers:
                fillers.pop(0)()
                progressed = True
            if not progressed:
                break

    def lane_setup(A, S):
        tmpslots = [(scrF[:, i * 512:(i + 1) * 512], Res()) for i in range(4)]
        A.tmp = Rot(tmpslots)
        A.ltmp = [[tmpslots[0], tmpslots[1]], [tmpslots[2], tmpslots[3]]]
        if S:
            A.lrope = [(scrF[:, 3072 + i * 1024:4096 + i * 1024], Res()) for i in range(2)]

    def load_rope(A, t):
        slot, slotR = A.rope.next()
        cos_ap = slot[:, 0:512]
        sin_ap = slot[:, 512:1024]
        DMA("sp", cos_ap, c_cos_d[:, t * 512:(t + 1) * 512], ("rope", A.rope.i % 2, 0), [], [slotR])
        DMA("sp", sin_ap, c_sin_d[:, t * 512:(t + 1) * 512], ("rope", A.rope.i % 2, 1), [], [slotR])
        return cos_ap, sin_ap, slotR

    def out_tokmajor(A, src, srcR, M, out_view, ncol=None, bank=None):
        pt, pr = pw() if bank is None else (ps[bank], psR[bank])
        for tb in range(4):
            TR(pt[:, tb * M:(tb + 1) * M], src[0:M, tb * 128:(tb + 1) * 128], ident32[0:M, 0:M], [srcR, constR], [pr])
        stg, stgR = A.ostg.next()
        EVAC(stg[:, 0:4 * M], pt[:, 0:4 * M], [pr], [stgR])
        sv = stg[:, 0:4 * M].rearrange("p (tb f) -> p tb f", tb=4)
        if ncol is not None:
            sv = sv[:, :, 0:ncol]
        DMA("pool", out_view, sv, ("okv", A.ostg.i % 4), [stgR], [], is_out=True)

    class Lay:
        pass

    def make_defer(look):
        q = []

        def defer(fn):
            q.append(fn)
            while len(q) > look:
                q.pop(0)()

        def flush():
            while q:
                q.pop(0)()

        return defer, flush

    def mixer_A(ph):
        S = ph == "S"
        T = 2048 if S else 1024
        NT = T // 512
        koff = 512 if S else 0
        voff = 4 if S else 0
        A = Lay()
        kpair = scrB[:, 0:2560]
        qslots = [(scrB[:, 2560 + i * 512:3072 + i * 512], Res()) for i in range(4)]
        Vh = scrB[:, 5120:7680].rearrange("p (k v) -> p k v", v=128)
        kR = [Res() for _ in range(5)]
        vR = [Res() for _ in range(5)]
        A.pt = Rot([(scrB[:, 8704 + i * 512:9216 + i * 512], Res()) for i in range(6)])
        sqslots = [(scrB[:, 11776 + i * 512:12288 + i * 512], Res()) for i in range(2)]
        A.sq = Rot(sqslots)
        A.lsq = sqslots
        lane_setup(A, S)
        if S:
            stgK = scrF[:, 2048:2560].rearrange("p (t f) -> p t f", t=4)
            stgV = scrF[:, 2560:3072].rearrange("p (t f) -> p t f", t=4)
            stgKR, stgVR = Res(), Res()
        else:
            A.ostg = Rot([(scrF[:, 2048 + i * 512:2560 + i * 512], Res()) for i in range(4)])
        qg = smallT[:, 0:1]
        kg = smallT[:, 1:2]
        defer, flush = make_defer(2)
        vbank = [0]
        for h in range(8):
            flush()
            s = wload([
                (lambda w: wview(w, 0, 128), wsrc(wqkv_a_d, h * 128, 128)),
                (lambda w: wview(w, 1024, 128), wsrc(wqkv_a_d, 1024 + h * 128, 128)),
                (lambda w: wview(w, 2048, 128), wsrc(wqkv_a_d, 2048 + h * 128, 128)),
            ])
            Wq = wview(Wt[s], 0, 128)
            Wk = wview(Wt[s], 1024, 128)
            Wv = wview(Wt[s], 2048, 128)
            if S:
                DMA("sp", stgK, cak_d[:, h * 128:(h + 1) * 128].rearrange("(t p) f -> p t f", p=128), "stgK", [], [stgKR])
                pt, pr = ps[7], psR[7]
                for tb in range(4):
                    TR(pt[:, tb * 128:(tb + 1) * 128], stgK[:, tb, :], ident32[:, :], [stgKR, constR], [pr])
                CP(kpair[:, 0:512], pt[:, :], [pr], [kR[0]], eng="act")
                DMA("sp", stgV, cav_d[:, h * 128:(h + 1) * 128].rearrange("(t p) f -> p t f", p=128), "stgV", [], [stgVR])
                CP(Vh[:, 0:4, :], stgV, [stgVR], [vR[0]], eng="dve")
            chains = []
            for t in range(NT):
                c0 = koff + t * 512
                aft = None
                if not S:
                    aft = (lambda kn, knR, bank, t=t, h=h: out_tokmajor(
                        A, kn, knR, 128, nak_d[t * 512:(t + 1) * 512, h * 128:(h + 1) * 128].rearrange("(tb p) f -> p tb f", p=128), bank=bank))
                chains.append(lambda lane, t=t, c0=c0, aft=aft, s=s, Wk=Wk: qk_chain(
                    A, lane, lambda c: Wk[:, c, :], [WR[s][1], SLOT[s]], t, kg, S, kpair[:, c0:c0 + 512], kR[1 + t], after=aft))
            for t in range(NT):
                qp, qR = qslots[t]
                chains.append(lambda lane, t=t, qp=qp, qR=qR, s=s, Wq=Wq: qk_chain(
                    A, lane, lambda c: Wq[:, c, :], [WR[s][0], SLOT[s]], t, qg, S, qp, qR))
            fillers = []
            for t in range(NT):
                vbank[0] += 1
                b = 4 + vbank[0] % 3
                pv_, pvr = ps[b], psR[b]
                for tb in range(4):
                    def vblk(t=t, tb=tb, pv_=pv_, pvr=pvr, s=s, Wv=Wv):
                        for c in range(8):
                            MM(pv_[:, tb * 128:(tb + 1) * 128], hT[:, c, t * 512 + tb * 128:t * 512 + (tb + 1) * 128], Wv[:, c, :],
                               c == 0, c == 7, [WR[s][2], SLOT[s], hR[c][t]], [pvr])
                    fillers.append(vblk)

                def vev(t=t, pv_=pv_, pvr=pvr, h=h):
                    CP(Vh[:, voff + 4 * t:voff + 4 * t + 4, :], pv_[:, :].rearrange("p (k v) -> p k v", v=128), [pvr], [vR[1 + t]], eng="act")
                    if not S:
                        stg, stgR = A.ostg.next()
                        CP(stg, pv_[:, :], [pvr], [stgR], eng="act")
                        DMA("pool", nav_d[t * 512:(t + 1) * 512, h * 128:(h + 1) * 128].rearrange("(tb p) f -> p tb f", p=128),
                            stg.rearrange("p (tb f) -> p tb f", tb=4), ("okv", A.ostg.i % 4), [stgR], [], is_out=True)
                fillers.append(vev)
            run_lanes(chains, fillers)
            for t in range(NT):
                qp, qR = qslots[t]
                if S:
                    segs = [(0, 512, [(kt * 128, kt, kR[0] if kt < 4 else kR[1 + (kt - 4) // 4], vR[0] if kt < 4 else vR[1 + (kt - 4) // 4]) for kt in range(20)])]
                else:
                    segs = []
                    for sq_ in range(2):
                        p0 = t * 512 + sq_ * 256
                        segs.append((sq_ * 256, 256, [(p0 + kt * 128, (p0 // 128) + kt, kR[1 + t], vR[1 + t]) for kt in range(2)]))
                for (qoff, N, ktiles) in segs:
                    nk = len(ktiles)
                    for i, (kc0, vi, kr, vr) in enumerate(ktiles):
                        sts = []
                        for m in range(2):
                            pst, pstr = pw()
                            MM(pst[:, 0:N], kpair[64 * m:64 * m + 64, kc0:kc0 + 128], qp[64 * m:64 * m + 64, qoff:qoff + N], True, True, [kr, qR], [pstr])
                            sts.append((pst, pstr))
                        for m in range(2):
                            pst, pstr = sts[m]
                            Pt, PtR = A.pt.next()
                            ACT(Pt[:, 0:N], pst[:, 0:N], AF.Exp, [pstr], [PtR], scale=0.125)

                            def pv(m=m, vi=vi, vr=vr, Pt=Pt, PtR=PtR, i=i, nk=nk, N=N, qoff=qoff):
                                MM(ps[4 + 2 * m][:, qoff:qoff + N], Vh[:, vi, :], Pt[:, 0:N], i == 0, i == nk - 1, [vr, PtR], [psR[4 + 2 * m]])
                                MM(ps[5 + 2 * m][:, qoff:qoff + N], onesb[:, :], Pt[:, 0:N], i == 0, i == nk - 1, [constR, PtR], [psR[5 + 2 * m]])
                            defer(pv)

                if True:
                    def fin(N=512, h=h, t=t, qoff=0):
                        ta, tar = A.tmp.next()
                        tb_, tbr = A.tmp.next()
                        tc_, tcr = A.tmp.next()
                        ACT(ta[:, 0:N], ps[5][:, 0:N], AF.Ln, [psR[5]], [tar])
                        ACT(ta[:, 0:N], ta[:, 0:N], AF.Exp, [tar], [tar], scale=-1.0)
                        TT(ta[:, 0:N], ps[4][:, 0:N], ta[:, 0:N], ALU.mult, [psR[4], tar], [tar])
                        ACT(tb_[:, 0:N], ps[7][:, 0:N], AF.Ln, [psR[7]], [tbr])
                        ACT(tb_[:, 0:N], tb_[:, 0:N], AF.Exp, [tbr], [tbr], scale=-1.0)
                        TT(tb_[:, 0:N], ps[6][:, 0:N], tb_[:, 0:N], ALU.mult, [psR[6], tbr], [tbr])
                        STT(tc_[:, 0:N], tb_[:, 0:N], neglam_ap, ta[:, 0:N], ALU.mult, ALU.add, [tar, tbr, constR], [tcr])
                        sq, sqr = A.sq.next()
                        ACT(sq[:, 0:N], tc_[:, 0:N], AF.Square, [tcr], [sqr])
                        pm, pmr = pw()
                        MM(pm[:, 0:N], mean128[:, :], sq[:, 0:N], True, True, [sqr, constR], [pmr])
                        ACT(ta[:, 0:N], pm[:, 0:N], AF.Ln, [pmr, constR, tar], [tar], bias=eps_ap)
                        ACT(ta[:, 0:N], ta[:, 0:N], AF.Exp, [tar], [tar], scale=-0.5)
                        STT(oT[:, h, t * 512 + qoff:t * 512 + qoff + N], tc_[:, 0:N], subg_ap, ta[:, 0:N], ALU.mult, ALU.mult,
                            [tcr, tar, constR], [oR[h][t]])
                    defer(fin)
            flush()

    def mixer_G(ph, kind):
        S = ph == "S"
        T = 2048 if S else 1024
        NT = T // 512
        koff = 512 if S else 0
        voff = 4 if S else 0
        isB = kind == "B"
        wqkv_d = wqkv_b_d if isB else wqkv_c_d
        ck_d, cv_d = (cbk_d, cbv_d) if isB else (cck_d, ccv_d)
        nk_d, nv_d = (nbk_d, nbv_d) if isB else (nck_d, ncv_d)
        qg = smallT[:, 7:8] if isB else smallT[:, 10:11]
        kg = smallT[:, 8:9] if isB else smallT[:, 11:12]
        A = Lay()
        kT2 = scrB[:, 0:2560]
        Va = scrB[:, 2560:5120].rearrange("p (k v) -> p k v", v=128)
        kR = [Res() for _ in range(5)]
        vR = [Res() for _ in range(5)]
        qslots = [(scrB[:, 5120 + i * 512:5632 + i * 512], Res()) for i in range(4)]
        A.pt = Rot([(scrB[:, 7168 + i * 512:7680 + i * 512], Res()) for i in range(4)])
        sqslots = [(scrB[:, 9216 + i * 512:9728 + i * 512], Res()) for i in range(2)]
        A.sq = Rot(sqslots)
        A.lsq = sqslots
        maskb = scrB[:, 10240:13312].rearrange("p (o q) -> p o q", o=6)
        maskR = Res()
        lane_setup(A, S)
        if S:
            stgK = scrF[:, 2048:2560].rearrange("p (t f) -> p t f", t=4)
            stgV = scrF[:, 2560:2816].rearrange("p (t f) -> p t f", t=4)
            stgM = scrF[:, 4096:4608]
            stgKR, stgVR, stgMR = Res(), Res(), Res()
            if isB:
                for o in range(6):
                    DMA("sp", stgM, c_maskb_d[o], "stgM", [], [stgMR])
                    CP(maskb[:, o, :], stgM, [stgMR], [maskR], eng="dve")
                P.fence(skip=skipw)
        else:
            A.ostg = Rot([(scrF[:, 2048 + i * 512:2560 + i * 512], Res()) for i in range(4)])
        defer, flush = make_defer(2)
        MEMSET(Va[:, :, 64:128], 1.0, [], [vR[0]])
        vonesR = vR[0]
        accrot = [0]
        vbank = [0]

        def attend(g, t, qpi):
            qp, qR = qslots[(t % 2) * 2 + qpi]
            if S:
                kts = [(kt * 128, kt, kR[0], vR[0], None) for kt in range(4)]
                if isB:
                    for o in range(-1, 5):
                        kt = 4 * t + o
                        if 0 <= kt < 16:
                            kts.append((512 + kt * 128, 4 + kt, kR[1 + kt // 4], vR[1 + kt // 4], o + 1))
                else:
                    kts += [(512 + kt * 128, 4 + kt, kR[1 + kt // 4], vR[1 + kt // 4], None) for kt in range(16)]
                segs = [(0, 512, kts)]
            else:
                segs = []
                for sq_ in range(2):
                    p0 = t * 512 + sq_ * 256
                    segs.append((sq_ * 256, 256, [(p0 + kt * 128, (p0 // 128) + kt, kR[1 + t], vR[1 + t], None) for kt in range(2)]))
            accs = []
            for hh in range(2):
                bi = 4 + accrot[0] % 4
                accrot[0] += 1
                accs.append((ps[bi], psR[bi]))
            for si, (qoff, N, ktiles) in enumerate(segs):
                nk = len(ktiles)
                for i, (kc0, vi, kr, vr, mo) in enumerate(ktiles):
                    sts = []
                    for hh in range(2):
                        pst, pstr = pw()
                        MM(pst[:, 0:N], kT2[64 * hh:64 * hh + 64, kc0:kc0 + 128], qp[64 * hh:64 * hh + 64, qoff:qoff + N], True, True, [kr, qR], [pstr])
                        sts.append((pst, pstr))
                    for hh in range(2):
                        pst, pstr = sts[hh]
                        acc, accR = accs[hh]
                        Pt, PtR = A.pt.next()
                        ACT(Pt[:, 0:N], pst[:, 0:N], AF.Exp, [pstr], [PtR], scale=0.125)
                        if mo is not None:
                            TT(Pt[:, 0:N], Pt[:, 0:N], maskb[:, mo, 0:N], ALU.mult, [PtR, maskR], [PtR])

                        def pv(acc=acc, accR=accR, vi=vi, vr=vr, Pt=Pt, PtR=PtR, i=i, nk=nk, N=N, qoff=qoff):
                            MM(acc[:, qoff:qoff + N], Va[:, vi, :], Pt[:, 0:N], i == 0, i == nk - 1, [vr, vonesR, PtR], [accR])
                        defer(pv)
            if True:
                for hh in range(2):
                    hq = g * 4 + qpi * 2 + hh
                    acc, accR = accs[hh]

                    def fin(acc=acc, accR=accR, hq=hq, N=512, t=t, qoff=0):
                        ta, tar = A.tmp.next()
                        if isB:
                            ACT(ta[64:128, 0:N], acc[64:128, 0:N], AF.Ln, [accR, constR], [tar], bias=esink[64:128, hq:hq + 1])
                        else:
                            ACT(ta[64:128, 0:N], acc[64:128, 0:N], AF.Ln, [accR], [tar])
                        ACT(ta[64:128, 0:N], ta[64:128, 0:N], AF.Exp, [tar], [tar], scale=-1.0)
                        pd = (hq % 2) * 64
                        TT(oT[pd:pd + 64, hq // 2, t * 512 + qoff:t * 512 + qoff + N], acc[0:64, 0:N], ta[64:128, 0:N], ALU.mult,
                           [accR, tar], [oR[hq // 2][t]])
                    defer(fin)

        for g in range(4):
            flush()
            s = wload([
                (lambda w: wview(w, 0, 256), wsrc(wqkv_d, g * 256, 256)),
                (lambda w: wview(w, 2048, 128)[:, :, 0:64], wsrc(wqkv_d, 1024 + g * 64, 64)),
                (lambda w: wview(w, 2048, 128)[:, :, 64:128], wsrc(wqkv_d, 1024 + g * 64, 64)),
                (lambda w: wview(w, 3072, 64), wsrc(wqkv_d, 1280 + g * 64, 64)),
            ])
            Wq = wview(Wt[s], 0, 256)
            Wk = wview(Wt[s], 2048, 128)
            Wv = wview(Wt[s], 3072, 64)
            if S:
                DMA("sp", stgK[:, :, 0:64], ck_d[:, g * 64:(g + 1) * 64].rearrange("(t p) f -> p t f", p=128), "stgK", [], [stgKR])
                DMA("sp", stgK[:, :, 64:128], ck_d[:, g * 64:(g + 1) * 64].rearrange("(t p) f -> p t f", p=128), "stgK2", [], [stgKR])
                pt, pr = ps[7], psR[7]
                for tb in range(4):
                    TR(pt[:, tb * 128:(tb + 1) * 128], stgK[:, tb, :], ident32[:, :], [stgKR, constR], [pr])
                CP(kT2[:, 0:512], pt[:, :], [pr], [kR[0]], eng="act")
                DMA("sp", stgV, cv_d[:, g * 64:(g + 1) * 64].rearrange("(t p) f -> p t f", p=128), "stgV", [], [stgVR])
                CP(Va[:, 0:4, 0:64], stgV, [stgVR, vonesR], [vR[0]], eng="dve")
            chains = []
            for t in range(NT):
                c0 = koff + t * 512
                aft = None
                if not S:
                    aft = (lambda kn, knR, bank, t=t, g=g: out_tokmajor(
                        A, kn, knR, 128, nk_d[t * 512:(t + 1) * 512, g * 64:(g + 1) * 64].rearrange("(tb p) f -> p tb f", p=128), ncol=64, bank=bank))
                chains.append(lambda lane, t=t, c0=c0, aft=aft, s=s, Wk=Wk: qk_chain(
                    A, lane, lambda c: Wk[:, c, :], [WR[s][1], WR[s][2], SLOT[s]], t, kg, S, kT2[:, c0:c0 + 512], kR[1 + t], after=aft))

            def qchain(t, qpi, s=s, Wq=Wq):
                qp, qR = qslots[(t % 2) * 2 + qpi]
                return lambda lane: qk_chain(A, lane, lambda c: Wq[:, c, qpi * 128:(qpi + 1) * 128], [WR[s][0], SLOT[s]], t, qg, S, qp, qR)
            for t in range(min(NT, 2)):
                for qpi in range(2):
                    chains.append(qchain(t, qpi))
            fillers = []
            for t in range(NT):
                vbank[0] += 1
                b = 4 + vbank[0] % 3
                pv_, pvr = ps[b], psR[b]
                for tb in range(4):
                    def vblk(t=t, tb=tb, pv_=pv_, pvr=pvr, s=s, Wv=Wv):
                        for c in range(8):
                            MM(pv_[:, tb * 64:(tb + 1) * 64], hT[:, c, t * 512 + tb * 128:t * 512 + (tb + 1) * 128], Wv[:, c, :],
                               c == 0, c == 7, [WR[s][3], SLOT[s], hR[c][t]], [pvr])
                    fillers.append(vblk)

                def vev(t=t, pv_=pv_, pvr=pvr, g=g):
                    CP(Va[:, voff + 4 * t:voff + 4 * t + 4, 0:64], pv_[:, 0:256].rearrange("p (k v) -> p k v", v=64), [pvr, vonesR], [vR[1 + t]], eng="act")
                    if not S:
                        stg, stgR = A.ostg.next()
                        CP(stg[:, 0:256], pv_[:, 0:256], [pvr], [stgR], eng="act")
                        DMA("pool", nv_d[t * 512:(t + 1) * 512, g * 64:(g + 1) * 64].rearrange("(tb p) f -> p tb f", p=128),
                            stg[:, 0:256].rearrange("p (tb f) -> p tb f", tb=4), ("okv", A.ostg.i % 4), [stgR], [], is_out=True)
                fillers.append(vev)
            run_lanes(chains, fillers)
            for t in range(min(NT, 2)):
                for qpi in range(2):
                    attend(g, t, qpi)
            if NT == 4:
                flush()
                chains = [qchain(t, qpi) for t in (2, 3) for qpi in range(2)]
                run_lanes(chains, [])
                for t in (2, 3):
                    for qpi in range(2):
                        attend(g, t, qpi)
        flush()

    def mixer_D(ph):
        S = ph == "S"
        T = 2048 if S else 1024
        NT = T // 512
        NCH = T // 64
        A = Lay()
        qTb = scrB[:, 0:2048]
        vtok = scrB[0:64, 2048:6144].rearrange("p (n v) -> p n v", v=128)
        vtok128 = scrB[:, 2048:6144].rearrange("p (n v) -> p n v", v=128)
        zR = Res()
        MEMSET(scrB[64:128, 2048:6144], 0.0, [], [zR])
        MEMSET(scrB[64:128, 8192:11264], 0.0, [], [zR])
        qR_ = [Res() for _ in range(4)]
        vtR = [Res() for _ in range(4)]
        A.qd = Rot([(scrB[:, 6144 + i * 512:6656 + i * 512], Res()) for i in range(2)])
        A.ki = Rot([(scrB[:, 7168 + i * 512:7680 + i * 512], Res()) for i in range(2)])
        A.koT = Rot([((scrB[0:64, 8192 + i * 1024:9216 + i * 1024].rearrange("p (n k) -> p n k", k=128), scrB[:, 8192 + i * 1024:9216 + i * 1024].rearrange("p (n k) -> p n k", k=128)), Res()) for i in range(2)])
        A.att = Rot([((scrB[0:64, 10240 + i * 512:10752 + i * 512], scrB[:, 10240 + i * 512:10752 + i * 512]), Res()) for i in range(2)])
        A.sbf = Rot([(scrB[:, 11264 + i * 128:11392 + i * 128], Res()) for i in range(3)])
        A.sq = Rot([(scrB[:, 11776 + i * 512:12288 + i * 512], Res()) for i in range(2)])
        oacc = scrF[:, 0:2048]
        oaR = [Res() for _ in range(4)]
        A.tmp = Rot([(scrF[:, 2048 + i * 512:2560 + i * 512], Res()) for i in range(5)])
        A.s32 = Rot([(scrF[:, 4608 + i * 128:4736 + i * 128], Res()) for i in range(3)])
        decs = [(misc[:, 16:24], Res()), (misc[:, 32:40], Res())]
        tots = [(misc[:, 24:32], Res()), (misc[:, 40:48], Res())]
        ptri = [0]

        def ptr():
            i = ptri[0] % 2
            ptri[0] += 1
            return ps[i], psR[i]
        gnd = smallT[:, 12:13]
        lv = lbv[:, :].rearrange("p (k d c) -> p k d c", k=2, d=2)
        poi = [0]
        for h in range(8):
            s = wload([
                (lambda w: wview(w, 0, 128), wsrc(win_d_d, h * 128, 128)),
                (lambda w: wview(w, 1024, 128), wsrc(win_d_d, 1024 + h * 128, 128)),
                (lambda w: wview(w, 2048, 128), wsrc(win_d_d, 2048 + h * 128, 128)),
                (lambda w: wview(w, 3072, 128), wsrc(win_d_d, 3072 + h * 128, 128)),
                (lambda w: wview(w, 4096, 128), wsrc(win_d_d, 4096 + h * 128, 128)),
            ])
            Wq = wview(Wt[s], 0, 128)
            Wf = [wview(Wt[s], 1024, 128), wview(Wt[s], 2048, 128)]
            Wi = wview(Wt[s], 3072, 128)
            Wgt = wview(Wt[s], 4096, 128)
            for t in range(NT):
                MEMSET(oacc[:, t * 512:(t + 1) * 512], 0.0, [], [oaR[t]])
            for t in range(NT):
                sl = slice(t * 512, (t + 1) * 512)
                pt, pr = pw()
                for c in range(8):
                    MM(pt[:, :], Wq[:, c, :], hT[:, c, sl], c == 0, c == 7, [WR[s][0], SLOT[s], hR[c][t]], [pr])
                ACT(qTb[:, sl], pt[:, :], AF.Silu, [pr], [qR_[t]])
                for half in range(2):
                    pv, pvr = pw()
                    for n4 in range(4):
                        n = half * 4 + n4
                        tk = t * 512 + n * 64
                        for c in range(8):
                            MM(pv[0:64, n4 * 128:(n4 + 1) * 128], hT[:, c, tk:tk + 64], Wi[:, c, :], c == 0, c == 7,
                               [WR[s][3], SLOT[s], hR[c][t]], [pvr])
                    EVAC(vtok[:, t * 8 + half * 4:t * 8 + half * 4 + 4, :], pv[0:64, :].rearrange("p (n v) -> p n v", v=128), [pvr], [vtR[t]])
            st = {}

            def prep(d, t, ui, s=s, Wf=Wf, h=h):
                lb_ap = lv[:, 0, d, h:h + 1]
                omlb_ap = lv[:, 1, d, h:h + 1]
                dec_, decR_ = decs[ui % 2]
                tot_, totR_ = tots[ui % 2]
                sl = slice(t * 512, (t + 1) * 512)
                pz, pzr = ptr()
                for c in range(8):
                    MM(pz[:, :], Wf[d][:, c, :], hT[:, c, sl], c == 0, c == 7, [WR[s][1 + d], SLOT[s], hR[c][t]], [pzr])
                f_, fR = A.tmp.next()
                ACT(f_, pz[:, :], AF.Exp, [pzr], [fR], scale=-1.0)
                yield
                ACT(f_, f_, AF.Ln, [fR, constR], [fR], bias=one_ap)
                yield
                ACT(f_, f_, AF.Exp, [fR], [fR], scale=-1.0)
                yield
                TS(f_, f_, omlb_ap, lb_ap, ALU.mult, ALU.add, [fR, constR], [fR])
                yield
                lf, lfR = A.tmp.next()
                ACT(lf, f_, AF.Ln, [fR], [lfR])
                TS(f_, f_, -1.0, 1.0, ALU.mult, ALU.add, [fR], [fR])
                yield
                cum, cumR = A.tmp.next()
                P.add("dve", lambda e, cum=cum, lf=lf: e.tensor_tensor_scan(cum, resetm[:, :], lf, 0.0, op0=ALU.mult, op1=ALU.add),
                      [lfR, constR], [cumR])
                cum3 = cum.rearrange("p (n c) -> p n c", c=64)
                CP(tot_[:, 0:8], cum3[:, :, 63], [cumR], [totR_], eng="dve")
                yield
                if d == 1:
                    TT(cum, lf, cum, ALU.subtract, [lfR, cumR], [cumR])
                    TT(cum3, cum3, tot_[:, 0:8].unsqueeze(2).to_broadcast([128, 8, 64]), ALU.add, [cumR, totR_], [cumR])
                    yield
                ACT(dec_[:, 0:8], tot_[:, 0:8], AF.Exp, [totR_], [decR_])
                e1, e1R = A.tmp.next()
                ACT(e1, cum, AF.Exp, [cumR], [e1R])
                STT(lf.rearrange("p (n c) -> p n c", c=64), cum3, -1.0, tot_[:, 0:8].unsqueeze(2).to_broadcast([128, 8, 64]),
                    ALU.mult, ALU.add, [cumR, totR_, lfR], [lfR])
                yield
                ACT(lf, lf, AF.Exp, [lfR], [lfR])
                qd, qdR = A.qd.next()
                TT(qd, qTb[:, sl], e1, ALU.mult, [qR_[t], e1R], [qdR])
                yield
                e2, e2R = A.tmp.next()
                ACT(e2, cum, AF.Exp, [cumR], [e2R], scale=-1.0)
                TT(lf, lf, f_, ALU.mult, [lfR, fR], [lfR])
                yield
                ki, kiR = A.ki.next()
                TT(ki, f_, e2, ALU.mult, [fR, e2R], [kiR])
                (koT, koT128), koTR = A.koT.next()
                for half in range(2):
                    pk, pkr = ptr()
                    for n4 in range(4):
                        n = half * 4 + n4
                        TR(pk[0:64, n4 * 128:(n4 + 1) * 128], lf[:, n * 64:(n + 1) * 64], ident32[:, :], [lfR, constR], [pkr])
                    EVAC(koT[:, half * 4:half * 4 + 4, :], pk[0:64, :].rearrange("p (n k) -> p n k", k=128), [pkr], [koTR])
                    yield
                pa_, par = ptr()
                for n in range(8):
                    cs = slice(n * 64, (n + 1) * 64)
                    MM(pa_[0:64, cs], ki[:, cs], qd[:, cs], True, True, [kiR, qdR], [par])
                (attm, attm128), attmR = A.att.next()
                TT(attm.rearrange("p (n c) -> p n c", c=64), pa_[0:64, :].rearrange("p (n c) -> p n c", c=64),
                   maskhf[:, d * 64:(d + 1) * 64].unsqueeze(1).to_broadcast([64, 8, 64]), ALU.mult, [par, constR], [attmR])
                yield
                pb = (2, 3) if ui % 2 == 0 else (6, 7)
                pds = [(ps[pb[0]], psR[pb[0]]), (ps[pb[1]], psR[pb[1]])]
                for n in range(8):
                    ng = t * 8 + n
                    MM(pds[n // 4][0][:, (n % 4) * 128:(n % 4 + 1) * 128], koT128[:, n, :], vtok128[:, ng, :], True, True,
                       [koTR, vtR[t], zR], [pds[n // 4][1]])
                st[ui] = (qd, qdR, attm128, attmR, pds, dec_, decR_)
                yield

            def chain(d, t, ui, h=h):
                qd, qdR, attm128, attmR, pds, dec_, decR_ = st.pop(ui)
                sl = slice(t * 512, (t + 1) * 512)
                first = (t == 0) if d == 0 else (t == NT - 1)
                if first:
                    s32, s32R = A.s32.next()
                    sbf, sbfR = A.sbf.next()
                    if S:
                        DMA("sp", s32, sd_d[d, h], ("s32", A.s32.i % 3), [], [s32R])
                    else:
                        MEMSET(s32, 0.0, [], [s32R])
                    CP(sbf, s32, [s32R], [sbfR], eng="dve")
                else:
                    s32, s32R, sbf, sbfR = cur["s"]
                po, por = ps[4 + (ui % 2)], psR[4 + (ui % 2)]
                chunks = list(range(8)) if d == 0 else list(range(7, -1, -1))
                for n in chunks:
                    ng = t * 8 + n
                    cs = slice(n * 64, (n + 1) * 64)
                    if not S and ((d == 0 and ng % 4 == 0) or (d == 1 and ng % 4 == 3)) and not (ng == (0 if d == 0 else NCH - 1)):
                        s32, s32R = A.s32.next()
                        sbf, sbfR = A.sbf.next()
                        MEMSET(s32, 0.0, [], [s32R])
                        MEMSET(sbf, 0.0, [], [sbfR])
                    MM(po[:, cs], vtok128[:, ng, :], attm128[:, cs], True, False, [vtR[t], attmR, zR], [por])
                    MM(po[:, cs], sbf, qd[:, cs], False, True, [sbfR, qdR], [por])
                    n32, n32R = A.s32.next()
                    pdn, pdnR = pds[n // 4]
                    STT(n32, s32, dec_[:, n:n + 1], pdn[:, (n % 4) * 128:(n % 4 + 1) * 128], ALU.mult, ALU.add, [s32R, decR_, pdnR], [n32R])
                    s32, s32R = n32, n32R
                    if not S and ((d == 0 and ng % 4 == 3) or (d == 1 and ng % 4 == 0)):
                        DMA("sp", nsd_d[ng // 4, d, h], s32, ("nsd", A.s32.i % 3), [s32R], [], is_out=True)
                    else:
                        sbf, sbfR = A.sbf.next()
                        CP(sbf, s32, [s32R], [sbfR], eng="dve")
                    yield
                TT(oacc[:, sl], oacc[:, sl], po[:, :], ALU.add, [oaR[t], por], [oaR[t]])
                cur["s"] = (s32, s32R, sbf, sbfR)
                yield

            cur = {}
            units = [(0, t) for t in range(NT)] + [(1, t) for t in range(NT - 1, -1, -1)]
            prev = None
            for ui, (d, t) in enumerate(units):
                pg = prep(d, t, ui)
                if prev is None:
                    for _ in pg:
                        pass
                else:
                    a_done = b_done = False
                    while not (a_done and b_done):
                        if not a_done:
                            a_done = next(pg, "END") == "END"
                        if not b_done:
                            b_done = next(prev, "END") == "END"
                prev = chain(d, t, ui)
            for _ in prev:
                pass
            for t in range(NT):
                sl = slice(t * 512, (t + 1) * 512)
                pg, pgr = pw()
                for c in range(8):
                    MM(pg[:, :], Wgt[:, c, :], hT[:, c, sl], c == 0, c == 7, [WR[s][4], SLOT[s], hR[c][t]], [pgr])
                sg, sgr = A.tmp.next()
                ACT(sg, pg[:, :], AF.Silu, [pgr], [sgr])
                sq, sqr = A.sq.next()
                ACT(sq, oacc[:, sl], AF.Square, [oaR[t]], [sqr])
                pm, pmr = pw()
                MM(pm[:, :], mean128[:, :], sq, True, True, [sqr, constR], [pmr])
                t1, t1r = A.tmp.next()
                ACT(t1, pm[:, :], AF.Ln, [pmr, constR], [t1r], bias=eps_ap)
                ACT(t1, t1, AF.Exp, [t1r], [t1r], scale=-0.5)
                STT(t1, oacc[:, sl], gnd, t1, ALU.mult, ALU.mult, [oaR[t], t1r, constR], [t1r])
                TT(oT[:, h, sl], t1, sg, ALU.mult, [t1r, sgr], [oR[h][t]])

    def skipw(key):
        return isinstance(key, tuple) and key[0] == "dma" and isinstance(key[1], tuple) and key[1][0] == "w"

    prologue()
    for ph in phases:
        S = ph == "S"
        T = 2048 if S else 1024
        NT = T // 512
        r = 1 if S else 0
        P.fence(skip=skipw)
        P.new_epoch()
        load_x(xs_d if S else xp_d, T)
        for li in range(nlayers):
            L = DL
            if li == 0:
                norm_mod(L, NT, li, 0, r)
            P.fence(skip=skipw)
            kind = li % 4
            if kind == 0:
                mixer_A(ph)
                wo_d = wo_a_d
            elif kind == 1:
                mixer_G(ph, "B")
                wo_d = wo_b_d
            elif kind == 2:
                mixer_G(ph, "C")
                wo_d = wo_c_d
            else:
                mixer_D(ph)
                wo_d = wo_d_d
            P.fence(skip=skipw)
            wo_proj(NT, li, r, wo_d, after=lambda t, li=li: norm_tile(L, t, li, 1, r))
            side = None
            if ph == phases[0] and li + 1 < nlayers:
                side = ada_steps(li + 1, [(scrF[:, 3072 + i * 1024:4096 + i * 1024].rearrange("p (c n) -> p c n", c=8), Res()) for i in range(2)],
                                 128, ps[7], psR[7])
            nxt = (lambda t, li=li: norm_tile(L, t, li + 1, 0, r)) if li + 1 < nlayers else None
            ffn(L, NT, li, r, side, after=nxt)
        store_y(ys_d if S else yp_d, T)

    semstack = ExitStack()
    with semstack:
        run = P.emit(lambda name: semstack.enter_context(nc.semaphore(name)))
        with nc.allow_non_contiguous_dma(reason="small strided loads"):
            with nc.Block() as block:
                @block.tensor
                def _(e):
                    run("pe", e)

                @block.scalar
                def _(e):
                    run("act", e)

                @block.vector
                def _(e):
                    run("dve", e)

                @block.gpsimd
                def _(e):
                    run("pool", e)

                @block.sync
                def _(e):
                    run("sp", e)
    es.close()
    return nc


def _consts():
    f = np.float32
    ident = np.eye(128, dtype=f)
    prot = np.zeros((128, 128), f)
    for p in range(128):
        if (p % 32) < 16:
            prot[p + 16, p] = -1.0
        else:
            prot[p - 16, p] = 1.0
    maskb = np.zeros((6, 128, 512), f)
    k = np.arange(128)[:, None]
    q = np.arange(512)[None, :]
    for o in range(-1, 5):
        maskb[o + 1] = (np.abs(q - 128 * o - k) <= 128).astype(f)
    s = np.arange(64)[:, None]
    c = np.arange(64)[None, :]
    maskh = np.stack([(c >= s).astype(f), (c <= s).astype(f)], 0)
    reset = np.ones((128, 512), f)
    reset[:, ::64] = 0.0
    tpos = np.arange(2048)
    row = (tpos // 64).astype(np.float64)
    col = (tpos % 64).astype(np.float64)
    inv_freq = 10000.0 ** (-np.arange(0, 32, 2, dtype=np.float64) / 32.0)
    ang = np.zeros((64, 2048), np.float64)
    for d in range(64):
        a = d // 32
        fi = d % 16
        ang[d] = (row if a == 0 else col) * inv_freq[fi]
    ang32 = np.zeros((64, 2048), np.float32)
    invf32 = (np.float32(10000.0) ** (-np.arange(0, 32, 2, dtype=np.float32) / np.float32(32.0))).astype(np.float32)
    for d in range(64):
        a = d // 32
        fi = d % 16
        ang32[d] = ((row if a == 0 else col).astype(np.float32) * invf32[fi]).astype(np.float32)
    cos = np.cos(ang32.astype(np.float64)).astype(f)
    sin = np.sin(ang32.astype(np.float64)).astype(f)
    cos = np.concatenate([cos, cos], 0)
    sin = np.concatenate([sin, sin], 0)
    return dict(c_ident=ident, c_prot=prot, c_maskb=maskb, c_maskh=maskh, c_reset=reset, c_cos=cos, c_sin=sin)


_NC_CACHE = {}


def _in_maps(inp, cores=range(8)):
    f = np.float32
    A = lambda x: np.ascontiguousarray(np.asarray(x, dtype=f))
    consts = _consts()
    small = np.zeros((16, 128), f)

    def dup(v):
        v = np.asarray(v, f).reshape(-1)
        return np.concatenate([v, v]) if v.size == 64 else v

    small[0] = dup(inp["qn_a"][0]); small[1] = dup(inp["kn_a"][0]); small[2] = np.asarray(inp["subln_a"][0], f)
    small[3, :64] = inp["lam_q1_a"][0]; small[4, :64] = inp["lam_k1_a"][0]
    small[5, :64] = inp["lam_q2_a"][0]; small[6, :64] = inp["lam_k2_a"][0]
    small[7] = dup(inp["qn_b"][0]); small[8] = dup(inp["kn_b"][0]); small[9, :16] = inp["sink_b"][0]
    small[10] = dup(inp["qn_c"][0]); small[11] = dup(inp["kn_c"][0]); small[12] = np.asarray(inp["gn_d"][0], f)
    shared = dict(
        w_ada=A(inp["w_ada"]), w_ffn_gate=A(inp["w_ffn_gate"]), w_ffn_up=A(inp["w_ffn_up"]), w_ffn_down=A(inp["w_ffn_down"]),
        w_qkv_a=A(inp["w_qkv_a"][0]), w_o_a=A(inp["w_o_a"][0]), w_qkv_b=A(inp["w_qkv_b"][0]), w_o_b=A(inp["w_o_b"][0]),
        w_qkv_c=A(inp["w_qkv_c"][0]), w_o_c=A(inp["w_o_c"][0]), w_in_d=A(inp["w_in_d"][0]), w_o_d=A(inp["w_o_d"][0]),
        small=small, **consts)
    maps = []
    for core in cores:
        b = core // 2
        vecs = np.zeros((384, 128), f)
        vecs[0:192] = np.asarray(inp["b_ada"], f).reshape(192, 128)
        vecs[192:224] = np.asarray(inp["norm_mix"], f).reshape(32, 128)
        vecs[224:256] = np.asarray(inp["norm_ffn"], f).reshape(32, 128)
        vecs[256:320] = np.asarray(inp["lb_logits_d"], f).reshape(64, 128)
        vecs[320:328] = np.asarray(inp["c_ctx"], f).reshape(8, 128)
        vecs[328:336] = np.asarray(inp["c"][b], f).reshape(8, 128)
        m = dict(shared)
        m.update(
            xs=A(inp["x_sample"][b]), xp=A(np.asarray(inp["x_prompt"][4 * core:4 * core + 4]).reshape(1024, 1024)), vecs=vecs,
            cak=A(np.asarray(inp["cache_a_k"][b, 0]).reshape(512, 1024)), cav=A(np.asarray(inp["cache_a_v"][b, 0]).reshape(512, 1024)),
            cbk=A(np.asarray(inp["cache_b_k"][b, 0]).reshape(512, 256)), cbv=A(np.asarray(inp["cache_b_v"][b, 0]).reshape(512, 256)),
            cck=A(np.asarray(inp["cache_c_k"][b, 0]).reshape(512, 256)), ccv=A(np.asarray(inp["cache_c_v"][b, 0]).reshape(512, 256)),
            sd=A(inp["state_d"][b, 0]))
        maps.append(m)
    return maps


def kernel(**inp):
    if "nc" not in _NC_CACHE:
        _NC_CACHE["nc"] = build()
    nc = _NC_CACHE["nc"]
    maps = _in_maps(inp)
    res = run_bass_kernel_spmd(nc, maps, core_ids=list(range(8)))
    R = res.results
    f = np.float32
    y_prompt = np.concatenate([R[c]["yp"].reshape(4, 256, 1024) for c in range(8)], 0).astype(f)
    y_sample = np.stack([R[2 * b]["ys"] for b in range(4)], 0).astype(f)
    cat = lambda name, shp: np.concatenate([R[c][name].reshape((4, 1, 256) + shp) for c in range(8)], 0).astype(f)
    new_a_k = cat("nak", (16, 64))
    new_a_v = cat("nav", (8, 128))
    new_b_k = cat("nbk", (4, 64))
    new_b_v = cat("nbv", (4, 64))
    new_c_k = cat("nck", (4, 64))
    new_c_v = cat("ncv", (4, 64))
    new_d = np.concatenate([R[c]["nsd"].reshape(4, 1, 2, 8, 128, 128) for c in range(8)], 0).astype(f)
    return (y_prompt, y_sample, new_a_k, new_a_v, new_b_k, new_b_v, new_c_k, new_c_v, new_d)
```

```python
import math, os
from contextlib import ExitStack
import numpy as np
import concourse.bass as bass
import concourse.mybir as mybir
from concourse.bass_utils import run_bass_kernel_spmd

F32 = mybir.dt.float32
BF16 = mybir.dt.bfloat16
AF = mybir.ActivationFunctionType
ALU = mybir.AluOpType
AX = mybir.AxisListType

ENGS = ("pe", "act", "dve", "pool", "sp")
EPS = 1e-6


class Res:
    __slots__ = ("name", "w", "r", "excl")

    def __init__(self, name="", excl=False):
        self.name = name
        self.w = None
        self.r = {}
        self.excl = excl


class Op:
    __slots__ = ("eng", "key", "idx", "fn", "waits", "needs_inc", "inc_val", "is_dma", "epoch")

    def __init__(self, eng, key, idx, fn, is_dma=False):
        self.eng = eng
        self.key = key
        self.idx = idx
        self.fn = fn
        self.waits = []
        self.needs_inc = False
        self.inc_val = None
        self.is_dma = is_dma
        self.epoch = 0


class Prog:
    def __init__(self, nc):
        self.nc = nc
        self.ops = {e: [] for e in ENGS}
        self.streams = {e: [] for e in ENGS}
        self.seen = {e: {} for e in ENGS}
        self.epoch = 0
        self.out_dma_keys = set()
        self.pending_fence = {e: [] for e in ENGS}

    def new_epoch(self):
        self.epoch += 1

    def fence(self, engines=("pe", "act", "dve", "sp"), skip=lambda key: False):
        lasts = [lst[-1] for key, lst in self.streams.items() if lst and not skip(key)]
        for e in engines:
            self.pending_fence[e] = list(lasts)

    def add(self, eng, fn, reads=(), writes=(), dma_key=None, is_out=False):
        is_dma = dma_key is not None
        key = ("dma", dma_key) if is_dma else eng
        if key not in self.streams:
            self.streams[key] = []
        op = Op(eng, key, len(self.streams[key]), fn, is_dma)
        op.epoch = 0 if is_dma else self.epoch
        deps = {}

        def put(d):
            if d is None:
                return
            cur = deps.get(d.key)
            if cur is None or cur.idx < d.idx:
                deps[d.key] = d

        for r in reads:
            put(r.w)
            if r.excl:
                for d in r.r.values():
                    if d.key != key:
                        put(d)
        for w in writes:
            put(w.w)
            for d in w.r.values():
                put(d)
        fence_keys = set()
        if self.pending_fence[eng]:
            for d in self.pending_fence[eng]:
                put(d)
                fence_keys.add(d.key)
            self.pending_fence[eng] = []
        seen = self.seen[eng]
        for k, d in deps.items():
            if k == "pe" and eng == "pe" and not is_dma and k not in fence_keys:
                continue
            if k == eng and not is_dma and d is op:
                continue
            if seen.get(k, -1) >= d.idx:
                continue
            seen[k] = d.idx
            d.needs_inc = True
            op.waits.append(d)
        self.streams[key].append(op)
        self.ops[eng].append(op)
        for r in reads:
            r.r[key] = op
        for w in writes:
            w.w = op
            w.r = {}
        if is_out:
            self.out_dma_keys.add(key)
        return op

    def emit(self, sem_alloc):
        sems = {}
        for key, lst in self.streams.items():
            cnt = {}
            for op in lst:
                if op.is_dma:
                    op.needs_inc = True
                if op.needs_inc:
                    sk = (key, op.epoch)
                    cnt[sk] = cnt.get(sk, 0) + (16 if op.is_dma else 1)
                    op.inc_val = cnt[sk]
                    if sk not in sems:
                        sems[sk] = sem_alloc("s%d" % len(sems))
        self.sems = sems
        final_out = []
        for key in self.out_dma_keys:
            last = self.streams[key][-1]
            final_out.append((sems[(key, last.epoch)], last.inc_val))

        def run(engname, eng):
            for op in self.ops[engname]:
                for d in op.waits:
                    eng.wait_ge(sems[(d.key, d.epoch)], d.inc_val)
                ins = op.fn(eng)
                if op.needs_inc:
                    ins.then_inc(sems[(op.key, op.epoch)], 16 if op.is_dma else 1)
            if engname == "sp":
                for s, v in final_out:
                    eng.wait_ge(s, v)

        return run


class Rot:
    def __init__(self, items):
        self.items = items
        self.i = 0

    def next(self):
        it = self.items[self.i % len(self.items)]
        self.i += 1
        return it


def build(nlayers=4, phases="SP"):
    nc = bass.Bass("TRN2", target_bir_lowering=False)
    P = Prog(nc)
    es = ExitStack()

    def din(name, shape):
        return nc.dram_tensor(name, list(shape), F32, kind="ExternalInput").ap()

    def dout(name, shape):
        return nc.dram_tensor(name, list(shape), F32, kind="ExternalOutput").ap()

    xs_d = din("xs", [2048, 1024])
    xp_d = din("xp", [1024, 1024])
    vecs_d = din("vecs", [384, 128])
    cak_d = din("cak", [512, 1024])
    cav_d = din("cav", [512, 1024])
    cbk_d = din("cbk", [512, 256])
    cbv_d = din("cbv", [512, 256])
    cck_d = din("cck", [512, 256])
    ccv_d = din("ccv", [512, 256])
    sd_d = din("sd", [2, 8, 128, 128])
    w_ada_d = din("w_ada", [4, 1024, 6144])
    wg_d = din("w_ffn_gate", [4, 1024, 2816])
    wu_d = din("w_ffn_up", [4, 1024, 2816])
    wd_d = din("w_ffn_down", [4, 2816, 1024])
    wqkv_a_d = din("w_qkv_a", [1024, 3072])
    wo_a_d = din("w_o_a", [1024, 1024])
    wqkv_b_d = din("w_qkv_b", [1024, 1536])
    wo_b_d = din("w_o_b", [1024, 1024])
    wqkv_c_d = din("w_qkv_c", [1024, 1536])
    wo_c_d = din("w_o_c", [1024, 1024])
    win_d_d = din("w_in_d", [1024, 5120])
    wo_d_d = din("w_o_d", [1024, 1024])
    small_d = din("small", [16, 128])
    c_ident_d = din("c_ident", [128, 128])
    c_prot_d = din("c_prot", [128, 128])
    c_maskb_d = din("c_maskb", [6, 128, 512])
    c_maskh_d = din("c_maskh", [2, 64, 64])
    c_reset_d = din("c_reset", [128, 512])
    c_cos_d = din("c_cos", [128, 2048])
    c_sin_d = din("c_sin", [128, 2048])

    ys_d = dout("ys", [2048, 1024])
    yp_d = dout("yp", [1024, 1024])
    nak_d = dout("nak", [1024, 1024])
    nav_d = dout("nav", [1024, 1024])
    nbk_d = dout("nbk", [1024, 256])
    nbv_d = dout("nbv", [1024, 256])
    nck_d = dout("nck", [1024, 256])
    ncv_d = dout("ncv", [1024, 256])
    nsd_d = dout("nsd", [4, 2, 8, 128, 128])

    def sb(name, shape, dt):
        return es.enter_context(nc.sbuf_tensor(name, shape, dt))

    xT_t = sb("xT", [128, 8 * 2048], F32)
    hT_t = sb("hT", [128, 8 * 2048], BF16)
    oT_t = sb("oT", [128, 8 * 2048], BF16)
    xT = xT_t[:, :].rearrange("p (c t) -> p c t", c=8)
    hT = hT_t[:, :].rearrange("p (c t) -> p c t", c=8)
    oT = oT_t[:, :].rearrange("p (c t) -> p c t", c=8)
    Wt = [sb("W%d" % i, [128, 6144], BF16) for i in range(2)]
    scrB = sb("scrB", [128, 13312], BF16)
    scrF = sb("scrF", [128, 5120], F32)
    ident32 = sb("ident32", [128, 128], F32)
    prot32 = sb("prot32", [128, 128], F32)
    onesb = sb("onesb", [128, 128], BF16)
    mean1024 = sb("mean1024", [128, 128], BF16)
    blk64 = sb("blk64", [128, 128], BF16)
    mean128 = sb("mean128", [128, 128], BF16)
    vT = sb("vT", [128, 384], F32)
    modt = sb("modt", [128, 4 * 48 * 2], F32)
    mod = modt[:, :].rearrange("p (l j r) -> p l j r", l=4, j=48)
    dert = sb("dert", [128, 4 * 2 * 8 * 2], F32)
    der = dert[:, :].rearrange("p (l k c r) -> p l k c r", l=4, k=2, c=8)
    smallT = sb("smallT", [128, 16], F32)
    misc = sb("misc", [128, 64], F32)
    esink = sb("esink", [128, 16], F32)
    lbt = sb("lbt", [128, 64], F32)
    lbv = sb("lbv", [128, 32], F32)
    condS = sb("condS", [128, 16], F32)
    maskhf = sb("maskhf", [64, 128], F32)
    resetm = sb("resetm", [128, 512], F32)
    ps = [es.enter_context(nc.psum_tensor("ps%d" % i, [128, 512], F32)) for i in range(8)]
    psR = [Res("ps%d" % i, excl=True) for i in range(8)]

    constR = Res("const")
    modR = [Res("mod%d" % i) for i in range(4)]
    xR = [[Res() for t in range(4)] for c in range(8)]
    hR = [[Res() for t in range(4)] for c in range(8)]
    oR = [[Res() for t in range(4)] for c in range(8)]
    WR = [[Res() for i in range(6)] for s in range(2)]
    SLOT = [Res(), Res()]

    eps_ap = misc[:, 0:1]
    neglam_ap = misc[:, 1:2]
    subg_ap = misc[:, 2:3]
    one_ap = misc[:, 3:4]

    def MM(out, lhsT, rhs, start, stop, R, Wr):
        P.add("pe", lambda e: e.matmul(out, lhsT, rhs, start=start, stop=stop), R, Wr)

    def TR(out, in_, ident, R, Wr):
        P.add("pe", lambda e: e.transpose(out, in_, ident), R, Wr)

    def ACT(out, in_, func, R, Wr, bias=None, scale=None):
        kw = {}
        if bias is not None:
            kw["bias"] = bias
        if scale is not None:
            kw["scale"] = scale
        P.add("act", lambda e: e.activation(out, in_, func, **kw), R, Wr)

    def TT(out, in0, in1, op, R, Wr, eng="dve"):
        P.add(eng, lambda e: e.tensor_tensor(out, in0, in1, op=op), R, Wr)

    def STT(out, in0, scalar, in1, op0, op1, R, Wr, eng="dve"):
        P.add(eng, lambda e: e.scalar_tensor_tensor(out, in0, scalar, in1, op0=op0, op1=op1), R, Wr)

    def TS(out, in0, s1, s2, op0, op1, R, Wr, eng="dve"):
        if s2 is None:
            P.add(eng, lambda e: e.tensor_scalar(out, in0, s1, None, op0=op0), R, Wr)
        else:
            P.add(eng, lambda e: e.tensor_scalar(out, in0, s1, s2, op0=op0, op1=op1), R, Wr)

    def CP(out, in_, R, Wr, eng="dve"):
        if eng == "act":
            P.add("act", lambda e: e.activation(out, in_, AF.Copy), R, Wr)
        else:
            P.add(eng, lambda e: e.tensor_copy(out, in_), R, Wr)

    def RECIP(out, in_, R, Wr):
        P.add("dve", lambda e: e.reciprocal(out, in_), R, Wr)

    def MEMSET(out, val, R, Wr, eng="dve"):
        P.add(eng, lambda e: e.memset(out, val), R, Wr)

    def DMA(q, out, in_, key, R, Wr, is_out=False):
        P.add(q, lambda e: e.dma_start(out=out, in_=in_), R, Wr, dma_key=key, is_out=is_out)

    pwi = [0]

    def pw():
        i = pwi[0] % 4
        pwi[0] += 1
        return ps[i], psR[i]

    evi = [0]

    def EVAC(out, in_, R, Wr):
        evi[0] += 1
        CP(out, in_, R, Wr, eng="act" if evi[0] % 2 else "dve")

    wjob = [0]

    def wload(parts):
        s = wjob[0] % 2
        wjob[0] += 1
        for i, (dstf, src) in enumerate(parts):
            wr = [WR[s][i]] + ([SLOT[s]] if i == 0 else [])
            DMA("pool", dstf(Wt[s]), src, ("w", s, i), [], wr)
        return s

    def wview(s_t, off, ncol):
        return s_t[:, off:off + 8 * ncol].rearrange("p (c n) -> p c n", c=8)

    def wsrc(w2d, col0, ncol):
        return w2d.rearrange("(c p) n -> p c n", p=128)[:, :, col0:col0 + ncol]

    def prologue():
        lamt = scrF[:, 384:640]
        DMA("sp", ident32[:, :], c_ident_d, "c0", [], [constR])
        DMA("sp", prot32[:, :], c_prot_d, "c1", [], [constR])
        DMA("sp", resetm[:, :], c_reset_d, "c2", [], [constR])
        DMA("sp", maskhf[:, :].rearrange("p (d c) -> p d c", d=2), c_maskh_d.rearrange("d s c -> s d c"), "c3", [], [constR])
        DMA("sp", smallT[:, :], small_d.rearrange("r p -> p r"), "c4", [], [constR])
        DMA("sp", esink[:, :], small_d[9, 0:16].partition_broadcast(128), "c5", [], [constR])
        for i in range(4):
            DMA("sp", lamt[:, i * 64:(i + 1) * 64], small_d[3 + i, 0:64].partition_broadcast(128), "c6", [], [constR])
        MEMSET(onesb[:, :], 1.0, [], [constR])
        MEMSET(mean1024[:, :], 1.0 / 1024, [], [constR])
        MEMSET(mean128[:, :], 1.0 / 128, [], [constR])
        MEMSET(blk64[:, :], 0.0, [], [constR])
        MEMSET(blk64[0:64, 0:64], 1.0 / 64, [], [constR])
        MEMSET(blk64[64:128, 64:128], 1.0 / 64, [], [constR])
        MEMSET(misc[:, :], 0.0, [], [constR])
        MEMSET(misc[:, 0:1], EPS, [], [constR])
        MEMSET(misc[:, 3:4], 1.0, [], [constR])
        stg = scrF[:, 0:384].rearrange("p (k f) -> p k f", k=3)
        stgR = Res()
        DMA("sp", stg, vecs_d.rearrange("(k p) f -> p k f", p=128), "c7", [], [stgR])
        for k in range(3):
            TR(ps[0][:, k * 128:(k + 1) * 128], stg[:, k, :], ident32[:, :], [stgR, constR], [psR[0]])
        CP(vT[:, :], ps[0][:, 0:384], [psR[0]], [constR])
        cS = condS[:, :].rearrange("p (c r) -> p c r", r=2)
        for r in range(2):
            ACT(cS[:, :, r], vT[:, 320 + 8 * r:328 + 8 * r], AF.Silu, [constR], [constR])
        ACT(esink[:, :], esink[:, :], AF.Exp, [constR], [constR])
        lam_init = 0.8 - 0.6 * math.exp(-0.3 * 0)
        TT(lamt[:, 0:64], lamt[:, 0:64], lamt[:, 64:128], ALU.mult, [constR], [constR])
        TT(lamt[:, 128:192], lamt[:, 128:192], lamt[:, 192:256], ALU.mult, [constR], [constR])
        P.add("dve", lambda e: e.reduce_sum(misc[:, 8:9], lamt[:, 0:64], axis=AX.X), [constR], [constR])
        P.add("dve", lambda e: e.reduce_sum(misc[:, 9:10], lamt[:, 128:192], axis=AX.X), [constR], [constR])
        ACT(misc[:, 8:10], misc[:, 8:10], AF.Exp, [constR], [constR])
        TT(misc[:, 10:11], misc[:, 9:10], misc[:, 8:9], ALU.subtract, [constR], [constR])
        TS(misc[:, 1:2], misc[:, 10:11], -lam_init, None, ALU.add, None, [constR], [constR])
        TS(misc[:, 2:3], smallT[:, 2:3], 1.0 - lam_init, None, ALU.mult, None, [constR], [constR])
        ACT(lbt[:, :], vT[:, 256:320], AF.Exp, [constR], [constR])
        lb4 = lbt[:, :].rearrange("p (d l c) -> p d l c", d=2, l=4)
        lv = lbv[:, :].rearrange("p (k d c) -> p k d c", k=2, d=2)
        li_d = 3
        for d in range(2):
            TT(lv[:, 0, d, :], lb4[:, d, 1, :], lb4[:, d, 2, :], ALU.add, [constR], [constR])
            TT(lv[:, 0, d, :], lv[:, 0, d, :], lb4[:, d, 3, :], ALU.add, [constR], [constR])
            TT(lv[:, 1, d, :], lv[:, 0, d, :], lb4[:, d, 0, :], ALU.add, [constR], [constR])
            RECIP(lv[:, 1, d, :], lv[:, 1, d, :], [constR], [constR])
            TT(lv[:, 0, d, :], lv[:, 0, d, :], lv[:, 1, d, :], ALU.mult, [constR], [constR])
            TS(lv[:, 1, d, :], lv[:, 0, d, :], -1.0, 1.0, ALU.mult, ALU.add, [constR], [constR])
        for step in ada_steps(0, [(scrF[:, 1024 + i * 2048:1024 + (i + 1) * 2048].rearrange("p (c n) -> p c n", c=8), Res()) for i in range(2)], 256, ps[4], psR[4]):
            pass

    def ada_steps(li, slots, ncol, acc, accR):
        cS = condS[:, :].rearrange("p (c r) -> p c r", r=2)
        rot = Rot(slots)
        ngrp = 6144 // ncol
        for jg in range(ngrp):
            wsl, wslR = rot.next()
            DMA("sp", wsl, wsrc(w_ada_d[li], jg * ncol, ncol), ("ada", rot.i % len(slots)), [], [wslR])
            for jj in range(ncol // 128):
                j = jg * (ncol // 128) + jj
                for kc in range(8):
                    MM(acc[:, 2 * j:2 * j + 2], wsl[:, kc, jj * 128:(jj + 1) * 128], cS[:, kc, :], kc == 0, kc == 7,
                       [wslR, constR], [accR])
            yield
        TT(mod[:, li, :, :], acc[:, 0:96].rearrange("p (j r) -> p j r", r=2),
           vT[:, li * 48:(li + 1) * 48].unsqueeze(2).to_broadcast([128, 48, 2]), ALU.add, [accR, constR], [modR[li]])
        STT(der[:, li, 0, :, :], mod[:, li, 8:16, :], 1.0, vT[:, 192 + li * 8:200 + li * 8].unsqueeze(2).to_broadcast([128, 8, 2]),
            ALU.add, ALU.mult, [constR, modR[li]], [modR[li]])
        STT(der[:, li, 1, :, :], mod[:, li, 32:40, :], 1.0, vT[:, 224 + li * 8:232 + li * 8].unsqueeze(2).to_broadcast([128, 8, 2]),
            ALU.add, ALU.mult, [constR, modR[li]], [modR[li]])
        yield

    def modv(li, m, c, r):
        return mod[:, li, m * 8 + c, r:r + 1]

    class Dense:
        pass

    def dense_layout():
        L = Dense()
        L.sq = Rot([(scrB[:, i * 512:(i + 1) * 512], Res()) for i in range(2)])
        L.a = Rot([(scrB[:, 1024 + i * 1024:1024 + (i + 1) * 1024].rearrange("p (j n) -> p j n", j=2), Res()) for i in range(2)])
        L.sd = Rot([(scrF[:, i * 512:(i + 1) * 512], Res()) for i in range(2)])
        L.tmp = Rot([(scrF[:, 1024 + i * 512:1024 + (i + 1) * 512], Res()) for i in range(2)])
        L.s = Rot([(scrF[:, 2048 + i * 512:2048 + (i + 1) * 512], Res()) for i in range(2)])
        L.stg = Rot([(scrF[:, 3072 + i * 1024:3072 + (i + 1) * 1024], Res()) for i in range(2)])
        return L

    DL = dense_layout()

    def load_x(x_d, T):
        L = DL
        for t in range(T // 512):
            for tb in range(4):
                stg, stgR = L.stg.next()
                r0 = t * 512 + tb * 128
                DMA("sp", stg, x_d[r0:r0 + 128, :], ("xin", L.stg.i % 2), [], [stgR])
                for c in range(8):
                    TR(ps[c][:, tb * 128:(tb + 1) * 128], stg[:, c * 128:(c + 1) * 128], ident32[:, :], [stgR, constR], [psR[c]])
            for c in range(8):
                EVAC(xT[:, c, t * 512:(t + 1) * 512], ps[c][:, :], [psR[c]], [xR[c][t]])

    def store_y(y_d, T):
        L = DL
        for tb in range(T // 128):
            t = tb // 4
            stg, stgR = L.stg.next()
            for c in range(8):
                b = 4 + (tb % 2) * 2 + c // 4
                TR(ps[b][:, (c % 4) * 128:(c % 4 + 1) * 128], xT[:, c, tb * 128:(tb + 1) * 128], ident32[:, :],
                   [xR[c][t], constR], [psR[b]])
            for hf in range(2):
                b = 4 + (tb % 2) * 2 + hf
                EVAC(stg[:, hf * 512:(hf + 1) * 512], ps[b][:, :], [psR[b]], [stgR])
            DMA("sp", y_d[tb * 128:(tb + 1) * 128, :], stg, ("yout", L.stg.i % 2), [stgR], [], is_out=True)

    def norm_mod(L, NT, li, which, r):
        for t in range(NT):
            norm_tile(L, t, li, which, r)

    def norm_tile(L, t, li, which, r):
        if True:
            sl = slice(t * 512, (t + 1) * 512)
            pt, pr = pw()
            for c in range(8):
                sq, sqr = L.sq.next()
                ACT(sq, xT[:, c, sl], AF.Square, [xR[c][t]], [sqr])
                MM(pt[:, :], mean1024[:, :], sq, c == 0, c == 7, [sqr, constR], [pr])
            sd, sdr = L.sd.next()
            ACT(sd, pt[:, :], AF.Ln, [pr, constR], [sdr], bias=eps_ap)
            ACT(sd, sd, AF.Exp, [sdr], [sdr], scale=-0.5)
            for c in range(8):
                tmp, tr = L.tmp.next()
                TT(tmp, xT[:, c, sl], sd, ALU.mult, [xR[c][t], sdr], [tr])
                ACT(hT[:, c, sl], tmp, AF.Identity, [tr, modR[li]], [hR[c][t]],
                    bias=modv(li, which * 3 + 0, c, r), scale=der[:, li, which, c, r:r + 1])

    def ffn(L, NT, li, r, side=None, after=None):
        fdefer, fflush = make_defer(1)
        fin_after = []
        nb = 3 if side is not None else 4
        for g in range(11):
            s = wload([
                (lambda w: wview(w, 0, 256), wsrc(wg_d[li], g * 256, 256)),
                (lambda w: wview(w, 2048, 256), wsrc(wu_d[li], g * 256, 256)),
                (lambda w: w[:, 4096:6144].rearrange("p (j n) -> p j n", j=2),
                 wd_d[li][g * 256:(g + 1) * 256, :].rearrange("(j p) n -> p j n", p=128)),
            ])
            Wg = wview(Wt[s], 0, 256)
            Wu = wview(Wt[s], 2048, 256)
            Wd = Wt[s][:, 4096:6144].rearrange("p (j n) -> p j n", j=2)
            for t in range(NT):
                sl = slice(t * 512, (t + 1) * 512)
                a, aR = L.a.next()
                for j in range(2):
                    pg, pgr = pw()
                    for c in range(8):
                        MM(pg[:, :], Wg[:, c, j * 128:(j + 1) * 128], hT[:, c, sl], c == 0, c == 7, [WR[s][0], SLOT[s], hR[c][t]], [pgr])
                    pu, pur = pw()
                    for c in range(8):
                        MM(pu[:, :], Wu[:, c, j * 128:(j + 1) * 128], hT[:, c, sl], c == 0, c == 7, [WR[s][1], SLOT[s], hR[c][t]], [pur])
                    sg, sgr = L.s.next()
                    ACT(sg, pg[:, :], AF.Silu, [pgr], [sgr])
                    TT(a[:, j, :], sg, pu[:, :], ALU.mult, [sgr, pur], [aR])

                if side is not None:
                    for _ in range(2):
                        next(side, None)

                def down(s=s, Wd=Wd, a=a, aR=aR, t=t, sl=sl, g=g):
                    if g == 10 and after is not None:
                        fin_after.append(t)
                    for cp in range(8):
                        b = 4 + cp % nb
                        for j in range(2):
                            MM(ps[b][:, :], Wd[:, j, cp * 128:(cp + 1) * 128], a[:, j, :], j == 0, j == 1, [WR[s][2], SLOT[s], aR], [psR[b]])
                        STT(xT[:, cp, sl], ps[b][:, :], modv(li, 5, cp, r), xT[:, cp, sl], ALU.mult, ALU.add,
                            [psR[b], xR[cp][t], modR[li]], [xR[cp][t]])
                    while fin_after:
                        after(fin_after.pop(0))
                fdefer(down)
        fflush()
        if side is not None:
            for _ in side:
                pass

    def wo_proj(NT, li, r, wo_d, after=None):
        for hf in range(2):
            s = wload([(lambda w: wview(w, 0, 512), wsrc(wo_d, hf * 512, 512))])
            Wo = wview(Wt[s], 0, 512)
            for t in range(NT):
                sl = slice(t * 512, (t + 1) * 512)
                for cl in range(4):
                    cp = hf * 4 + cl
                    b = 4 + cl
                    for c in range(8):
                        MM(ps[b][:, :], Wo[:, c, cl * 128:(cl + 1) * 128], oT[:, c, sl], c == 0, c == 7, [WR[s][0], SLOT[s], oR[c][t]], [psR[b]])
                    STT(xT[:, cp, sl], ps[b][:, :], modv(li, 2, cp, r), xT[:, cp, sl], ALU.mult, ALU.add,
                        [psR[b], xR[cp][t], modR[li]], [xR[cp][t]])
                if hf == 1 and after is not None:
                    after(t)

    def qk_proj(A, lhs_fn, lhsR, M, t, gain_ap, rope, dst, dstR):
        sl = slice(t * 512, (t + 1) * 512)
        pt, pr = pw()
        for c in range(8):
            MM(pt[0:M, :], lhs_fn(c), hT[:, c, sl], c == 0, c == 7, lhsR + [hR[c][t]], [pr])
        sq, sqr = A.sq.next()
        ACT(sq[0:M, :], pt[0:M, :], AF.Square, [pr], [sqr])
        pm, pmr = pw()
        MM(pm[0:M, :], blk64[0:M, 0:M], sq[0:M, :], True, True, [sqr, constR], [pmr])
        t1, t1r = A.tmp.next()
        ACT(t1[0:M, :], pm[0:M, :], AF.Ln, [pmr, constR], [t1r], bias=eps_ap[0:M, :])
        ACT(t1[0:M, :], t1[0:M, :], AF.Exp, [t1r], [t1r], scale=-0.5)
        t2, t2r = A.tmp.next()
        STT(t2[0:M, :], pt[0:M, :], gain_ap, t1[0:M, :], ALU.mult, ALU.mult, [pr, t1r, constR], [t2r])
        if rope is not None:
            cos_ap, sin_ap, ropeR = rope
            pq, pqr = pw()
            MM(pq[0:M, :], prot32[0:M, 0:M], t2[0:M, :], True, True, [t2r, constR], [pqr])
            t3, t3r = A.tmp.next()
            TT(t3[0:M, :], t2[0:M, :], cos_ap[0:M, :], ALU.mult, [t2r, ropeR], [t3r])
            TT(t1[0:M, :], pq[0:M, :], sin_ap[0:M, :], ALU.mult, [pqr, ropeR, t1r], [t1r])
            TT(dst, t3[0:M, :], t1[0:M, :], ALU.add, [t3r, t1r], [dstR])
        else:
            CP(dst, t2[0:M, :], [t2r], [dstR], eng="act")
        return t2, t2r

    def qk_chain(A, lane, lhs_fn, lhsR, t, gain_ap, use_rope, dst, dstR, after=None):
        bX, bY = 2 * lane, 2 * lane + 1
        pX, pXr, pY, pYr = ps[bX], psR[bX], ps[bY], psR[bY]
        (t1, t1r), (t2, t2r) = A.ltmp[lane]
        sq, sqr = A.lsq[lane]
        sl = slice(t * 512, (t + 1) * 512)
        if use_rope:
            slot, ropeR = A.lrope[lane]
            cos_ap, sin_ap = slot[:, 0:512], slot[:, 512:1024]
            DMA("sp", cos_ap, c_cos_d[:, t * 512:(t + 1) * 512], ("rope", lane, 0), [], [ropeR])
            DMA("sp", sin_ap, c_sin_d[:, t * 512:(t + 1) * 512], ("rope", lane, 1), [], [ropeR])
        for c in range(8):
            MM(pX[:, :], lhs_fn(c), hT[:, c, sl], c == 0, c == 7, lhsR + [hR[c][t]], [pXr])
        ACT(sq, pX[:, :], AF.Square, [pXr], [sqr])
        yield
        MM(pY[:, :], blk64[:, :], sq, True, True, [sqr, constR], [pYr])
        ACT(t1, pY[:, :], AF.Ln, [pYr, constR], [t1r], bias=eps_ap)
        yield
        ACT(t1, t1, AF.Exp, [t1r], [t1r], scale=-0.5)
        yield
        STT(t2, pX[:, :], gain_ap, t1, ALU.mult, ALU.mult, [pXr, t1r, constR], [t2r])
        yield
        if use_rope:
            MM(pX[:, :], prot32[:, :], t2, True, True, [t2r, constR], [pXr])
            yield
            TT(t1, pX[:, :], sin_ap, ALU.mult, [pXr, ropeR, t1r], [t1r])
            yield
            TT(t2, t2, cos_ap, ALU.mult, [t2r, ropeR], [t2r])
            yield
            TT(dst, t2, t1, ALU.add, [t2r, t1r], [dstR])
        else:
            CP(dst, t2, [t2r], [dstR], eng="act")
            if after is not None:
                after(t2, t2r, bY)
        yield

    def run_lanes(chains, fillers):
        active = [None, None]
        it = iter(chains)
        while True:
            progressed = False
            for lane in range(2):
                if active[lane] is None:
                    nxt = next(it, None)
                    if nxt is not None:
                        active[lane] = nxt(lane)
                if active[lane] is not None:
                    if next(active[lane], "END") == "END":
                        active[lane] = None
                    progressed = True
            if fillers:
                fillers.pop(0)()
                progressed = True
            if not progressed:
                break

    def lane_setup(A, S):
        tmpslots = [(scrF[:, i * 512:(i + 1) * 512], Res()) for i in range(4)]
        A.tmp = Rot(tmpslots)
        A.ltmp = [[tmpslots[0], tmpslots[1]], [tmpslots[2], tmpslots[3]]]
        if S:
            A.lrope = [(scrF[:, 3072 + i * 1024:4096 + i * 1024], Res()) for i in range(2)]

    def load_rope(A, t):
        slot, slotR = A.rope.next()
        cos_ap = slot[:, 0:512]
        sin_ap = slot[:, 512:1024]
        DMA("sp", cos_ap, c_cos_d[:, t * 512:(t + 1) * 512], ("rope", A.rope.i % 2, 0), [], [slotR])
        DMA("sp", sin_ap, c_sin_d[:, t * 512:(t + 1) * 512], ("rope", A.rope.i % 2, 1), [], [slotR])
        return cos_ap, sin_ap, slotR

    def out_tokmajor(A, src, srcR, M, out_view, ncol=None, bank=None):
        pt, pr = pw() if bank is None else (ps[bank], psR[bank])
        for tb in range(4):
            TR(pt[:, tb * M:(tb + 1) * M], src[0:M, tb * 128:(tb + 1) * 128], ident32[0:M, 0:M], [srcR, constR], [pr])
        stg, stgR = A.ostg.next()
        EVAC(stg[:, 0:4 * M], pt[:, 0:4 * M], [pr], [stgR])
        sv = stg[:, 0:4 * M].rearrange("p (tb f) -> p tb f", tb=4)
        if ncol is not None:
            sv = sv[:, :, 0:ncol]
        DMA("pool", out_view, sv, ("okv", A.ostg.i % 4), [stgR], [], is_out=True)

    class Lay:
        pass

    def make_defer(look):
        q = []

        def defer(fn):
            q.append(fn)
            while len(q) > look:
                q.pop(0)()

        def flush():
            while q:
                q.pop(0)()

        return defer, flush

    def mixer_A(ph):
        S = ph == "S"
        T = 2048 if S else 1024
        NT = T // 512
        koff = 512 if S else 0
        voff = 4 if S else 0
        A = Lay()
        kpair = scrB[:, 0:2560]
        qslots = [(scrB[:, 2560 + i * 512:3072 + i * 512], Res()) for i in range(4)]
        Vh = scrB[:, 5120:7680].rearrange("p (k v) -> p k v", v=128)
        kR = [Res() for _ in range(5)]
        vR = [Res() for _ in range(5)]
        A.pt = Rot([(scrB[:, 8704 + i * 512:9216 + i * 512], Res()) for i in range(6)])
        sqslots = [(scrB[:, 11776 + i * 512:12288 + i * 512], Res()) for i in range(2)]
        A.sq = Rot(sqslots)
        A.lsq = sqslots
        lane_setup(A, S)
        if S:
            stgK = scrF[:, 2048:2560].rearrange("p (t f) -> p t f", t=4)
            stgV = scrF[:, 2560:3072].rearrange("p (t f) -> p t f", t=4)
            stgKR, stgVR = Res(), Res()
        else:
            A.ostg = Rot([(scrF[:, 2048 + i * 512:2560 + i * 512], Res()) for i in range(4)])
        qg = smallT[:, 0:1]
        kg = smallT[:, 1:2]
        defer, flush = make_defer(2)
        vbank = [0]
        for h in range(8):
            flush()
            s = wload([
                (lambda w: wview(w, 0, 128), wsrc(wqkv_a_d, h * 128, 128)),
                (lambda w: wview(w, 1024, 128), wsrc(wqkv_a_d, 1024 + h * 128, 128)),
                (lambda w: wview(w, 2048, 128), wsrc(wqkv_a_d, 2048 + h * 128, 128)),
            ])
            Wq = wview(Wt[s], 0, 128)
            Wk = wview(Wt[s], 1024, 128)
            Wv = wview(Wt[s], 2048, 128)
            if S:
                DMA("sp", stgK, cak_d[:, h * 128:(h + 1) * 128].rearrange("(t p) f -> p t f", p=128), "stgK", [], [stgKR])
                pt, pr = ps[7], psR[7]
                for tb in range(4):
                    TR(pt[:, tb * 128:(tb + 1) * 128], stgK[:, tb, :], ident32[:, :], [stgKR, constR], [pr])
                CP(kpair[:, 0:512], pt[:, :], [pr], [kR[0]], eng="act")
                DMA("sp", stgV, cav_d[:, h * 128:(h + 1) * 128].rearrange("(t p) f -> p t f", p=128), "stgV", [], [stgVR])
                CP(Vh[:, 0:4, :], stgV, [stgVR], [vR[0]], eng="dve")
            chains = []
            for t in range(NT):
                c0 = koff + t * 512
                aft = None
                if not S:
                    aft = (lambda kn, knR, bank, t=t, h=h: out_tokmajor(
                        A, kn, knR, 128, nak_d[t * 512:(t + 1) * 512, h * 128:(h + 1) * 128].rearrange("(tb p) f -> p tb f", p=128), bank=bank))
                chains.append(lambda lane, t=t, c0=c0, aft=aft, s=s, Wk=Wk: qk_chain(
                    A, lane, lambda c: Wk[:, c, :], [WR[s][1], SLOT[s]], t, kg, S, kpair[:, c0:c0 + 512], kR[1 + t], after=aft))
            for t in range(NT):
                qp, qR = qslots[t]
                chains.append(lambda lane, t=t, qp=qp, qR=qR, s=s, Wq=Wq: qk_chain(
                    A, lane, lambda c: Wq[:, c, :], [WR[s][0], SLOT[s]], t, qg, S, qp, qR))
            fillers = []
            for t in range(NT):
                vbank[0] += 1
                b = 4 + vbank[0] % 3
                pv_, pvr = ps[b], psR[b]
                for tb in range(4):
                    def vblk(t=t, tb=tb, pv_=pv_, pvr=pvr, s=s, Wv=Wv):
                        for c in range(8):
                            MM(pv_[:, tb * 128:(tb + 1) * 128], hT[:, c, t * 512 + tb * 128:t * 512 + (tb + 1) * 128], Wv[:, c, :],
                               c == 0, c == 7, [WR[s][2], SLOT[s], hR[c][t]], [pvr])
                    fillers.append(vblk)

                def vev(t=t, pv_=pv_, pvr=pvr, h=h):
                    CP(Vh[:, voff + 4 * t:voff + 4 * t + 4, :], pv_[:, :].rearrange("p (k v) -> p k v", v=128), [pvr], [vR[1 + t]], eng="act")
                    if not S:
                        stg, stgR = A.ostg.next()
                        CP(stg, pv_[:, :], [pvr], [stgR], eng="act")
                        DMA("pool", nav_d[t * 512:(t + 1) * 512, h * 128:(h + 1) * 128].rearrange("(tb p) f -> p tb f", p=128),
                            stg.rearrange("p (tb f) -> p tb f", tb=4), ("okv", A.ostg.i % 4), [stgR], [], is_out=True)
                fillers.append(vev)
            run_lanes(chains, fillers)
            for t in range(NT):
                qp, qR = qslots[t]
                if S:
                    segs = [(0, 512, [(kt * 128, kt, kR[0] if kt < 4 else kR[1 + (kt - 4) // 4], vR[0] if kt < 4 else vR[1 + (kt - 4) // 4]) for kt in range(20)])]
                else:
                    segs = []
                    for sq_ in range(2):
                        p0 = t * 512 + sq_ * 256
                        segs.append((sq_ * 256, 256, [(p0 + kt * 128, (p0 // 128) + kt, kR[1 + t], vR[1 + t]) for kt in range(2)]))
                for (qoff, N, ktiles) in segs:
                    nk = len(ktiles)
                    for i, (kc0, vi, kr, vr) in enumerate(ktiles):
                        sts = []
                        for m in range(2):
                            pst, pstr = pw()
                            MM(pst[:, 0:N], kpair[64 * m:64 * m + 64, kc0:kc0 + 128], qp[64 * m:64 * m + 64, qoff:qoff + N], True, True, [kr, qR], [pstr])
                            sts.append((pst, pstr))
                        for m in range(2):
                            pst, pstr = sts[m]
                            Pt, PtR = A.pt.next()
                            ACT(Pt[:, 0:N], pst[:, 0:N], AF.Exp, [pstr], [PtR], scale=0.125)

                            def pv(m=m, vi=vi, vr=vr, Pt=Pt, PtR=PtR, i=i, nk=nk, N=N, qoff=qoff):
                                MM(ps[4 + 2 * m][:, qoff:qoff + N], Vh[:, vi, :], Pt[:, 0:N], i == 0, i == nk - 1, [vr, PtR], [psR[4 + 2 * m]])
                                MM(ps[5 + 2 * m][:, qoff:qoff + N], onesb[:, :], Pt[:, 0:N], i == 0, i == nk - 1, [constR, PtR], [psR[5 + 2 * m]])
                            defer(pv)

                if True:
                    def fin(N=512, h=h, t=t, qoff=0):
                        ta, tar = A.tmp.next()
                        tb_, tbr = A.tmp.next()
                        tc_, tcr = A.tmp.next()
                        ACT(ta[:, 0:N], ps[5][:, 0:N], AF.Ln, [psR[5]], [tar])
                        ACT(ta[:, 0:N], ta[:, 0:N], AF.Exp, [tar], [tar], scale=-1.0)
                        TT(ta[:, 0:N], ps[4][:, 0:N], ta[:, 0:N], ALU.mult, [psR[4], tar], [tar])
                        ACT(tb_[:, 0:N], ps[7][:, 0:N], AF.Ln, [psR[7]], [tbr])
                        ACT(tb_[:, 0:N], tb_[:, 0:N], AF.Exp, [tbr], [tbr], scale=-1.0)
                        TT(tb_[:, 0:N], ps[6][:, 0:N], tb_[:, 0:N], ALU.mult, [psR[6], tbr], [tbr])
                        STT(tc_[:, 0:N], tb_[:, 0:N], neglam_ap, ta[:, 0:N], ALU.mult, ALU.add, [tar, tbr, constR], [tcr])
                        sq, sqr = A.sq.next()
                        ACT(sq[:, 0:N], tc_[:, 0:N], AF.Square, [tcr], [sqr])
                        pm, pmr = pw()
                        MM(pm[:, 0:N], mean128[:, :], sq[:, 0:N], True, True, [sqr, constR], [pmr])
                        ACT(ta[:, 0:N], pm[:, 0:N], AF.Ln, [pmr, constR, tar], [tar], bias=eps_ap)
                        ACT(ta[:, 0:N], ta[:, 0:N], AF.Exp, [tar], [tar], scale=-0.5)
                        STT(oT[:, h, t * 512 + qoff:t * 512 + qoff + N], tc_[:, 0:N], subg_ap, ta[:, 0:N], ALU.mult, ALU.mult,
                            [tcr, tar, constR], [oR[h][t]])
                    defer(fin)
            flush()

    def mixer_G(ph, kind):
        S = ph == "S"
        T = 2048 if S else 1024
        NT = T // 512
        koff = 512 if S else 0
        voff = 4 if S else 0
        isB = kind == "B"
        wqkv_d = wqkv_b_d if isB else wqkv_c_d
        ck_d, cv_d = (cbk_d, cbv_d) if isB else (cck_d, ccv_d)
        nk_d, nv_d = (nbk_d, nbv_d) if isB else (nck_d, ncv_d)
        qg = smallT[:, 7:8] if isB else smallT[:, 10:11]
        kg = smallT[:, 8:9] if isB else smallT[:, 11:12]
        A = Lay()
        kT2 = scrB[:, 0:2560]
        Va = scrB[:, 2560:5120].rearrange("p (k v) -> p k v", v=128)
        kR = [Res() for _ in range(5)]
        vR = [Res() for _ in range(5)]
        qslots = [(scrB[:, 5120 + i * 512:5632 + i * 512], Res()) for i in range(4)]
        A.pt = Rot([(scrB[:, 7168 + i * 512:7680 + i * 512], Res()) for i in range(4)])
        sqslots = [(scrB[:, 9216 + i * 512:9728 + i * 512], Res()) for i in range(2)]
        A.sq = Rot(sqslots)
        A.lsq = sqslots
        maskb = scrB[:, 10240:13312].rearrange("p (o q) -> p o q", o=6)
        maskR = Res()
        lane_setup(A, S)
        if S:
            stgK = scrF[:, 2048:2560].rearrange("p (t f) -> p t f", t=4)
            stgV = scrF[:, 2560:2816].rearrange("p (t f) -> p t f", t=4)
            stgM = scrF[:, 4096:4608]
            stgKR, stgVR, stgMR = Res(), Res(), Res()
            if isB:
                for o in range(6):
                    DMA("sp", stgM, c_maskb_d[o], "stgM", [], [stgMR])
                    CP(maskb[:, o, :], stgM, [stgMR], [maskR], eng="dve")
                P.fence(skip=skipw)
        else:
            A.ostg = Rot([(scrF[:, 2048 + i * 512:2560 + i * 512], Res()) for i in range(4)])
        defer, flush = make_defer(2)
        MEMSET(Va[:, :, 64:128], 1.0, [], [vR[0]])
        vonesR = vR[0]
        accrot = [0]
        vbank = [0]

        def attend(g, t, qpi):
            qp, qR = qslots[(t % 2) * 2 + qpi]
            if S:
                kts = [(kt * 128, kt, kR[0], vR[0], None) for kt in range(4)]
                if isB:
                    for o in range(-1, 5):
                        kt = 4 * t + o
                        if 0 <= kt < 16:
                            kts.append((512 + kt * 128, 4 + kt, kR[1 + kt // 4], vR[1 + kt // 4], o + 1))
                else:
                    kts += [(512 + kt * 128, 4 + kt, kR[1 + kt // 4], vR[1 + kt // 4], None) for kt in range(16)]
                segs = [(0, 512, kts)]
            else:
                segs = []
                for sq_ in range(2):
                    p0 = t * 512 + sq_ * 256
                    segs.append((sq_ * 256, 256, [(p0 + kt * 128, (p0 // 128) + kt, kR[1 + t], vR[1 + t], None) for kt in range(2)]))
            accs = []
            for hh in range(2):
                bi = 4 + accrot[0] % 4
                accrot[0] += 1
                accs.append((ps[bi], psR[bi]))
            for si, (qoff, N, ktiles) in enumerate(segs):
                nk = len(ktiles)
                for i, (kc0, vi, kr, vr, mo) in enumerate(ktiles):
                    sts = []
                    for hh in range(2):
                        pst, pstr = pw()
                        MM(pst[:, 0:N], kT2[64 * hh:64 * hh + 64, kc0:kc0 + 128], qp[64 * hh:64 * hh + 64, qoff:qoff + N], True, True, [kr, qR], [pstr])
                        sts.append((pst, pstr))
                    for hh in range(2):
                        pst, pstr = sts[hh]
                        acc, accR = accs[hh]
                        Pt, PtR = A.pt.next()
                        ACT(Pt[:, 0:N], pst[:, 0:N], AF.Exp, [pstr], [PtR], scale=0.125)
                        if mo is not None:
                            TT(Pt[:, 0:N], Pt[:, 0:N], maskb[:, mo, 0:N], ALU.mult, [PtR, maskR], [PtR])

                        def pv(acc=acc, accR=accR, vi=vi, vr=vr, Pt=Pt, PtR=PtR, i=i, nk=nk, N=N, qoff=qoff):
                            MM(acc[:, qoff:qoff + N], Va[:, vi, :], Pt[:, 0:N], i == 0, i == nk - 1, [vr, vonesR, PtR], [accR])
                        defer(pv)
            if True:
                for hh in range(2):
                    hq = g * 4 + qpi * 2 + hh
                    acc, accR = accs[hh]

                    def fin(acc=acc, accR=accR, hq=hq, N=512, t=t, qoff=0):
                        ta, tar = A.tmp.next()
                        if isB:
                            ACT(ta[64:128, 0:N], acc[64:128, 0:N], AF.Ln, [accR, constR], [tar], bias=esink[64:128, hq:hq + 1])
                        else:
                            ACT(ta[64:128, 0:N], acc[64:128, 0:N], AF.Ln, [accR], [tar])
                        ACT(ta[64:128, 0:N], ta[64:128, 0:N], AF.Exp, [tar], [tar], scale=-1.0)
                        pd = (hq % 2) * 64
                        TT(oT[pd:pd + 64, hq // 2, t * 512 + qoff:t * 512 + qoff + N], acc[0:64, 0:N], ta[64:128, 0:N], ALU.mult,
                           [accR, tar], [oR[hq // 2][t]])
                    defer(fin)

        for g in range(4):
            flush()
            s = wload([
                (lambda w: wview(w, 0, 256), wsrc(wqkv_d, g * 256, 256)),
                (lambda w: wview(w, 2048, 128)[:, :, 0:64], wsrc(wqkv_d, 1024 + g * 64, 64)),
                (lambda w: wview(w, 2048, 128)[:, :, 64:128], wsrc(wqkv_d, 1024 + g * 64, 64)),
                (lambda w: wview(w, 3072, 64), wsrc(wqkv_d, 1280 + g * 64, 64)),
            ])
            Wq = wview(Wt[s], 0, 256)
            Wk = wview(Wt[s], 2048, 128)
            Wv = wview(Wt[s], 3072, 64)
            if S:
                DMA("sp", stgK[:, :, 0:64], ck_d[:, g * 64:(g + 1) * 64].rearrange("(t p) f -> p t f", p=128), "stgK", [], [stgKR])
                DMA("sp", stgK[:, :, 64:128], ck_d[:, g * 64:(g + 1) * 64].rearrange("(t p) f -> p t f", p=128), "stgK2", [], [stgKR])
                pt, pr = ps[7], psR[7]
                for tb in range(4):
                    TR(pt[:, tb * 128:(tb + 1) * 128], stgK[:, tb, :], ident32[:, :], [stgKR, constR], [pr])
                CP(kT2[:, 0:512], pt[:, :], [pr], [kR[0]], eng="act")
                DMA("sp", stgV, cv_d[:, g * 64:(g + 1) * 64].rearrange("(t p) f -> p t f", p=128), "stgV", [], [stgVR])
                CP(Va[:, 0:4, 0:64], stgV, [stgVR, vonesR], [vR[0]], eng="dve")
            chains = []
            for t in range(NT):
                c0 = koff + t * 512
                aft = None
                if not S:
                    aft = (lambda kn, knR, bank, t=t, g=g: out_tokmajor(
                        A, kn, knR, 128, nk_d[t * 512:(t + 1) * 512, g * 64:(g + 1) * 64].rearrange("(tb p) f -> p tb f", p=128), ncol=64, bank=bank))
                chains.append(lambda lane, t=t, c0=c0, aft=aft, s=s, Wk=Wk: qk_chain(
                    A, lane, lambda c: Wk[:, c, :], [WR[s][1], WR[s][2], SLOT[s]], t, kg, S, kT2[:, c0:c0 + 512], kR[1 + t], after=aft))

            def qchain(t, qpi, s=s, Wq=Wq):
                qp, qR = qslots[(t % 2) * 2 + qpi]
                return lambda lane: qk_chain(A, lane, lambda c: Wq[:, c, qpi * 128:(qpi + 1) * 128], [WR[s][0], SLOT[s]], t, qg, S, qp, qR)
            for t in range(min(NT, 2)):
                for qpi in range(2):
                    chains.append(qchain(t, qpi))
            fillers = []
            for t in range(NT):
                vbank[0] += 1
                b = 4 + vbank[0] % 3
                pv_, pvr = ps[b], psR[b]
                for tb in range(4):
                    def vblk(t=t, tb=tb, pv_=pv_, pvr=pvr, s=s, Wv=Wv):
                        for c in range(8):
                            MM(pv_[:, tb * 64:(tb + 1) * 64], hT[:, c, t * 512 + tb * 128:t * 512 + (tb + 1) * 128], Wv[:, c, :],
                               c == 0, c == 7, [WR[s][3], SLOT[s], hR[c][t]], [pvr])
                    fillers.append(vblk)

                def vev(t=t, pv_=pv_, pvr=pvr, g=g):
                    CP(Va[:, voff + 4 * t:voff + 4 * t + 4, 0:64], pv_[:, 0:256].rearrange("p (k v) -> p k v", v=64), [pvr, vonesR], [vR[1 + t]], eng="act")
                    if not S:
                        stg, stgR = A.ostg.next()
                        CP(stg[:, 0:256], pv_[:, 0:256], [pvr], [stgR], eng="act")
                        DMA("pool", nv_d[t * 512:(t + 1) * 512, g * 64:(g + 1) * 64].rearrange("(tb p) f -> p tb f", p=128),
                            stg[:, 0:256].rearrange("p (tb f) -> p tb f", tb=4), ("okv", A.ostg.i % 4), [stgR], [], is_out=True)
                fillers.append(vev)
            run_lanes(chains, fillers)
            for t in range(min(NT, 2)):
                for qpi in range(2):
                    attend(g, t, qpi)
            if NT == 4:
                flush()
                chains = [qchain(t, qpi) for t in (2, 3) for qpi in range(2)]
                run_lanes(chains, [])
                for t in (2, 3):
                    for qpi in range(2):
                        attend(g, t, qpi)
        flush()

    def mixer_D(ph):
        S = ph == "S"
        T = 2048 if S else 1024
        NT = T // 512
        NCH = T // 64
        A = Lay()
        qTb = scrB[:, 0:2048]
        vtok = scrB[0:64, 2048:6144].rearrange("p (n v) -> p n v", v=128)
        vtok128 = scrB[:, 2048:6144].rearrange("p (n v) -> p n v", v=128)
        zR = Res()
        MEMSET(scrB[64:128, 2048:6144], 0.0, [], [zR])
        MEMSET(scrB[64:128, 8192:11264], 0.0, [], [zR])
        qR_ = [Res() for _ in range(4)]
        vtR = [Res() for _ in range(4)]
        A.qd = Rot([(scrB[:, 6144 + i * 512:6656 + i * 512], Res()) for i in range(2)])
        A.ki = Rot([(scrB[:, 7168 + i * 512:7680 + i * 512], Res()) for i in range(2)])
        A.koT = Rot([((scrB[0:64, 8192 + i * 1024:9216 + i * 1024].rearrange("p (n k) -> p n k", k=128), scrB[:, 8192 + i * 1024:9216 + i * 1024].rearrange("p (n k) -> p n k", k=128)), Res()) for i in range(2)])
        A.att = Rot([((scrB[0:64, 10240 + i * 512:10752 + i * 512], scrB[:, 10240 + i * 512:10752 + i * 512]), Res()) for i in range(2)])
        A.sbf = Rot([(scrB[:, 11264 + i * 128:11392 + i * 128], Res()) for i in range(3)])
        A.sq = Rot([(scrB[:, 11776 + i * 512:12288 + i * 512], Res()) for i in range(2)])
        oacc = scrF[:, 0:2048]
        oaR = [Res() for _ in range(4)]
        A.tmp = Rot([(scrF[:, 2048 + i * 512:2560 + i * 512], Res()) for i in range(5)])
        A.s32 = Rot([(scrF[:, 4608 + i * 128:4736 + i * 128], Res()) for i in range(3)])
        decs = [(misc[:, 16:24], Res()), (misc[:, 32:40], Res())]
        tots = [(misc[:, 24:32], Res()), (misc[:, 40:48], Res())]
        ptri = [0]

        def ptr():
            i = ptri[0] % 2
            ptri[0] += 1
            return ps[i], psR[i]
        gnd = smallT[:, 12:13]
        lv = lbv[:, :].rearrange("p (k d c) -> p k d c", k=2, d=2)
        poi = [0]
        for h in range(8):
            s = wload([
                (lambda w: wview(w, 0, 128), wsrc(win_d_d, h * 128, 128)),
                (lambda w: wview(w, 1024, 128), wsrc(win_d_d, 1024 + h * 128, 128)),
                (lambda w: wview(w, 2048, 128), wsrc(win_d_d, 2048 + h * 128, 128)),
                (lambda w: wview(w, 3072, 128), wsrc(win_d_d, 3072 + h * 128, 128)),
                (lambda w: wview(w, 4096, 128), wsrc(win_d_d, 4096 + h * 128, 128)),
            ])
            Wq = wview(Wt[s], 0, 128)
            Wf = [wview(Wt[s], 1024, 128), wview(Wt[s], 2048, 128)]
            Wi = wview(Wt[s], 3072, 128)
            Wgt = wview(Wt[s], 4096, 128)
            for t in range(NT):
                MEMSET(oacc[:, t * 512:(t + 1) * 512], 0.0, [], [oaR[t]])
            for t in range(NT):
                sl = slice(t * 512, (t + 1) * 512)
                pt, pr = pw()
                for c in range(8):
                    MM(pt[:, :], Wq[:, c, :], hT[:, c, sl], c == 0, c == 7, [WR[s][0], SLOT[s], hR[c][t]], [pr])
                ACT(qTb[:, sl], pt[:, :], AF.Silu, [pr], [qR_[t]])
                for half in range(2):
                    pv, pvr = pw()
                    for n4 in range(4):
                        n = half * 4 + n4
                        tk = t * 512 + n * 64
                        for c in range(8):
                            MM(pv[0:64, n4 * 128:(n4 + 1) * 128], hT[:, c, tk:tk + 64], Wi[:, c, :], c == 0, c == 7,
                               [WR[s][3], SLOT[s], hR[c][t]], [pvr])
                    EVAC(vtok[:, t * 8 + half * 4:t * 8 + half * 4 + 4, :], pv[0:64, :].rearrange("p (n v) -> p n v", v=128), [pvr], [vtR[t]])
            st = {}

            def prep(d, t, ui, s=s, Wf=Wf, h=h):
                lb_ap = lv[:, 0, d, h:h + 1]
                omlb_ap = lv[:, 1, d, h:h + 1]
                dec_, decR_ = decs[ui % 2]
                tot_, totR_ = tots[ui % 2]
                sl = slice(t * 512, (t + 1) * 512)
                pz, pzr = ptr()
                for c in range(8):
                    MM(pz[:, :], Wf[d][:, c, :], hT[:, c, sl], c == 0, c == 7, [WR[s][1 + d], SLOT[s], hR[c][t]], [pzr])
                f_, fR = A.tmp.next()
                ACT(f_, pz[:, :], AF.Exp, [pzr], [fR], scale=-1.0)
                yield
                ACT(f_, f_, AF.Ln, [fR, constR], [fR], bias=one_ap)
                yield
                ACT(f_, f_, AF.Exp, [fR], [fR], scale=-1.0)
                yield
                TS(f_, f_, omlb_ap, lb_ap, ALU.mult, ALU.add, [fR, constR], [fR])
                yield
                lf, lfR = A.tmp.next()
                ACT(lf, f_, AF.Ln, [fR], [lfR])
                TS(f_, f_, -1.0, 1.0, ALU.mult, ALU.add, [fR], [fR])
                yield
                cum, cumR = A.tmp.next()
                P.add("dve", lambda e, cum=cum, lf=lf: e.tensor_tensor_scan(cum, resetm[:, :], lf, 0.0, op0=ALU.mult, op1=ALU.add),
                      [lfR, constR], [cumR])
                cum3 = cum.rearrange("p (n c) -> p n c", c=64)
                CP(tot_[:, 0:8], cum3[:, :, 63], [cumR], [totR_], eng="dve")
                yield
                if d == 1:
                    TT(cum, lf, cum, ALU.subtract, [lfR, cumR], [cumR])
                    TT(cum3, cum3, tot_[:, 0:8].unsqueeze(2).to_broadcast([128, 8, 64]), ALU.add, [cumR, totR_], [cumR])
                    yield
                ACT(dec_[:, 0:8], tot_[:, 0:8], AF.Exp, [totR_], [decR_])
                e1, e1R = A.tmp.next()
                ACT(e1, cum, AF.Exp, [cumR], [e1R])
                STT(lf.rearrange("p (n c) -> p n c", c=64), cum3, -1.0, tot_[:, 0:8].unsqueeze(2).to_broadcast([128, 8, 64]),
                    ALU.mult, ALU.add, [cumR, totR_, lfR], [lfR])
                yield
                ACT(lf, lf, AF.Exp, [lfR], [lfR])
                qd, qdR = A.qd.next()
                TT(qd, qTb[:, sl], e1, ALU.mult, [qR_[t], e1R], [qdR])
                yield
                e2, e2R = A.tmp.next()
                ACT(e2, cum, AF.Exp, [cumR], [e2R], scale=-1.0)
                TT(lf, lf, f_, ALU.mult, [lfR, fR], [lfR])
                yield
                ki, kiR = A.ki.next()
                TT(ki, f_, e2, ALU.mult, [fR, e2R], [kiR])
                (koT, koT128), koTR = A.koT.next()
                for half in range(2):
                    pk, pkr = ptr()
                    for n4 in range(4):
                        n = half * 4 + n4
                        TR(pk[0:64, n4 * 128:(n4 + 1) * 128], lf[:, n * 64:(n + 1) * 64], ident32[:, :], [lfR, constR], [pkr])
                    EVAC(koT[:, half * 4:half * 4 + 4, :], pk[0:64, :].rearrange("p (n k) -> p n k", k=128), [pkr], [koTR])
                    yield
                pa_, par = ptr()
                for n in range(8):
                    cs = slice(n * 64, (n + 1) * 64)
                    MM(pa_[0:64, cs], ki[:, cs], qd[:, cs], True, True, [kiR, qdR], [par])
                (attm, attm128), attmR = A.att.next()
                TT(attm.rearrange("p (n c) -> p n c", c=64), pa_[0:64, :].rearrange("p (n c) -> p n c", c=64),
                   maskhf[:, d * 64:(d + 1) * 64].unsqueeze(1).to_broadcast([64, 8, 64]), ALU.mult, [par, constR], [attmR])
                yield
                pb = (2, 3) if ui % 2 == 0 else (6, 7)
                pds = [(ps[pb[0]], psR[pb[0]]), (ps[pb[1]], psR[pb[1]])]
                for n in range(8):
                    ng = t * 8 + n
                    MM(pds[n // 4][0][:, (n % 4) * 128:(n % 4 + 1) * 128], koT128[:, n, :], vtok128[:, ng, :], True, True,
                       [koTR, vtR[t], zR], [pds[n // 4][1]])
                st[ui] = (qd, qdR, attm128, attmR, pds, dec_, decR_)
                yield

            def chain(d, t, ui, h=h):
                qd, qdR, attm128, attmR, pds, dec_, decR_ = st.pop(ui)
                sl = slice(t * 512, (t + 1) * 512)
                first = (t == 0) if d == 0 else (t == NT - 1)
                if first:
                    s32, s32R = A.s32.next()
                    sbf, sbfR = A.sbf.next()
                    if S:
                        DMA("sp", s32, sd_d[d, h], ("s32", A.s32.i % 3), [], [s32R])
                    else:
                        MEMSET(s32, 0.0, [], [s32R])
                    CP(sbf, s32, [s32R], [sbfR], eng="dve")
                else:
                    s32, s32R, sbf, sbfR = cur["s"]
                po, por = ps[4 + (ui % 2)], psR[4 + (ui % 2)]
                chunks = list(range(8)) if d == 0 else list(range(7, -1, -1))
                for n in chunks:
                    ng = t * 8 + n
                    cs = slice(n * 64, (n + 1) * 64)
                    if not S and ((d == 0 and ng % 4 == 0) or (d == 1 and ng % 4 == 3)) and not (ng == (0 if d == 0 else NCH - 1)):
                        s32, s32R = A.s32.next()
                        sbf, sbfR = A.sbf.next()
                        MEMSET(s32, 0.0, [], [s32R])
                        MEMSET(sbf, 0.0, [], [sbfR])
                    MM(po[:, cs], vtok128[:, ng, :], attm128[:, cs], True, False, [vtR[t], attmR, zR], [por])
                    MM(po[:, cs], sbf, qd[:, cs], False, True, [sbfR, qdR], [por])
                    n32, n32R = A.s32.next()
                    pdn, pdnR = pds[n // 4]
                    STT(n32, s32, dec_[:, n:n + 1], pdn[:, (n % 4) * 128:(n % 4 + 1) * 128], ALU.mult, ALU.add, [s32R, decR_, pdnR], [n32R])
                    s32, s32R = n32, n32R
                    if not S and ((d == 0 and ng % 4 == 3) or (d == 1 and ng % 4 == 0)):
                        DMA("sp", nsd_d[ng // 4, d, h], s32, ("nsd", A.s32.i % 3), [s32R], [], is_out=True)
                    else:
                        sbf, sbfR = A.sbf.next()
                        CP(sbf, s32, [s32R], [sbfR], eng="dve")
                    yield
                TT(oacc[:, sl], oacc[:, sl], po[:, :], ALU.add, [oaR[t], por], [oaR[t]])
                cur["s"] = (s32, s32R, sbf, sbfR)
                yield

            cur = {}
            units = [(0, t) for t in range(NT)] + [(1, t) for t in range(NT - 1, -1, -1)]
            prev = None
            for ui, (d, t) in enumerate(units):
                pg = prep(d, t, ui)
                if prev is None:
                    for _ in pg:
                        pass
                else:
                    a_done = b_done = False
                    while not (a_done and b_done):
                        if not a_done:
                            a_done = next(pg, "END") == "END"
                        if not b_done:
                            b_done = next(prev, "END") == "END"
                prev = chain(d, t, ui)
            for _ in prev:
                pass
            sgs = []
            for t in range(NT):
                sl = slice(t * 512, (t + 1) * 512)
                pg, pgr = pw()
                for c in range(8):
                    MM(pg[:, :], Wgt[:, c, :], hT[:, c, sl], c == 0, c == 7, [WR[s][4], SLOT[s], hR[c][t]], [pgr])
                sg, sgr = A.tmp.next()
                ACT(sg, pg[:, :], AF.Silu, [pgr], [sgr])
                sgs.append((sg, sgr))
            for t in range(NT):
                sl = slice(t * 512, (t + 1) * 512)
                sg, sgr = sgs[t]
                sq, sqr = A.sq.next()
                ACT(sq, oacc[:, sl], AF.Square, [oaR[t]], [sqr])
                pm, pmr = pw()
                MM(pm[:, :], mean128[:, :], sq, True, True, [sqr, constR], [pmr])
                t1, t1r = A.tmp.next()
                ACT(t1, pm[:, :], AF.Ln, [pmr, constR], [t1r], bias=eps_ap)
                ACT(t1, t1, AF.Exp, [t1r], [t1r], scale=-0.5)
                STT(t1, oacc[:, sl], gnd, t1, ALU.mult, ALU.mult, [oaR[t], t1r, constR], [t1r])
                TT(oT[:, h, sl], t1, sg, ALU.mult, [t1r, sgr], [oR[h][t]])

    def skipw(key):
        return isinstance(key, tuple) and key[0] == "dma" and isinstance(key[1], tuple) and key[1][0] == "w"

    prologue()
    for ph in phases:
        S = ph == "S"
        T = 2048 if S else 1024
        NT = T // 512
        r = 1 if S else 0
        P.fence(skip=skipw)
        P.new_epoch()
        load_x(xs_d if S else xp_d, T)
        for li in range(nlayers):
            L = DL
            if li == 0:
                norm_mod(L, NT, li, 0, r)
            P.fence(skip=skipw)
            kind = li % 4
            if kind == 0:
                mixer_A(ph)
                wo_d = wo_a_d
            elif kind == 1:
                mixer_G(ph, "B")
                wo_d = wo_b_d
            elif kind == 2:
                mixer_G(ph, "C")
                wo_d = wo_c_d
            else:
                mixer_D(ph)
                wo_d = wo_d_d
            P.fence(skip=skipw)
            wo_proj(NT, li, r, wo_d, after=lambda t, li=li: norm_tile(L, t, li, 1, r))
            side = None
            if ph == phases[0] and li + 1 < nlayers:
                side = ada_steps(li + 1, [(scrF[:, 3072 + i * 1024:4096 + i * 1024].rearrange("p (c n) -> p c n", c=8), Res()) for i in range(2)],
                                 128, ps[7], psR[7])
            nxt = (lambda t, li=li: norm_tile(L, t, li + 1, 0, r)) if li + 1 < nlayers else None
            ffn(L, NT, li, r, side, after=nxt)
        store_y(ys_d if S else yp_d, T)

    semstack = ExitStack()
    with semstack:
        run = P.emit(lambda name: semstack.enter_context(nc.semaphore(name)))
        with nc.allow_non_contiguous_dma(reason="small strided loads"):
            with nc.Block() as block:
                @block.tensor
                def _(e):
                    run("pe", e)

                @block.scalar
                def _(e):
                    run("act", e)

                @block.vector
                def _(e):
                    run("dve", e)

                @block.gpsimd
                def _(e):
                    run("pool", e)

                @block.sync
                def _(e):
                    run("sp", e)
    es.close()
    return nc


def _consts():
    f = np.float32
    ident = np.eye(128, dtype=f)
    prot = np.zeros((128, 128), f)
    for p in range(128):
        if (p % 32) < 16:
            prot[p + 16, p] = -1.0
        else:
            prot[p - 16, p] = 1.0
    maskb = np.zeros((6, 128, 512), f)
    k = np.arange(128)[:, None]
    q = np.arange(512)[None, :]
    for o in range(-1, 5):
        maskb[o + 1] = (np.abs(q - 128 * o - k) <= 128).astype(f)
    s = np.arange(64)[:, None]
    c = np.arange(64)[None, :]
    maskh = np.stack([(c >= s).astype(f), (c <= s).astype(f)], 0)
    reset = np.ones((128, 512), f)
    reset[:, ::64] = 0.0
    tpos = np.arange(2048)
    row = (tpos // 64).astype(np.float64)
    col = (tpos % 64).astype(np.float64)
    inv_freq = 10000.0 ** (-np.arange(0, 32, 2, dtype=np.float64) / 32.0)
    ang = np.zeros((64, 2048), np.float64)
    for d in range(64):
        a = d // 32
        fi = d % 16
        ang[d] = (row if a == 0 else col) * inv_freq[fi]
    ang32 = np.zeros((64, 2048), np.float32)
    invf32 = (np.float32(10000.0) ** (-np.arange(0, 32, 2, dtype=np.float32) / np.float32(32.0))).astype(np.float32)
    for d in range(64):
        a = d // 32
        fi = d % 16
        ang32[d] = ((row if a == 0 else col).astype(np.float32) * invf32[fi]).astype(np.float32)
    cos = np.cos(ang32.astype(np.float64)).astype(f)
    sin = np.sin(ang32.astype(np.float64)).astype(f)
    cos = np.concatenate([cos, cos], 0)
    sin = np.concatenate([sin, sin], 0)
    return dict(c_ident=ident, c_prot=prot, c_maskb=maskb, c_maskh=maskh, c_reset=reset, c_cos=cos, c_sin=sin)


_NC_CACHE = {}


def _in_maps(inp, cores=range(8)):
    f = np.float32
    A = lambda x: np.ascontiguousarray(np.asarray(x, dtype=f))
    consts = _consts()
    small = np.zeros((16, 128), f)

    def dup(v):
        v = np.asarray(v, f).reshape(-1)
        return np.concatenate([v, v]) if v.size == 64 else v

    small[0] = dup(inp["qn_a"][0]); small[1] = dup(inp["kn_a"][0]); small[2] = np.asarray(inp["subln_a"][0], f)
    small[3, :64] = inp["lam_q1_a"][0]; small[4, :64] = inp["lam_k1_a"][0]
    small[5, :64] = inp["lam_q2_a"][0]; small[6, :64] = inp["lam_k2_a"][0]
    small[7] = dup(inp["qn_b"][0]); small[8] = dup(inp["kn_b"][0]); small[9, :16] = inp["sink_b"][0]
    small[10] = dup(inp["qn_c"][0]); small[11] = dup(inp["kn_c"][0]); small[12] = np.asarray(inp["gn_d"][0], f)
    shared = dict(
        w_ada=A(inp["w_ada"]), w_ffn_gate=A(inp["w_ffn_gate"]), w_ffn_up=A(inp["w_ffn_up"]), w_ffn_down=A(inp["w_ffn_down"]),
        w_qkv_a=A(inp["w_qkv_a"][0]), w_o_a=A(inp["w_o_a"][0]), w_qkv_b=A(inp["w_qkv_b"][0]), w_o_b=A(inp["w_o_b"][0]),
        w_qkv_c=A(inp["w_qkv_c"][0]), w_o_c=A(inp["w_o_c"][0]), w_in_d=A(inp["w_in_d"][0]), w_o_d=A(inp["w_o_d"][0]),
        small=small, **consts)
    maps = []
    for core in cores:
        b = core // 2
        vecs = np.zeros((384, 128), f)
        vecs[0:192] = np.asarray(inp["b_ada"], f).reshape(192, 128)
        vecs[192:224] = np.asarray(inp["norm_mix"], f).reshape(32, 128)
        vecs[224:256] = np.asarray(inp["norm_ffn"], f).reshape(32, 128)
        vecs[256:320] = np.asarray(inp["lb_logits_d"], f).reshape(64, 128)
        vecs[320:328] = np.asarray(inp["c_ctx"], f).reshape(8, 128)
        vecs[328:336] = np.asarray(inp["c"][b], f).reshape(8, 128)
        m = dict(shared)
        m.update(
            xs=A(inp["x_sample"][b]), xp=A(np.asarray(inp["x_prompt"][4 * core:4 * core + 4]).reshape(1024, 1024)), vecs=vecs,
            cak=A(np.asarray(inp["cache_a_k"][b, 0]).reshape(512, 1024)), cav=A(np.asarray(inp["cache_a_v"][b, 0]).reshape(512, 1024)),
            cbk=A(np.asarray(inp["cache_b_k"][b, 0]).reshape(512, 256)), cbv=A(np.asarray(inp["cache_b_v"][b, 0]).reshape(512, 256)),
            cck=A(np.asarray(inp["cache_c_k"][b, 0]).reshape(512, 256)), ccv=A(np.asarray(inp["cache_c_v"][b, 0]).reshape(512, 256)),
            sd=A(inp["state_d"][b, 0]))
        maps.append(m)
    return maps


def kernel(**inp):
    if "nc" not in _NC_CACHE:
        _NC_CACHE["nc"] = build()
    nc = _NC_CACHE["nc"]
    maps = _in_maps(inp)
    res = run_bass_kernel_spmd(nc, maps, core_ids=list(range(8)))
    R = res.results
    f = np.float32
    y_prompt = np.concatenate([R[c]["yp"].reshape(4, 256, 1024) for c in range(8)], 0).astype(f)
    y_sample = np.stack([R[2 * b]["ys"] for b in range(4)], 0).astype(f)
    cat = lambda name, shp: np.concatenate([R[c][name].reshape((4, 1, 256) + shp) for c in range(8)], 0).astype(f)
    new_a_k = cat("nak", (16, 64))
    new_a_v = cat("nav", (8, 128))
    new_b_k = cat("nbk", (4, 64))
    new_b_v = cat("nbv", (4, 64))
    new_c_k = cat("nck", (4, 64))
    new_c_v = cat("ncv", (4, 64))
    new_d = np.concatenate([R[c]["nsd"].reshape(4, 1, 2, 8, 128, 128) for c in range(8)], 0).astype(f)
    return (y_prompt, y_sample, new_a_k, new_a_v, new_b_k, new_b_v, new_c_k, new_c_v, new_d)
```
